# Optimizing a Trainium2 kernel written in Bass

```python
import jax
import jax.numpy as jnp
from jax import lax
import numpy as np

D_MODEL = 1024
BATCH = 32
SEQ = 256
DEPTH = 4
DEC_BATCH = 8
DEC_SEQ = 2048
PAST_LEN = 256

GRID_W = 64
HEAD_DIM = 64
ATT_HEADS = 8
ATT_KV = 2
WIN_HEADS = 8
WIN_KV = 2
WINDOW = 128
HG_HEADS = 4
HG_DK = 128
HG_DV = 128
BRANCH_W = 512
N_BRANCH = 3
D_FF = 2816
Q_BLOCK = 128
CHUNK = 32
ROPE_THETA = 10000.0
EPS = 1e-6
NEG_INF = -1e30
TINY = 1e-30
N_MOD = 9
ATT_Q_W = ATT_HEADS * HEAD_DIM
ATT_KV_W = ATT_KV * HEAD_DIM
WIN_Q_W = WIN_HEADS * HEAD_DIM
WIN_KV_W = WIN_KV * HEAD_DIM
HG_W = HG_HEADS * HG_DK
IN_SECTIONS = (ATT_Q_W, ATT_KV_W, ATT_KV_W, WIN_Q_W, WIN_KV_W, WIN_KV_W,
               HG_W, HG_W, HG_W, HG_W, HG_W, D_MODEL, D_MODEL, D_MODEL)
IN_W = ATT_Q_W + 2 * ATT_KV_W + WIN_Q_W + 2 * WIN_KV_W + 5 * HG_W + 3 * D_MODEL

kernel_name = 'hybrid_diffusion_prefix_trunk_step'


def _rmsnorm(x, g):
    xf = x.astype(jnp.float32)
    y = xf * lax.rsqrt(jnp.mean(xf * xf, axis=-1, keepdims=True) + EPS)
    return (y * g.astype(jnp.float32)).astype(x.dtype)


def _swiglu(h, w_gu, w_down):
    gu = h @ w_gu
    return (jax.nn.silu(gu[..., :D_FF]) * gu[..., D_FF:]) @ w_down


def _rot(x, cos, sin):
    n = cos.shape[-1]
    x1, x2 = x[..., :n], x[..., n:]
    return jnp.concatenate([x1 * cos - x2 * sin, x1 * sin + x2 * cos], axis=-1)


def _rope_2d(x, rope):
    cos_r, sin_r, cos_c, sin_c = rope
    shape = (x.shape[1],) + (1,) * (x.ndim - 3) + (cos_r.shape[-1],)
    cast = lambda t: t.reshape(shape).astype(x.dtype)
    half = HEAD_DIM // 2
    return jnp.concatenate([_rot(x[..., :half], cast(cos_r), cast(sin_r)),
                            _rot(x[..., half:], cast(cos_c), cast(sin_c))], axis=-1)


def _attend(q, k, v, mask, sink):
    s = jnp.einsum('bqgrd,bkgd->bgrqk', q, k).astype(jnp.float32) * (HEAD_DIM ** -0.5)
    if mask is not None:
        s = jnp.where(mask, s, NEG_INF)
    if sink is None:
        p = jax.nn.softmax(s, axis=-1)
    else:
        sk = sink.astype(jnp.float32)[None, :, :, None, None]
        m = jnp.maximum(jnp.max(s, axis=-1, keepdims=True), sk)
        e = jnp.exp(s - m)
        p = e / (jnp.sum(e, axis=-1, keepdims=True) + jnp.exp(sk - m))
    return jnp.einsum('bgrqk,bkgd->bqgrd', p.astype(v.dtype), v)


def _dense_blocked(q, k, v, sink):
    b, s, g, r, d = q.shape
    nb = s // Q_BLOCK
    qb = jnp.moveaxis(q.reshape(b, nb, Q_BLOCK, g, r, d), 1, 0)
    ob = lax.map(lambda qi: _attend(qi, k, v, None, sink), qb)
    return jnp.moveaxis(ob, 0, 1).reshape(b, s, g, r, d)


def _window_blocked(q, k, v, k_ctx, v_ctx, sink):
    b, s, g, r, d = q.shape
    nb = s // Q_BLOCK
    n_ctx = k_ctx.shape[1]

    def bands(t):
        tp = jnp.pad(t, ((0, 0), (Q_BLOCK, Q_BLOCK), (0, 0), (0, 0))).reshape(b, nb + 2, Q_BLOCK, g, d)
        band = jnp.concatenate([tp[:, :-2], tp[:, 1:-1], tp[:, 2:]], axis=2)
        return jnp.moveaxis(band, 1, 0)

    qb = jnp.moveaxis(q.reshape(b, nb, Q_BLOCK, g, r, d), 1, 0)
    q_off = jnp.arange(Q_BLOCK)
    k_off = jnp.arange(3 * Q_BLOCK) - Q_BLOCK
    ctx_ok = jnp.ones((Q_BLOCK, n_ctx), dtype=bool)

    def step(args):
        qi, ki, vi, i = args
        t = i * Q_BLOCK + q_off
        src = i * Q_BLOCK + k_off
        band_ok = ((jnp.abs(t[:, None] - src[None, :]) <= WINDOW)
                   & (src >= 0)[None, :] & (src < s)[None, :])
        mask = jnp.concatenate([ctx_ok, band_ok], axis=1)
        return _attend(qi, jnp.concatenate([k_ctx, ki], axis=1),
                       jnp.concatenate([v_ctx, vi], axis=1), mask, sink)

    ob = lax.map(step, (qb, bands(k), bands(v), jnp.arange(nb)))
    return jnp.moveaxis(ob, 0, 1).reshape(b, s, g, r, d)


def _gla_scan(q, k, v, logf, s0):
    b, s, h, _ = q.shape
    n = s // CHUNK

    def chunks(t):
        return t.reshape(b, n, CHUNK, h, t.shape[-1]).transpose(1, 0, 3, 2, 4)

    causal = jnp.tril(jnp.ones((CHUNK, CHUNK), dtype=bool))[:, :, None]

    def step(state, xs):
        qc, kc, vc, gc = xs
        bcum = lax.cumsum(gc, axis=2)
        decay = jnp.exp(jnp.where(causal, bcum[:, :, :, None, :] - bcum[:, :, None, :, :], NEG_INF))
        scores = jnp.einsum('bhtd,bhsd,bhtsd->bhts', qc, kc, decay)
        out = (jnp.einsum('bhts,bhsv->bhtv', scores, vc)
               + jnp.einsum('bhtd,bhdv->bhtv', qc * jnp.exp(bcum), state))
        blast = bcum[:, :, -1:, :]
        new_state = (jnp.exp(blast)[:, :, 0, :, None] * state
                     + jnp.einsum('bhsd,bhsv->bhdv', kc * jnp.exp(blast - bcum), vc))
        return new_state, out

    s_fin, oc = lax.scan(step, s0.astype(jnp.float32), (chunks(q), chunks(k), chunks(v), chunks(logf)))
    return oc.transpose(1, 0, 3, 2, 4).reshape(b, s, h, v.shape[-1]), s_fin


def _hgrn(q, z_fwd, z_bwd, i, g, lb, norm_g, s0):
    b, s, _ = q.shape
    heads = lambda t, d: t.astype(jnp.float32).reshape(b, s, HG_HEADS, d)
    qh = heads(jax.nn.silu(q), HG_DK)
    vh = heads(i, HG_DV)

    def gates(z, lbd):
        zf = z.astype(jnp.float32)
        f = lbd + (1.0 - lbd) * jax.nn.sigmoid(zf)
        logf = jnp.log(jnp.maximum(f, TINY))
        key = (1.0 - lbd) * jax.nn.sigmoid(-zf)
        return heads(key, HG_DK), heads(logf, HG_DK)

    k_f, g_f = gates(z_fwd, lb[0])
    k_b, g_b = gates(z_bwd, lb[1])
    rev = lambda t: jnp.flip(t, axis=1)
    o_f, s_f = _gla_scan(qh, k_f, vh, g_f, s0[:, 0])
    o_b, s_b = _gla_scan(rev(qh), rev(k_b), rev(vh), rev(g_b), s0[:, 1])
    o = _rmsnorm(o_f + rev(o_b), norm_g) * jax.nn.silu(heads(g, HG_DV))
    return o.reshape(b, s, HG_HEADS * HG_DV).astype(q.dtype), jnp.stack([s_f, s_b], axis=1)


def _split_cols(t):
    out, start = [], 0
    for w in IN_SECTIONS:
        out.append(t[..., start:start + w])
        start += w
    return out


def _mixer(h, lw, rope, ctx):
    b, s, _ = h.shape
    (aq, ak, av, wq, wk, wv, hq, hf_fwd, hf_bwd, hi, hg, ga, gb, gc) = _split_cols(h @ lw['w_in'])
    aq = _rmsnorm(aq.reshape(b, s, ATT_KV, ATT_HEADS // ATT_KV, HEAD_DIM), lw['q_norm_g'])
    ak = _rmsnorm(ak.reshape(b, s, ATT_KV, HEAD_DIM), lw['k_norm_g'])
    av = av.reshape(b, s, ATT_KV, HEAD_DIM)
    wq = wq.reshape(b, s, WIN_KV, WIN_HEADS // WIN_KV, HEAD_DIM)
    wk = wk.reshape(b, s, WIN_KV, HEAD_DIM)
    wv = wv.reshape(b, s, WIN_KV, HEAD_DIM)
    sink = lw['sink'].reshape(WIN_KV, WIN_HEADS // WIN_KV)
    if ctx is None:
        o_att = _dense_blocked(aq, ak, av, None)
        o_win = _dense_blocked(wq, wk, wv, sink)
        s0 = jnp.zeros((b, 2, HG_HEADS, HG_DK, HG_DV), jnp.float32)
    else:
        ck_att, cv_att, ck_win, cv_win, s0 = ctx
        aq, ak = _rope_2d(aq, rope), _rope_2d(ak, rope)
        wq, wk = _rope_2d(wq, rope), _rope_2d(wk, rope)
        o_att = _dense_blocked(aq, jnp.concatenate([ck_att, ak], axis=1),
                               jnp.concatenate([cv_att, av], axis=1), None)
        o_win = _window_blocked(wq, wk, wv, ck_win, cv_win, sink)
    o_hg, s_fin = _hgrn(hq, hf_fwd, hf_bwd, hi, hg, lw['lb'], lw['hg_norm_g'], s0)
    wb = lw['w_branch']
    merged = (jax.nn.sigmoid(ga) * (o_att.reshape(b, s, BRANCH_W) @ wb[0])
              + jax.nn.sigmoid(gb) * (o_hg @ wb[1])
              + jax.nn.sigmoid(gc) * (o_win.reshape(b, s, BRANCH_W) @ wb[2]))
    y = merged @ lw['w_out']
    new_ctx = (ak, av, wk, wv, s_fin) if ctx is None else None
    return y, new_ctx


def _layer(x, cond, lw, rope, ctx):
    mod = (jax.nn.silu(cond) @ lw['w_mod'] + lw['b_mod']).reshape(cond.shape[0], 1, N_MOD, D_MODEL)

    def pre(j, t):
        return _rmsnorm(t, lw['norm_g'][j]) * (1.0 + mod[:, :, 3 * j + 1]) + mod[:, :, 3 * j]

    x = x + 0.5 * mod[:, :, 2] * _swiglu(pre(0, x), lw['ffn_in'][0], lw['ffn_out'][0])
    y, new_ctx = _mixer(pre(1, x), lw, rope, ctx)
    x = x + mod[:, :, 5] * y
    x = x + 0.5 * mod[:, :, 8] * _swiglu(pre(2, x), lw['ffn_in'][1], lw['ffn_out'][1])
    return x, new_ctx


def setup_inputs(seed: int = 0) -> dict:
    key = jax.random.key(seed)
    ks = jax.random.split(key, 22)
    nrm = lambda k, shape, scale: jax.random.normal(k, shape, jnp.float32) * scale
    return {
        'x_prompt': nrm(ks[0], (BATCH, SEQ, D_MODEL), 1.0),
        'x_sample': nrm(ks[1], (DEC_BATCH, DEC_SEQ, D_MODEL), 1.0),
        'cache_k_attn': nrm(ks[2], (DEC_BATCH, DEPTH, PAST_LEN, ATT_KV, HEAD_DIM), 1.0),
        'cache_v_attn': nrm(ks[3], (DEC_BATCH, DEPTH, PAST_LEN, ATT_KV, HEAD_DIM), 1.0),
        'cache_k_win': nrm(ks[4], (DEC_BATCH, DEPTH, PAST_LEN, WIN_KV, HEAD_DIM), 1.0),
        'cache_v_win': nrm(ks[5], (DEC_BATCH, DEPTH, PAST_LEN, WIN_KV, HEAD_DIM), 1.0),
        'state_hgrn': nrm(ks[6], (DEC_BATCH, DEPTH, 2, HG_HEADS, HG_DK, HG_DV), 0.5),
        'c': nrm(ks[7], (DEC_BATCH, D_MODEL), 1.0),
        'c_ctx': nrm(ks[8], (D_MODEL,), 1.0),
        'w_mod': nrm(ks[9], (DEPTH, D_MODEL, N_MOD * D_MODEL), D_MODEL ** -0.5),
        'b_mod': nrm(ks[10], (DEPTH, N_MOD * D_MODEL), 0.01),
        'norm_g': 1.0 + nrm(ks[11], (DEPTH, 3, D_MODEL), 0.01),
        'w_ffn_in': nrm(ks[12], (DEPTH, 2, D_MODEL, 2 * D_FF), D_MODEL ** -0.5),
        'w_ffn_out': nrm(ks[13], (DEPTH, 2, D_FF, D_MODEL), D_FF ** -0.5),
        'w_in': nrm(ks[14], (DEPTH, D_MODEL, IN_W), D_MODEL ** -0.5),
        'qk_norm_g': 1.0 + nrm(ks[15], (DEPTH, 2, HEAD_DIM), 0.01),
        'lower_bounds': nrm(ks[16], (DEPTH, 2, HG_W), 0.1),
        'hg_norm_g': 1.0 + nrm(ks[17], (DEPTH, HG_DV), 0.01),
        'sink_logit': nrm(ks[18], (DEPTH, WIN_HEADS), 0.5),
        'w_branch': nrm(ks[19], (DEPTH, N_BRANCH, BRANCH_W, D_MODEL), BRANCH_W ** -0.5),
        'w_out': nrm(ks[20], (DEPTH, D_MODEL, D_MODEL), D_MODEL ** -0.5),
        'final_norm_g': 1.0 + nrm(ks[21], (D_MODEL,), 0.01),
    }


def reference(x_prompt, x_sample, cache_k_attn, cache_v_attn, cache_k_win, cache_v_win, state_hgrn,
              c, c_ctx, w_mod, b_mod, norm_g, w_ffn_in, w_ffn_out, w_in, qk_norm_g, lower_bounds,
              hg_norm_g, sink_logit, w_branch, w_out, final_norm_g):
    lb_soft = jax.nn.softmax(lower_bounds.astype(jnp.float32), axis=0)
    lb_all = jnp.cumsum(lb_soft, axis=0) - lb_soft[0]
    layers = [dict(w_mod=w_mod[l], b_mod=b_mod[l], norm_g=norm_g[l], ffn_in=w_ffn_in[l],
                   ffn_out=w_ffn_out[l], w_in=w_in[l], q_norm_g=qk_norm_g[l, 0],
                   k_norm_g=qk_norm_g[l, 1], sink=sink_logit[l], lb=lb_all[l],
                   hg_norm_g=hg_norm_g[l], w_branch=w_branch[l], w_out=w_out[l])
              for l in range(DEPTH)]

    xp = x_prompt
    cond_ctx = c_ctx[None, :]
    ctx_out = []
    for l in range(DEPTH):
        xp, cx = _layer(xp, cond_ctx, layers[l], None, None)
        ctx_out.append(cx)
    y_prompt = _rmsnorm(xp, final_norm_g)
    new_k_attn = jnp.stack([cx[0] for cx in ctx_out], axis=1)
    new_v_attn = jnp.stack([cx[1] for cx in ctx_out], axis=1)
    new_k_win = jnp.stack([cx[2] for cx in ctx_out], axis=1)
    new_v_win = jnp.stack([cx[3] for cx in ctx_out], axis=1)
    new_state_hgrn = jnp.stack([cx[4] for cx in ctx_out], axis=1)

    n_lat = x_sample.shape[1]
    rows = n_lat // GRID_W
    row = jnp.repeat(jnp.arange(rows, dtype=jnp.float32), GRID_W)
    col = jnp.tile(jnp.arange(GRID_W, dtype=jnp.float32), rows)
    axis_dim = HEAD_DIM // 2
    inv = ROPE_THETA ** (-jnp.arange(0, axis_dim, 2, dtype=jnp.float32) / axis_dim)
    ang_r = row[:, None] * inv
    ang_c = col[:, None] * inv
    rope = (jnp.cos(ang_r), jnp.sin(ang_r), jnp.cos(ang_c), jnp.sin(ang_c))
    xs = x_sample
    for l in range(DEPTH):
        ctx = (cache_k_attn[:, l], cache_v_attn[:, l], cache_k_win[:, l], cache_v_win[:, l], state_hgrn[:, l])
        xs, _ = _layer(xs, c, layers[l], rope, ctx)
    y_sample = _rmsnorm(xs, final_norm_g)
    return (y_prompt, y_sample, new_k_attn, new_v_attn, new_k_win, new_v_win, new_state_hgrn)
```

```python
from contextlib import ExitStack
import math
import os
import numpy as np
SUB = int(os.environ.get("MK_SUB", "99"))
SUB2 = int(os.environ.get("MK_SUB2", "99"))

import concourse.bass as bass
import concourse.mybir as mybir
from concourse.bass_utils import run_bass_kernel_spmd

F32 = mybir.dt.float32
BF16 = mybir.dt.bfloat16
AF = mybir.ActivationFunctionType
ALU = mybir.AluOpType

D = 1024
DFF = 2816
DEPTH = 4
NPROMPT = 4
SEQ = 256
DSEQ = 2048
PAST = 256
INW = 7168
EPS = 1e-6
NCORES = 8

O_AQ, O_AK, O_AV = 0, 512, 640
O_WQ, O_WK, O_WV = 768, 1280, 1408
O_HQ, O_HF, O_HB, O_HI, O_HG = 1536, 2048, 2560, 3072, 3584
O_GA, O_GB, O_GC = 4096, 5120, 6144

EPOCH = 12000
DMA_RING = 12


class Tile:
    __slots__ = ("ap", "w", "r", "name", "psum")

    def __init__(self, ap, name=""):
        self.ap = ap
        self.w = {}
        self.r = {}
        self.name = name
        self.psum = False

    def inherit(self, others):
        for o in others:
            for src in (o.w, o.r):
                for k, v in src.items():
                    if self.w.get(k, 0) < v:
                        self.w[k] = v


class Prog:
    ENGS = ("pe", "act", "dve", "pool", "sp")

    def __init__(self, nc):
        self.nc = nc
        self.items = {e: [] for e in self.ENGS}
        self.count = {e: 0 for e in self.ENGS}
        self.epoch = {e: 0 for e in self.ENGS}
        self.seen = {e: {} for e in self.ENGS}
        self.semkeys = []
        self.semset = set()
        self.dma_next = {e: 0 for e in self.ENGS}
        self.dma_val = {}
        self.stack = ExitStack()
        self.scopes = []
        self.residue = Tile(None, "residue")
        self.nops = 0

    def _reg(self, t):
        t.inherit([self.residue])
        if self.scopes:
            self.scopes[-1][1].append(t)
        return t

    def sbuf(self, name, shape, dtype):
        st = self.scopes[-1][0] if self.scopes else self.stack
        self.uid = getattr(self, "uid", 0) + 1
        name = f"{name}_{self.uid}"
        h = st.enter_context(self.nc.sbuf_tensor(name, list(shape), dtype))
        return self._reg(Tile(h, name))

    def view(self, ap, name=""):
        return self._reg(Tile(ap, name))

    def psum(self, name, shape, dtype):
        h = self.stack.enter_context(self.nc.psum_tensor(name, list(shape), dtype))
        t = Tile(h, name)
        t.psum = True
        return t

    def push_scope(self):
        self.scopes.append((ExitStack(), []))

    def pop_scope(self):
        st, tiles = self.scopes.pop()
        self.residue.inherit(tiles)
        st.close()

    def _sem(self, key):
        if key not in self.semset:
            self.semset.add(key)
            self.semkeys.append(key)
        return key

    def _wait(self, eng, key, val):
        if self.seen[eng].get(key, 0) >= val:
            return
        self.seen[eng][key] = val
        self.items[eng].append(("wait", key, val))

    def _deps(self, eng, reads, writes):
        need = {}
        for t in reads:
            for k, v in t.w.items():
                if need.get(k, 0) < v:
                    need[k] = v
            if t.psum:
                for k, v in t.r.items():
                    if k[0] != eng and need.get(k, 0) < v:
                        need[k] = v
        for t in writes:
            for d in (t.w, t.r):
                for k, v in d.items():
                    if need.get(k, 0) < v:
                        need[k] = v
        for k, v in need.items():
            if eng == "pe" and k[0] == "pe":
                continue
            self._wait(eng, k, v)

    def _mark(self, reads, writes, key, val):
        for t in reads:
            if t.r.get(key, 0) < val:
                t.r[key] = val
        for t in writes:
            if t.w.get(key, 0) < val:
                t.w[key] = val

    def op(self, eng, fn, reads=(), writes=()):
        if self.count[eng] >= EPOCH:
            self.epoch[eng] += 1
            self.count[eng] = 0
        key = self._sem((eng, self.epoch[eng]))
        self._deps(eng, reads, writes)
        self.count[eng] += 1
        val = self.count[eng]
        self.items[eng].append(("op", fn, key))
        self._mark(reads, writes, key, val)
        self.nops += 1
        return (key, val)

    def dma(self, eng, fn, reads=(), writes=()):
        i = self.dma_next[eng]
        self.dma_next[eng] = (i + 1) % DMA_RING
        key = self._sem(("dma", eng, i))
        prev = self.dma_val.get(key, 0)
        if prev:
            self._wait(eng, key, prev)
        self._deps(eng, reads, writes)
        val = prev + 16
        self.dma_val[key] = val
        self.items[eng].append(("dma", fn, key))
        self._mark(reads, writes, key, val)

    def wait_all_dmas(self, eng):
        for key, val in self.dma_val.items():
            self._wait(eng, key, val)

    def emit(self):
        nc = self.nc
        sems = {}
        for key in self.semkeys:
            nm = "s_" + "_".join(str(x) for x in key)
            sems[key] = self.stack.enter_context(nc.semaphore(nm))
        items = self.items

        def run(handle, lst):
            for it in lst:
                if it[0] == "wait":
                    handle.wait_ge(sems[it[1]], it[2])
                elif it[0] == "op":
                    it[1](handle).then_inc(sems[it[2]], 1)
                else:
                    it[1](handle).then_inc(sems[it[2]], 16)

        with nc.Block() as block:
            @block.tensor
            def _(e):
                run(e, items["pe"])

            @block.scalar
            def _(e):
                run(e, items["act"])

            @block.vector
            def _(e):
                run(e, items["dve"])

            @block.gpsimd
            def _(e):
                run(e, items["pool"])

            @block.sync
            def _(e):
                run(e, items["sp"])
        self.stack.close()


class Builder:
    def __init__(self, stage=99, nlayers=DEPTH, LW=DEPTH):
        self.stage = stage
        self.nlayers = nlayers
        self.LW = LW
        DEPTH = LW
        nc = bass.Bass("TRN2", target_bir_lowering=False)
        self.nc = nc
        self.P = Prog(nc)

        def din(name, shape):
            return nc.dram_tensor(name, list(shape), F32, kind="ExternalInput").ap()

        def dout(name, shape):
            return nc.dram_tensor(name, list(shape), F32, kind="ExternalOutput").ap()

        self.xin = din("xin", [3072, D])
        self.cond = din("cond", [2, D])
        self.ck_att = din("ck_att", [DEPTH, PAST, 128])
        self.cv_att = din("cv_att", [DEPTH, PAST, 128])
        self.ck_win = din("ck_win", [DEPTH, PAST, 128])
        self.cv_win = din("cv_win", [DEPTH, PAST, 128])
        self.state = din("state", [DEPTH, 2, 4, 128, 128])
        self.w_mod = din("w_mod", [DEPTH, D, 9 * D])
        self.b_mod = din("b_mod", [DEPTH, 9 * D])
        self.norm_g = din("norm_g", [DEPTH, 3, D])
        self.w_ffn_in = din("w_ffn_in", [DEPTH, 2, D, 2 * DFF])
        self.w_ffn_out = din("w_ffn_out", [DEPTH, 2, DFF, D])
        self.w_in = din("w_in", [DEPTH, D, INW])
        self.qk_g = din("qk_norm_g", [DEPTH, 2, 64])
        self.lbounds = din("lower_bounds", [DEPTH, 2, 512])
        self.hg_g = din("hg_norm_g", [DEPTH, 128])
        self.sink = din("sink_logit", [DEPTH, 8])
        self.w_branch = din("w_branch", [DEPTH, 3, 512, D])
        self.w_out = din("w_out", [DEPTH, D, D])
        self.final_g = din("final_norm_g", [D])
        self.cst = din("cst", [12, 128, 128])
        self.rope = din("rope", [2, 128, DSEQ])

        self.y = dout("y", [3072, D])
        self.o_k_att = dout("o_k_att", [NPROMPT, DEPTH, SEQ, 128])
        self.o_v_att = dout("o_v_att", [NPROMPT, DEPTH, SEQ, 128])
        self.o_k_win = dout("o_k_win", [NPROMPT, DEPTH, SEQ, 128])
        self.o_v_win = dout("o_v_win", [NPROMPT, DEPTH, SEQ, 128])
        self.o_state = dout("o_state", [NPROMPT, DEPTH, 2, 4, 128, 128])

    def mm(self, ps, ps_ap, lt, lt_ap, rt, rt_ap, start, stop, **kw):
        return self.P.op("pe", lambda e: e.matmul(ps_ap, lt_ap, rt_ap, start=start, stop=stop, **kw),
                  reads=[lt, rt], writes=[ps])

    def tr(self, ps, ps_ap, src, src_ap, ident, ident_ap):
        self.P.op("pe", lambda e: e.transpose(ps_ap, src_ap, ident_ap), reads=[src, ident], writes=[ps])

    def act(self, out_t, out_ap, in_t, in_ap, func, reads=(), **kw):
        self.P.op("act", lambda e: e.activation(out_ap, in_ap, func, **kw),
                  reads=[in_t] + list(reads), writes=[out_t])

    def tt(self, eng, out_t, out_ap, a_t, a_ap, b_t, b_ap, op):
        self.P.op(eng, lambda e: e.tensor_tensor(out_ap, a_ap, b_ap, op), reads=[a_t, b_t], writes=[out_t])

    def ts(self, eng, out_t, out_ap, a_t, a_ap, s1, s2, op0, op1=ALU.bypass, reads=()):
        self.P.op(eng, lambda e: e.tensor_scalar(out_ap, a_ap, s1, s2, op0, op1),
                  reads=[a_t] + list(reads), writes=[out_t])

    def stt(self, eng, out_t, out_ap, a_t, a_ap, scalar, b_t, b_ap, op0, op1, reads=()):
        self.P.op(eng, lambda e: e.scalar_tensor_tensor(out_ap, a_ap, scalar, b_ap, op0, op1),
                  reads=[a_t, b_t] + list(reads), writes=[out_t])

    def copy(self, eng, out_t, out_ap, in_t, in_ap):
        if eng == "act":
            self.P.op("act", lambda e: e.copy(out_ap, in_ap), reads=[in_t], writes=[out_t])
        else:
            self.P.op(eng, lambda e: e.tensor_copy(out_ap, in_ap), reads=[in_t], writes=[out_t])

    def recip(self, out_t, out_ap, in_t, in_ap):
        self.P.op("dve", lambda e: e.reciprocal(out_ap, in_ap), reads=[in_t], writes=[out_t])

    def memset(self, eng, t, ap, val):
        self.P.op(eng, lambda e: e.memset(ap, val), writes=[t])

    def dma(self, q, out_ap, in_ap, reads=(), writes=(), **kw):
        self.P.dma(q, lambda e: e.dma_start(out=out_ap, in_=in_ap, **kw), reads=reads, writes=writes)

    def getps(self):
        i = self.ps_next
        self.ps_next = (i + 1) % len(self.ps)
        return self.ps[i]

    def getpsl(self):
        i = self.psl_next
        self.psl_next = (i + 1) % 2
        return self.psl[i]

    def getw(self):
        i = self.w_next
        self.w_next = (i + 1) % len(self.wring)
        return self.wring[i]

    def wload(self, src_ap, shape):
        t = self.getw()
        a, b = shape
        v = t.ap[:, 0:a * b].rearrange("p (a b) -> p a b", b=b)
        self.dma("pool", v, src_ap, writes=[t])
        return t, v

    def build(self):
        P = self.P
        nc = self.nc
        allps = [P.psum(f"ps{i}", [128, 512], F32) for i in range(8)]
        self.ps = allps[0:6]
        self.psl = allps[6:8]
        self.psl_next = 0
        self.ps_next = 0
        self.wring = [P.sbuf(f"wr{i}", [128, 4096], BF16) for i in range(4)]
        self.w_next = 0
        ident = P.sbuf("ident", [128, 128], F32)
        self.ident = ident
        self.dma("sp", ident.ap[:], self.cst[0], writes=[ident])
        cbf = P.sbuf("cbf", [128, 11, 128], BF16)
        self.cbf = cbf
        self.dma("pool", cbf.ap[:], self.cst[1:12].rearrange("c p f -> p c f"), writes=[cbf])
        ones = P.sbuf("ones", [128, 128], BF16)
        self.ones = ones
        self.memset("dve", ones, ones.ap[:], 1.0)
        negh = P.sbuf("negh", [128, 1], F32)
        self.negh = negh
        self.memset("pool", negh, negh.ap[:], -0.5)

        self.prologue()
        for g in [int(c) for c in os.environ.get("MK_GROUPS", "01")]:
            self.run_group(g)
        P.wait_all_dmas("sp")
        P.emit()
        return nc

    def prologue(self):
        P = self.P
        NV = 5 * 128
        vecs = P.sbuf("vecs", [128, NV], F32)
        self.vecs = vecs
        P.push_scope()
        stg = [P.sbuf(f"stg{i}", [128, 128], F32) for i in range(5)]
        for s in stg:
            self.memset("dve", s, s.ap[:], 0.0)
        self.c_cond = 0
        self.c_ng = 16
        self.c_fg = 112
        self.c_hg = 120
        self.c_lb = 128
        self.c_qk = 160
        self.c_bm = [168, 256, 384, 512]
        q = "sp"
        self.dma(q, stg[0].ap[0:16, :], self.cond.rearrange("m (kc p) -> (m kc) p", p=128), writes=[stg[0]])
        LW = self.LW
        self.dma(q, stg[0].ap[16:16 + LW * 24, :], self.norm_g.rearrange("l j (kc p) -> (l j kc) p", p=128), writes=[stg[0]])
        self.dma(q, stg[0].ap[112:120, :], self.final_g.rearrange("(kc p) -> kc p", p=128), writes=[stg[0]])
        self.dma(q, stg[0].ap[120:120 + LW, :], self.hg_g, writes=[stg[0]])
        self.dma(q, stg[1].ap[0:LW * 8, :], self.lbounds.rearrange("l r (h p) -> (l r h) p", p=128), writes=[stg[1]])
        qkv = self.qk_g.rearrange("l w d -> (l w) d")
        self.dma(q, stg[1].ap[32:32 + 2 * LW, 0:64], qkv, writes=[stg[1]])
        self.dma(q, stg[1].ap[32:32 + 2 * LW, 64:128], qkv, writes=[stg[1]])
        self.dma(q, stg[1].ap[40:112, :], self.b_mod[0].rearrange("(j p) -> j p", p=128), writes=[stg[1]])
        for l in range(1, LW):
            self.dma(q, stg[l + 1].ap[0:72, :], self.b_mod[l].rearrange("(j p) -> j p", p=128), writes=[stg[l + 1]])
        for i in range(5):
            ps = self.getps()
            self.tr(ps, ps.ap[:, 0:128], stg[i], stg[i].ap[:], self.ident, self.ident.ap[:])
            self.copy("dve", vecs, vecs.ap[:, i * 128:(i + 1) * 128], ps, ps.ap[:, 0:128])
        P.pop_scope()
        V = vecs.ap
        P.push_scope()
        tmp = P.sbuf("ptmp", [128, 64], F32)
        self.act(tmp, tmp.ap[:, 0:16], vecs, V[:, 0:16], AF.Exp, scale=-1.0)
        self.ts("dve", tmp, tmp.ap[:, 0:16], tmp, tmp.ap[:, 0:16], 1.0, None, ALU.add)
        self.recip(tmp, tmp.ap[:, 0:16], tmp, tmp.ap[:, 0:16])
        self.tt("dve", vecs, V[:, 0:16], vecs, V[:, 0:16], tmp, tmp.ap[:, 0:16], ALU.mult)
        lbv = V[:, self.c_lb:self.c_lb + 32].rearrange("p (l c) -> p l c", c=8)
        e = tmp.ap[:, 16:48].rearrange("p (l c) -> p l c", c=8)
        self.act(tmp, e, vecs, lbv, AF.Exp)
        s = tmp.ap[:, 48:56]
        self.tt("dve", tmp, s, tmp, e[:, 0, :], tmp, e[:, 1, :], ALU.add)
        self.tt("dve", tmp, s, tmp, s, tmp, e[:, 2, :], ALU.add)
        self.tt("dve", tmp, s, tmp, s, tmp, e[:, 3, :], ALU.add)
        self.recip(tmp, s, tmp, s)
        for l in range(4):
            self.tt("dve", tmp, e[:, l, :], tmp, e[:, l, :], tmp, s, ALU.mult)
        self.memset("dve", vecs, lbv[:, 0, :], 0.0)
        self.copy("dve", vecs, lbv[:, 1, :], tmp, e[:, 1, :])
        self.tt("dve", vecs, lbv[:, 2, :], vecs, lbv[:, 1, :], tmp, e[:, 2, :], ALU.add)
        self.tt("dve", vecs, lbv[:, 3, :], vecs, lbv[:, 2, :], tmp, e[:, 3, :], ALU.add)
        P.pop_scope()
        oml = P.sbuf("oml", [128, 32], F32)
        self.oml = oml
        self.ts("dve", oml, oml.ap[:], vecs, V[:, self.c_lb:self.c_lb + 32], -1.0, 1.0, ALU.mult, ALU.add)
        esink = P.sbuf("esink", [128, 32], F32)
        self.esink = esink
        self.memset("dve", esink, esink.ap[:], 0.0)
        self.dma("sp", esink.ap[:, 0:8 * LW], self.sink.rearrange("l h -> (l h)").partition_broadcast(128), writes=[esink])
        self.act(esink, esink.ap[:], esink, esink.ap[:], AF.Exp)

        modv = P.sbuf("modv", [128, DEPTH, 72, 2], F32)
        self.modv = modv
        P.push_scope()
        wm = [P.sbuf(f"wm{i}", [128, 8, 1152], F32) for i in range(2)]
        nl = self.nlayers
        for l in range(nl):
            psm = self.getps()
            pv = psm.ap[:, 0:144].rearrange("p (j m) -> p j m", m=2)
            for pc in range(8):
                wt = wm[(l * 8 + pc) % 2]
                src = self.w_mod[l].rearrange("(kc p) f -> p kc f", p=128)[:, :, pc * 1152:(pc + 1) * 1152]
                self.dma("sp", wt.ap[:], src, writes=[wt])
                for fj in range(9):
                    j72 = pc * 9 + fj
                    for kc in range(8):
                        self.mm(psm, pv[:, j72, :], wt, wt.ap[:, kc, fj * 128:(fj + 1) * 128],
                                vecs, V[:, kc:kc + 9:8], start=(kc == 0), stop=(kc == 7))
            bm = V[:, self.c_bm[l]:self.c_bm[l] + 72]
            mv = modv.ap[:, l, :, :]
            self.tt("dve", modv, mv, psm, pv, vecs, bm.unsqueeze(2).to_broadcast([128, 72, 2]), ALU.add)
            for j in range(3):
                sc = modv.ap[:, l, (3 * j + 1) * 8:(3 * j + 2) * 8, :]
                g = V[:, self.c_ng + l * 24 + j * 8: self.c_ng + l * 24 + j * 8 + 8]
                self.stt("dve", modv, sc, modv, sc, 1.0, vecs, g.unsqueeze(2).to_broadcast([128, 8, 2]),
                         ALU.add, ALU.mult)
            for j in (0, 2):
                gt = modv.ap[:, l, (3 * j + 2) * 8:(3 * j + 3) * 8, :]
                self.ts("dve", modv, gt, modv, gt, 0.5, None, ALU.mult)
        P.pop_scope()

    def mod_ap(self, l, i, kc, m):
        return self.modv.ap[:, l, i * 8 + kc, m:m + 1]

    def run_group(self, g):
        P = self.P
        ntok = 1024 if g == 0 else 2048
        row0 = 0 if g == 0 else 1024
        nblk = ntok // 512
        self.g = g
        self.ntok = ntok
        P.push_scope()
        xT = P.sbuf(f"xT{g}", [128, 8, ntok], F32)
        xt = [[P.view(xT.ap[:, kc, b * 512:(b + 1) * 512], f"x{kc}_{b}") for b in range(nblk)] for kc in range(8)]
        self.xt = xt
        P.push_scope()
        xs = [P.sbuf(f"xs{i}", [128, 4, D], F32) for i in range(2)]
        for b in range(nblk):
            st = xs[b % 2]
            src = self.xin[row0 + b * 512: row0 + (b + 1) * 512, :].rearrange("(tt p) f -> p tt f", p=128)
            self.dma("sp", st.ap[:], src, writes=[st])
            for kc in range(8):
                ps = self.getps()
                for tt_ in range(4):
                    self.tr(ps, ps.ap[:, tt_ * 128:(tt_ + 1) * 128], st, st.ap[:, tt_, kc * 128:(kc + 1) * 128],
                            self.ident, self.ident.ap[:])
                self.copy("act" if kc % 2 else "dve", xt[kc][b], xt[kc][b].ap, ps, ps.ap[:])
        P.pop_scope()

        for l in range(self.nlayers):
            if self.stage >= 1:
                self.ffn(l, 0)
            if self.stage >= 2:
                self.mixer(l)
            if self.stage >= 3:
                self.ffn(l, 1)

        P.push_scope()
        yo = [P.sbuf(f"yo{i}", [128, 4, D], F32) for i in range(2)]
        sq = [P.sbuf(f"fsq{i}", [128, 512], BF16) for i in range(2)]
        vv = P.sbuf("fv", [128, 512], F32)
        rstd = P.sbuf("frstd", [128, 512], F32)
        tn = [P.sbuf(f"ftn{i}", [128, 512], F32) for i in range(2)]
        for b in range(nblk):
            if self.stage >= 4:
                self.rstd_of(b, sq, vv, rstd)
            ot = yo[b % 2]
            for kc in range(8):
                t = tn[kc % 2]
                if self.stage >= 4:
                    gcol = self.vecs.ap[:, self.c_fg + kc:self.c_fg + kc + 1]
                    self.stt("dve", t, t.ap[:], xt[kc][b], xt[kc][b].ap, gcol, rstd, rstd.ap[:],
                             ALU.mult, ALU.mult, reads=[self.vecs])
                    src_t, src_ap = t, t.ap
                else:
                    src_t, src_ap = xt[kc][b], xt[kc][b].ap
                ps = self.getps()
                for tt_ in range(4):
                    self.tr(ps, ps.ap[:, tt_ * 128:(tt_ + 1) * 128], src_t, src_ap[:, tt_ * 128:(tt_ + 1) * 128],
                            self.ident, self.ident.ap[:])
                self.copy("act", ot, ot.ap[:, :, kc * 128:(kc + 1) * 128],
                          ps, ps.ap[:].rearrange("p (t f) -> p t f", f=128))
            dst = self.y[row0 + b * 512: row0 + (b + 1) * 512, :].rearrange("(tt p) f -> p tt f", p=128)
            self.dma("sp", dst, ot.ap[:], reads=[ot])
        P.pop_scope()
        P.pop_scope()

    def rstd_of(self, b, sq, vv, rstd, cols=None):
        xt = self.xt
        c0, c1 = cols if cols else (0, 512)
        n = c1 - c0
        ps = self.getps()
        for kc in range(8):
            s = sq[kc % 2]
            self.act(s, s.ap[:, 0:n], xt[kc][b], xt[kc][b].ap[:, c0:c1], AF.Square)
            self.mm(ps, ps.ap[:, 0:n], self.ones, self.ones.ap[:], s, s.ap[:, 0:n], start=(kc == 0), stop=(kc == 7))
        self.ts("dve", vv, vv.ap[:, 0:n], ps, ps.ap[:, 0:n], 1.0 / D, EPS, ALU.mult, ALU.add)
        self.tt("pool", rstd, rstd.ap[:, 0:n], vv, vv.ap[:, 0:n], self.negh, self.negh.ap[:, 0:1].to_broadcast([128, n]), ALU.pow)

    def adaln(self, l, j, b, hts, sq, vv, rstd, tn, cols=None):
        m = 0 if self.g == 0 else 1
        c0, c1 = cols if cols else (0, 512)
        n = c1 - c0
        self.rstd_of(b, sq, vv, rstd, cols)
        for kc in range(8):
            t = tn[kc % 2]
            x = self.xt[kc][b]
            self.tt("dve", t, t.ap[:, 0:n], x, x.ap[:, c0:c1], rstd, rstd.ap[:, 0:n], ALU.mult)
            ht, hap = hts[kc]
            self.act(ht, hap, t, t.ap[:, 0:n], AF.Identity, reads=[self.modv],
                     scale=self.mod_ap(l, 3 * j + 1, kc, m), bias=self.mod_ap(l, 3 * j, kc, m))

    def ffn(self, l, i):
        P = self.P
        j = 0 if i == 0 else 2
        m = 0 if self.g == 0 else 1
        nst = self.ntok // 1024
        P.push_scope()
        hT = P.sbuf("f_hT", [128, 8, 1024], BF16)
        ht = [[P.view(hT.ap[:, kc, h * 512:(h + 1) * 512]) for h in range(2)] for kc in range(8)]
        aT = P.sbuf("f_aT", [128, 11, 1024], BF16)
        at = [[P.view(aT.ap[:, f, h * 512:(h + 1) * 512]) for h in range(2)] for f in range(11)]
        sq = [P.sbuf(f"f_sq{k}", [128, 512], BF16) for k in range(2)]
        vv = P.sbuf("f_v", [128, 512], F32)
        rstd = P.sbuf("f_rstd", [128, 512], F32)
        tn = [P.sbuf(f"f_tn{k}", [128, 512], F32) for k in range(2)]
        sl = [P.sbuf(f"f_sl{k}", [128, 512], F32) for k in range(2)]
        wgu = self.w_ffn_in[l, i].rearrange("(kc p) f -> p kc f", p=128)
        wdn = self.w_ffn_out[l, i].rearrange("(fc p) d -> p fc d", p=128)
        for st in range(nst):
            for h in range(2):
                b = st * 2 + h
                self.adaln(l, j, b, [(ht[kc][h], ht[kc][h].ap) for kc in range(8)], sq, vv, rstd, tn)
            for fh in range(2):
                f0 = fh * 11
                for fp in range(0, 11, 2):
                    nf = min(2, 11 - fp)
                    wt = self.getw()
                    wv = wt.ap[:, 0:8 * 2 * 256].rearrange("p (k u c) -> p k u c", u=2, c=256)
                    c = (f0 + fp) * 128
                    self.dma("pool", wv[:, :, 0, 0:nf * 128], wgu[:, :, c:c + nf * 128], writes=[wt])
                    self.dma("pool", wv[:, :, 1, 0:nf * 128], wgu[:, :, DFF + c:DFF + c + nf * 128], writes=[wt])
                    for ff in range(nf):
                        for h in range(2):
                            pg, pu = self.getps(), self.getps()
                            for u, ps in ((0, pg), (1, pu)):
                                for kc in range(8):
                                    self.mm(ps, ps.ap[:], wt, wv[:, kc, u, ff * 128:(ff + 1) * 128],
                                            ht[kc][h], ht[kc][h].ap, start=(kc == 0), stop=(kc == 7))
                            s = sl[(ff + h) % 2]
                            self.act(s, s.ap[:], pg, pg.ap[:], AF.Silu)
                            a = at[fp + ff][h]
                            self.tt("dve", a, a.ap, s, s.ap[:], pu, pu.ap[:], ALU.mult)
                for dp in range(0, 8, 2):
                    wt = self.getw()
                    wv = wt.ap[:, 0:11 * 256].rearrange("p (f c) -> p f c", c=256)
                    self.dma("pool", wv, wdn[:, f0:f0 + 11, dp * 128:(dp + 2) * 128], writes=[wt])
                    for dd in range(2):
                        dc = dp + dd
                        for h in range(2):
                            b = st * 2 + h
                            ps = self.getps()
                            for f in range(11):
                                self.mm(ps, ps.ap[:], wt, wv[:, f, dd * 128:(dd + 1) * 128],
                                        at[f][h], at[f][h].ap, start=(f == 0), stop=(f == 10))
                            x = self.xt[dc][b]
                            self.stt("dve", x, x.ap, ps, ps.ap[:], self.mod_ap(l, 3 * j + 2, dc, m), x, x.ap,
                                     ALU.mult, ALU.add, reads=[self.modv])
        P.pop_scope()

    def mixer(self, l):
        P = self.P
        S = (self.g == 1)
        TL = 512 if S else 256
        self.TL = TL
        nsub = TL // 128
        self.nsub = nsub
        NK = (DSEQ + PAST) if S else SEQ
        nkb = NK // 128
        ctxk = PAST if S else 0
        P.push_scope()
        M = {}
        self.M = M
        M["kA"] = P.sbuf("m_kA", [128, NK], BF16)
        M["kW"] = P.sbuf("m_kW", [128, NK], BF16)
        M["vA"] = P.sbuf("m_vA", [128, nkb, 2, 65], BF16)
        M["vW"] = P.sbuf("m_vW", [128, nkb, 2, 65], BF16)
        for n in ("vA", "vW"):
            self.memset("pool", M[n], M[n].ap[:, :, :, 64:65], 1.0)
        M["Sf"] = [P.sbuf(f"m_Sf{h}", [128, 128], F32) for h in range(4)]
        M["Sb"] = [P.sbuf(f"m_Sb{h}", [128, 128], F32) for h in range(4)]
        M["Sbf"] = [P.sbuf(f"m_Sbf{i}", [128, 128], BF16) for i in range(2)]
        ntile = (DSEQ // TL) if S else 1
        M["bound"] = [[P.sbuf(f"m_bd{t}_{h}", [128, 128], F32) for h in range(4)] for t in range(ntile - 1)]
        M["hT"] = P.sbuf("m_hT", [128, 8, TL], BF16)
        M["hts"] = [P.view(M["hT"].ap[:, kc, :]) for kc in range(8)]
        M["sq"] = [P.sbuf(f"m_sq{k}", [128, TL], BF16) for k in range(2)]
        M["f"] = [P.sbuf(f"m_f{k}", [128, TL], F32) for k in range(8)]
        M["qo"] = P.sbuf("m_qo", [128, 8, TL], BF16)
        M["qT"] = Tile(M["qo"].ap[:, 0:4, :], "qT")
        M["otok"] = Tile(M["qo"].ap[:, 4:8, :].rearrange("p a t -> p (a t)").rearrange("p (s f) -> p s f", f=512), "otok")
        M["mT"] = Tile(M["qo"].ap[:], "mT")
        for n_ in ("qT", "otok", "mT"):
            M[n_].w = M["qo"].w
            M[n_].r = M["qo"].r
        M["pT"] = [P.sbuf(f"m_pT{k}", [128, TL], BF16) for k in range(2)]
        M["oT"] = P.sbuf("m_oT", [128, 4, TL], BF16)
        M["mg"] = P.sbuf("m_mg", [128, 8, TL], F32)
        M["mgs"] = [P.view(M["mg"].ap[:, fc, :]) for fc in range(8)]
        M["b16"] = [P.sbuf(f"m_b16_{k}", [128, TL], BF16) for k in range(5)]
        M["khtok"] = P.sbuf("m_khtok", [128, nsub, 128], BF16)
        M["Vt"] = P.sbuf("m_Vt", [128, nsub, 512], BF16)
        M["AT"] = [P.sbuf(f"m_AT{k}", [128, 128], BF16) for k in range(4)]
        M["dec"] = P.sbuf("m_dec", [128, 2, 16], F32)
        M["sm"] = P.sbuf("m_sm", [128, 16], F32)
        M["cm"] = P.sbuf("m_cm", [128, TL], BF16)
        self.memset("pool", M["cm"], M["cm"].ap[:], 1.0)
        self.memset("pool", M["cm"], M["cm"].ap[:, 0:TL:32], 0.0)
        M["cm16"] = P.sbuf("m_cm16", [128, TL], BF16)
        self.memset("pool", M["cm16"], M["cm16"].ap[:], 1.0)
        self.memset("pool", M["cm16"], M["cm16"].ap[:, 0:TL:16], 0.0)
        if S:
            M["rc"] = P.sbuf("m_rc", [128, TL], F32)
            M["rs"] = P.sbuf("m_rs", [128, TL], F32)
        else:
            M["ko"] = P.sbuf("m_ko", [128, nsub, 2, 128], F32)
            M["vo"] = P.sbuf("m_vo", [128, nsub, 256], F32)

        if S:
            for ck, cv, kn, vn in ((self.ck_att, self.cv_att, "kA", "vA"), (self.ck_win, self.cv_win, "kW", "vW")):
                st = M["f"][0]
                stv = st.ap[:, 0:256].rearrange("p (b f) -> p b f", f=128)
                self.dma("sp", stv, ck[l].rearrange("(b p) f -> p b f", p=128), writes=[st])
                ps = self.getps()
                for b_ in range(2):
                    self.tr(ps, ps.ap[:, b_ * 128:(b_ + 1) * 128], st, stv[:, b_, :], self.ident, self.ident.ap[:])
                self.copy("act", M[kn], M[kn].ap[:, 0:256], ps, ps.ap[:, 0:256])
                for b_ in range(2):
                    self.dma("pool", M[vn].ap[:, b_, :, 0:64],
                             cv[l][b_ * 128:(b_ + 1) * 128, :].rearrange("p (g d) -> p g d", d=64), writes=[M[vn]])
            for h in range(4):
                self.dma("sp", M["Sf"][h].ap[:], self.state[l, 0, h], writes=[M["Sf"][h]])
                self.dma("sp", M["Sb"][h].ap[:], self.state[l, 1, h], writes=[M["Sb"][h]])
            tiles = [(t, t, 0, 512, t * 512) for t in range(4)]
            if SUB2 < 2:
                tiles = []
            for tl in reversed(tiles):
                self.pass1(l, tl, None)
            for tl in tiles:
                self.pass2(l, tl, None)
        else:
            for sq_ in range(NPROMPT):
                for h in range(4):
                    self.memset("pool", M["Sf"][h], M["Sf"][h].ap[:], 0.0)
                    self.memset("pool", M["Sb"][h], M["Sb"][h].ap[:], 0.0)
                tl = (0, sq_ // 2, (sq_ % 2) * 256, (sq_ % 2) * 256 + 256, 0)
                self.pass1(l, tl, sq_)
                self.pass2(l, tl, sq_)
                for h in range(4):
                    self.dma("sp", self.o_state[sq_, l, 0, h], M["Sf"][h].ap[:], reads=[M["Sf"][h]])
                    self.dma("sp", self.o_state[sq_, l, 1, h], M["Sb"][h].ap[:], reads=[M["Sb"][h]])
        P.pop_scope()

    def m_adaln(self, l, tl):
        M = self.M
        _, b, c0, c1, _ = tl
        self.adaln(l, 1, b, [(M["hts"][kc], M["hts"][kc].ap) for kc in range(8)], M["sq"],
                   M["f"][0], M["f"][1], M["f"][2:4], cols=(c0, c1))

    def proj_fm(self, wt, wap_fn, n=8):
        M = self.M
        ps = self.getps()
        for kc in range(n):
            self.mm(ps, ps.ap[:, 0:self.TL], wt, wap_fn(kc), M["hts"][kc], M["hts"][kc].ap,
                    start=(kc == 0), stop=(kc == n - 1))
        return ps

    def headnorm(self, ps, gcol, out_t, nrm_mat_ap, dim):
        M = self.M
        TL = self.TL
        sq = M["sq"][0]
        self.act(sq, sq.ap[:], ps, ps.ap[:, 0:TL], AF.Square)
        p2 = self.getps()
        self.mm(p2, p2.ap[:, 0:TL], self.cbf, nrm_mat_ap, sq, sq.ap[:], start=True, stop=True)
        vv, rstd = M["f"][4], M["f"][5]
        self.ts("dve", vv, vv.ap[:], p2, p2.ap[:, 0:TL], 1.0 / dim, EPS, ALU.mult, ALU.add)
        self.tt("pool", rstd, rstd.ap[:], vv, vv.ap[:], self.negh, self.negh.ap[:, 0:1].to_broadcast([128, TL]), ALU.pow)
        self.stt("dve", out_t, out_t.ap[:], ps, ps.ap[:, 0:TL], gcol, rstd, rstd.ap[:], ALU.mult, ALU.mult,
                 reads=[self.vecs])

    def rope_to(self, kn, dst_t, dst_ap):
        M = self.M
        TL = self.TL
        kb = M["sq"][1]
        if SUB2 < 4:
            self.copy("act", dst_t, dst_ap, kn, kn.ap[:])
            return
        self.copy("act", kb, kb.ap[:], kn, kn.ap[:])
        ps = self.getps()
        self.mm(ps, ps.ap[:, 0:TL], self.cbf, self.cbf.ap[:, 2, :], kb, kb.ap[:], start=True, stop=True)
        t1, t2 = M["f"][6], M["f"][7]
        if SUB2 < 5:
            self.copy("act", dst_t, dst_ap, ps, ps.ap[:, 0:TL])
            return
        self.tt("pool", t1, t1.ap[:], kn, kn.ap[:], M["rc"], M["rc"].ap[:], ALU.mult)
        if SUB2 < 6:
            self.copy("act", dst_t, dst_ap, t1, t1.ap[:])
            return
        self.tt("dve", t2, t2.ap[:], ps, ps.ap[:, 0:TL], M["rs"], M["rs"].ap[:], ALU.mult)
        if SUB2 < 7:
            self.copy("act", dst_t, dst_ap, t2, t2.ap[:])
            return
        self.tt("dve", dst_t, dst_ap, t1, t1.ap[:], t2, t2.ap[:], ALU.add)

    def sigmoid_parts(self, ps, r_t):
        TL = self.TL
        self.act(r_t, r_t.ap[:], ps, ps.ap[:, 0:TL], AF.Exp, scale=-1.0)
        self.ts("pool", r_t, r_t.ap[:], r_t, r_t.ap[:], 1.0, None, ALU.add)
        self.recip(r_t, r_t.ap[:], r_t, r_t.ap[:])

    def gates_f(self, l, h, r, ps, f_t, k_t, lg_t, b_t):
        M = self.M
        col = l * 8 + r * 4 + h
        lb = self.vecs.ap[:, self.c_lb + col:self.c_lb + col + 1]
        om = self.oml.ap[:, col:col + 1]
        self.sigmoid_parts(ps, f_t)
        self.ts("dve", f_t, f_t.ap[:], f_t, f_t.ap[:], om, lb, ALU.mult, ALU.add, reads=[self.vecs, self.oml])
        self.ts("pool", k_t, k_t.ap[:], f_t, f_t.ap[:], -1.0, 1.0, ALU.mult, ALU.add)
        self.act(lg_t, lg_t.ap[:], f_t, f_t.ap[:], AF.Ln)
        cm = M["cm"]
        self.P.op("dve", lambda e: e.tensor_tensor_scan(b_t.ap[:], cm.ap[:], lg_t.ap[:], 0.0, ALU.mult, ALU.add),
                  reads=[cm, lg_t], writes=[b_t])

    def vtok(self, wt, wv):
        M = self.M
        for sub in range(self.nsub):
            ps = self.getps()
            for kc in range(8):
                self.mm(ps, ps.ap[:], M["hts"][kc], M["hts"][kc].ap[:, sub * 128:(sub + 1) * 128],
                        wt, wv[:, kc, :], start=(kc == 0), stop=(kc == 7))
            self.copy("act", M["Vt"], M["Vt"].ap[:, sub, :], ps, ps.ap[:])

    def kh_transpose(self, kh):
        M = self.M
        ps = self.getps()
        pb = ps.ap[:].bitcast(BF16)
        for sub in range(self.nsub):
            self.tr(ps, pb[:, sub * 128:(sub + 1) * 128], kh, kh.ap[:, sub * 128:(sub + 1) * 128],
                    self.cbf, self.cbf.ap[:, 0, :])
        n = self.nsub * 128
        self.copy("act", M["khtok"], M["khtok"].ap[:].rearrange("p s d -> p (s d)"), ps, pb[:, 0:n])

    def u_mats(self, h, sub):
        M = self.M
        ps = self.getps()
        pv = ps.ap[:].rearrange("p (j v) -> p j v", v=128)
        for j in range(4):
            tok = self.mm(ps, pv[:, j, :], M["khtok"], M["khtok"].ap[32 * j:32 * j + 32, sub, :],
                          M["Vt"], M["Vt"].ap[32 * j:32 * j + 32, sub, h * 128:(h + 1) * 128],
                          start=True, stop=True, tile_position=(32 * j, 0))
            self.P._wait("pe", tok[0], tok[1])
        return ps, pv

    def pass1(self, l, tl, seq):
        M = self.M
        P = self.P
        S = (self.g == 1)
        TL, nsub = self.TL, self.nsub
        idx, b, c0, c1, soff = tl
        koff = (PAST if S else 0) + soff
        wi = self.w_in[l].rearrange("(kc p) f -> p kc f", p=128)
        self.m_adaln(l, tl)
        if S:
            self.dma("sp", M["rc"].ap[:], self.rope[0][:, soff:soff + TL], writes=[M["rc"]])
            self.dma("sp", M["rs"].ap[:], self.rope[1][:, soff:soff + TL], writes=[M["rs"]])
        if SUB2 < 3:
            return
        wt = self.getw()
        wv = wt.ap[:, 0:8 * 256].rearrange("p (k c) -> p k c", c=256)
        self.dma("pool", wv[:, :, 0:128], wi[:, :, O_AK:O_AK + 128], writes=[wt])
        self.dma("pool", wv[:, :, 128:256], wi[:, :, O_WK:O_WK + 128], writes=[wt])
        for which, kn_name in ((0, "kA"), (1, "kW")):
            ps = self.proj_fm(wt, lambda kc, w=which: wv[:, kc, w * 128:(w + 1) * 128])
            kn = M["f"][2 + which]
            if which == 0:
                gk = self.vecs.ap[:, self.c_qk + l * 2 + 1:self.c_qk + l * 2 + 2]
                self.headnorm(ps, gk, kn, self.cbf.ap[:, 1, :], 64)
            else:
                self.copy("act", kn, kn.ap[:], ps, ps.ap[:, 0:TL])
            dst = M[kn_name]
            if S:
                self.rope_to(kn, dst, dst.ap[:, koff:koff + TL])
            else:
                self.copy("act", dst, dst.ap[:, koff:koff + TL], kn, kn.ap[:])
                for sub in range(nsub):
                    p2 = self.getps()
                    self.tr(p2, p2.ap[:, 0:128], kn, kn.ap[:, sub * 128:(sub + 1) * 128], self.ident, self.ident.ap[:])
                    self.copy("act", M["ko"], M["ko"].ap[:, sub, which, :], p2, p2.ap[:, 0:128])
        if not S:
            self.dma("sp", self.o_k_att[seq, l].rearrange("(s p) f -> p s f", p=128), M["ko"].ap[:, :, 0, :], reads=[M["ko"]])
            self.dma("sp", self.o_k_win[seq, l].rearrange("(s p) f -> p s f", p=128), M["ko"].ap[:, :, 1, :], reads=[M["ko"]])
        if SUB < 2:
            return
        wt = self.getw()
        wv2 = wt.ap[:, 0:8 * 256].rearrange("p (k c) -> p k c", c=256)
        self.dma("pool", wv2[:, :, 0:128], wi[:, :, O_AV:O_AV + 128], writes=[wt])
        self.dma("pool", wv2[:, :, 128:256], wi[:, :, O_WV:O_WV + 128], writes=[wt])
        for sub in range(nsub):
            ps = self.getps()
            for kc in range(8):
                self.mm(ps, ps.ap[:, 0:256], M["hts"][kc], M["hts"][kc].ap[:, sub * 128:(sub + 1) * 128],
                        wt, wv2[:, kc, :], start=(kc == 0), stop=(kc == 7))
            kb = koff // 128 + sub
            self.copy("act", M["vA"], M["vA"].ap[:, kb, :, 0:64], ps, ps.ap[:, 0:128].rearrange("p (g d) -> p g d", d=64))
            self.copy("dve", M["vW"], M["vW"].ap[:, kb, :, 0:64], ps, ps.ap[:, 128:256].rearrange("p (g d) -> p g d", d=64))
            if not S:
                self.copy("act", M["vo"], M["vo"].ap[:, sub, :], ps, ps.ap[:, 0:256])
        if not S:
            self.dma("sp", self.o_v_att[seq, l].rearrange("(s p) f -> p s f", p=128), M["vo"].ap[:, :, 0:128], reads=[M["vo"]])
            self.dma("sp", self.o_v_win[seq, l].rearrange("(s p) f -> p s f", p=128), M["vo"].ap[:, :, 128:256], reads=[M["vo"]])
        if SUB < 3:
            return
        wti = self.getw()
        wvi = wti.ap[:, 0:4096].rearrange("p (k c) -> p k c", c=512)
        self.dma("pool", wvi, wi[:, :, O_HI:O_HI + 512], writes=[wti])
        self.vtok(wti, wvi)
        wtb = self.getw()
        wvb = wtb.ap[:, 0:4096].rearrange("p (k c) -> p k c", c=512)
        self.dma("pool", wvb, wi[:, :, O_HB:O_HB + 512], writes=[wtb])
        nch = TL // 32
        for h in range(4):
            if idx < len(M["bound"]):
                self.copy("pool", M["bound"][idx][h], M["bound"][idx][h].ap[:], M["Sb"][h], M["Sb"][h].ap[:])
            ps = self.proj_fm(wtb, lambda kc, h=h: wvb[:, kc, h * 128:(h + 1) * 128])
            f_t, k_t, lg_t, b_t = M["f"][2], M["f"][3], M["f"][4], M["f"][5]
            self.gates_f(l, h, 1, ps, f_t, k_t, lg_t, b_t)
            self.khat_dec_b(k_t, lg_t, b_t, M["b16"][0], 1)
            self.kh_transpose(M["b16"][0])
            Sb = M["Sb"][h]
            for sub in reversed(range(nsub)):
                psu, pv = self.u_mats(h, sub)
                for j in reversed(range(4)):
                    c = sub * 4 + j
                    self.stt("dve", Sb, Sb.ap[:], Sb, Sb.ap[:], M["dec"].ap[:, 1, c:c + 1], psu, pv[:, j, :],
                             ALU.mult, ALU.add, reads=[M["dec"]])

    def khat_dec_b(self, k_t, lg_t, b_t, kh16, r):
        M = self.M
        TL = self.TL
        nch = TL // 32
        t = M["f"][6]
        self.tt("pool", t, t.ap[:], b_t, b_t.ap[:], lg_t, lg_t.ap[:], ALU.subtract)
        self.act(t, t.ap[:], t, t.ap[:], AF.Exp)
        self.tt("dve", kh16, kh16.ap[:], k_t, k_t.ap[:], t, t.ap[:], ALU.mult)
        self.act(M["dec"], M["dec"].ap[:, r, 0:nch], b_t, b_t.ap[:, 31:TL:32], AF.Exp)

    def attention(self, l, tl, which):
        M = self.M
        S = (self.g == 1)
        TL, nsub = self.TL, self.nsub
        idx, b, c0, c1, soff = tl
        wi = self.w_in[l].rearrange("(kc p) f -> p kc f", p=128)
        o_q = O_AQ if which == 0 else O_WQ
        kX, vX = (M["kA"], M["vA"]) if which == 0 else (M["kW"], M["vW"])
        wt = self.getw()
        wv = wt.ap[:, 0:4096].rearrange("p (k c) -> p k c", c=512)
        for c in range(4):
            self.dma("pool", wv[:, :, c * 128:c * 128 + 64], wi[:, :, o_q + c * 64:o_q + c * 64 + 64], writes=[wt])
            self.dma("pool", wv[:, :, c * 128 + 64:c * 128 + 128], wi[:, :, o_q + (4 + c) * 64:o_q + (4 + c) * 64 + 64], writes=[wt])
        qT = M["qT"]
        for c in range(4):
            ps = self.proj_fm(wt, lambda kc, c=c: wv[:, kc, c * 128:(c + 1) * 128])
            if which == 0:
                qn = M["f"][2]
                gq = self.vecs.ap[:, self.c_qk + l * 2:self.c_qk + l * 2 + 1]
                self.headnorm(ps, gq, qn, self.cbf.ap[:, 1, :], 64)
                if S:
                    self.rope_to(qn, qT, qT.ap[:, c, :])
                else:
                    self.copy("act", qT, qT.ap[:, c, :], qn, qn.ap[:])
            else:
                if S:
                    qn = M["f"][2]
                    self.copy("act", qn, qn.ap[:], ps, ps.ap[:, 0:TL])
                    self.rope_to(qn, qT, qT.ap[:, c, :])
                else:
                    self.copy("act", qT, qT.ap[:, c, :], ps, ps.ap[:, 0:TL])
        sched = []
        if which == 0 or not S:
            nkb = (DSEQ + PAST) // 128 if S else SEQ // 128
            sched = [(kb, 0, nsub - 1, {}) for kb in range(nkb)]
        else:
            sched = [(0, 0, nsub - 1, {}), (1, 0, nsub - 1, {})]
            qb0 = soff // 128
            for kbi in range(qb0 - 1, qb0 + nsub + 1):
                if kbi < 0 or kbi >= DSEQ // 128:
                    continue
                lo = max(kbi - 1, qb0) - qb0
                hi = min(kbi + 1, qb0 + nsub - 1) - qb0
                masks = {}
                for sub in range(lo, hi + 1):
                    qb = qb0 + sub
                    if kbi == qb - 1:
                        masks[sub] = 3
                    elif kbi == qb + 1:
                        masks[sub] = 4
                sched.append((2 + kbi, lo, hi, masks))
        otok = M["otok"]
        for c in range(4):
            for a in range(2):
                head = c + 4 * a
                acc = self.getpsl()
                av = acc.ap[:, 0:nsub * 65].rearrange("p (s e) -> p s e", e=65)
                first = {sub: True for sub in range(nsub)}
                started = False
                last_kb = {}
                for (kb, lo, hi, masks) in sched:
                    for sub in range(lo, hi + 1):
                        last_kb[sub] = kb
                for si, (kb, lo, hi, masks) in enumerate(sched):
                    n0, n1 = lo * 128, (hi + 1) * 128
                    sps = self.getps()
                    self.mm(sps, sps.ap[:, n0:n1], kX, kX.ap[64 * a:64 * a + 64, kb * 128:(kb + 1) * 128],
                            qT, qT.ap[64 * a:64 * a + 64, c, n0:n1], start=True, stop=True)
                    pT = M["pT"][si % 2]
                    self.act(pT, pT.ap[:, n0:n1], sps, sps.ap[:, n0:n1], AF.Exp, scale=0.125)
                    for sub, mi in masks.items():
                        self.tt("pool", pT, pT.ap[:, sub * 128:(sub + 1) * 128], pT, pT.ap[:, sub * 128:(sub + 1) * 128],
                                self.cbf, self.cbf.ap[:, mi, :], ALU.mult)
                    for sub in range(lo, hi + 1):
                        self.mm(acc, av[:, sub, :], pT, pT.ap[:, sub * 128:(sub + 1) * 128],
                                vX, vX.ap[:, kb, a, :], start=(not started), stop=(last_kb[sub] == kb),
                                skip_group_check=True)
                        started = True
                sm = M["sm"]
                if which == 1:
                    self.ts("dve", sm, sm.ap[:, 0:nsub], acc, av[:, :, 64], self.esink.ap[:, l * 8 + head:l * 8 + head + 1],
                            None, ALU.add, reads=[self.esink])
                    self.recip(sm, sm.ap[:, 0:nsub], sm, sm.ap[:, 0:nsub])
                else:
                    self.recip(sm, sm.ap[:, 0:nsub], acc, av[:, :, 64])
                self.tt("dve", otok, otok.ap[:, :, head * 64:(head + 1) * 64], acc, av[:, :, 0:64],
                        sm, sm.ap[:, 0:nsub].unsqueeze(2).to_broadcast([128, nsub, 64]), ALU.mult)
        for k4 in range(4):
            ps = self.getps()
            pb = ps.ap[:].bitcast(BF16)
            for sub in range(nsub):
                self.tr(ps, pb[:, sub * 128:(sub + 1) * 128], otok, otok.ap[:, sub, k4 * 128:(k4 + 1) * 128],
                        self.cbf, self.cbf.ap[:, 0, :])
            self.copy("act", M["oT"], M["oT"].ap[:, k4, :], ps, pb[:, 0:TL])

    def hgrn2(self, l, tl):
        M = self.M
        S = (self.g == 1)
        TL, nsub = self.TL, self.nsub
        idx, b, c0, c1, soff = tl
        nch = TL // 32
        wi = self.w_in[l].rearrange("(kc p) f -> p kc f", p=128)
        wti = self.getw()
        wvi = wti.ap[:, 0:4096].rearrange("p (k c) -> p k c", c=512)
        self.dma("pool", wvi, wi[:, :, O_HI:O_HI + 512], writes=[wti])
        self.vtok(wti, wvi)
        F = M["f"]
        for h in range(4):
            wt = self.getw()
            wv = wt.ap[:, 0:4096].rearrange("p (k c) -> p k c", c=512)
            for ti, off in enumerate((O_HQ, O_HF, O_HB, O_HG)):
                self.dma("pool", wv[:, :, ti * 128:(ti + 1) * 128], wi[:, :, off + h * 128:off + (h + 1) * 128], writes=[wt])
            ps = self.proj_fm(wt, lambda kc: wv[:, kc, 0:128])
            Q = F[0]
            self.sigmoid_parts(ps, Q)
            self.tt("dve", Q, Q.ap[:], Q, Q.ap[:], ps, ps.ap[:, 0:TL], ALU.mult)
            ps = self.proj_fm(wt, lambda kc: wv[:, kc, 384:512])
            OG = F[1]
            self.sigmoid_parts(ps, OG)
            self.tt("dve", OG, OG.ap[:], OG, OG.ap[:], ps, ps.ap[:, 0:TL], ALU.mult)
            osum = F[7]
            for r in range(2):
                ps = self.proj_fm(wt, lambda kc, r=r: wv[:, kc, 128 + r * 128:256 + r * 128])
                f_t, k_t, lg_t, b_t = F[2], F[3], F[4], F[5]
                self.gates_f(l, h, r, ps, f_t, k_t, lg_t, b_t)
                qI, q1, k1, k2, kh16 = M["b16"]
                X, Y = f_t, F[6]
                n16 = TL // 16
                bl = b_t.ap[:].rearrange("p (c i) -> p c i", i=32)[:, :, 31:32].to_broadcast([128, nch, 32])
                b3 = b_t.ap[:].rearrange("p (c i) -> p c i", i=32)
                v32 = lambda t_: t_.ap[:].rearrange("p (c i) -> p c i", i=32)
                v16 = lambda t_: t_.ap[:].rearrange("p (c i) -> p c i", i=16)
                xl16 = X.ap[:].rearrange("p (c i) -> p c i", i=16)[:, :, 15:16].to_broadcast([128, n16, 16])
                cm16 = M["cm16"]

                def expmul(dst16, src_t, scale, base_t):
                    tmp = Y if src_t is X else X
                    self.act(tmp, tmp.ap[:], src_t, src_t.ap[:], AF.Exp, scale=scale)
                    self.tt("dve", dst16, dst16.ap[:], base_t, base_t.ap[:], tmp, tmp.ap[:], ALU.mult)

                if r == 0:
                    self.tt("pool", Y, v32(Y), b_t, bl, b_t, b3, ALU.subtract)
                    expmul(kh16, Y, 1.0, k_t)
                    self.act(M["dec"], M["dec"].ap[:, 0, 0:nch], b_t, b_t.ap[:, 31:TL:32], AF.Exp)
                    self.act(Y, Y.ap[:], b_t, b_t.ap[:], AF.Exp)
                    self.tt("dve", qI, qI.ap[:], Q, Q.ap[:], Y, Y.ap[:], ALU.mult)
                    self.P.op("dve", lambda e: e.tensor_tensor_scan(X.ap[:], cm16.ap[:], lg_t.ap[:], 0.0, ALU.mult, ALU.add),
                              reads=[cm16, lg_t], writes=[X])
                    expmul(q1, X, 1.0, Q)
                    expmul(k1, X, -1.0, k_t)
                    self.tt("pool", Y, v16(Y), X, xl16, X, v16(X), ALU.subtract)
                    self.act(Y, Y.ap[:], Y, Y.ap[:], AF.Exp)
                    self.tt("dve", k2, k2.ap[:], k_t, k_t.ap[:], Y, Y.ap[:], ALU.mult)
                else:
                    self.khat_dec_b(k_t, lg_t, b_t, kh16, 1)
                    self.tt("pool", X, v32(X), b_t, bl, b_t, b3, ALU.subtract)
                    self.tt("pool", X, X.ap[:], X, X.ap[:], lg_t, lg_t.ap[:], ALU.add)
                    expmul(qI, X, 1.0, Q)
                    self.P.op("dve", lambda e: e.tensor_tensor_scan(X.ap[:], cm16.ap[:], lg_t.ap[:], 0.0, ALU.mult, ALU.add),
                              reads=[cm16, lg_t], writes=[X])
                    self.tt("pool", Y, Y.ap[:], X, X.ap[:], lg_t, lg_t.ap[:], ALU.subtract)
                    self.act(Y, Y.ap[:], Y, Y.ap[:], AF.Exp)
                    self.tt("dve", k2, k2.ap[:], k_t, k_t.ap[:], Y, Y.ap[:], ALU.mult)
                    self.tt("pool", Y, v16(Y), X, xl16, X, v16(X), ALU.subtract)
                    self.tt("pool", Y, Y.ap[:], Y, Y.ap[:], lg_t, lg_t.ap[:], ALU.add)
                    expmul(q1, Y, 1.0, Q)
                    expmul(k1, Y, -1.0, k_t)
                qt16 = qI
                self.kh_transpose(kh16)
                St = M["Sf"][h] if r == 0 else M["Sb"][h]
                if r == 1:
                    if idx < len(M["bound"]):
                        self.copy("pool", St, St.ap[:], M["bound"][idx][h], M["bound"][idx][h].ap[:])
                    elif S:
                        self.dma("sp", St.ap[:], self.state[l, 1, h], writes=[St])
                    else:
                        self.memset("pool", St, St.ap[:], 0.0)
                acc = self.getpsl()
                started = False
                subs = range(nsub) if r == 0 else reversed(range(nsub))
                nbf = 0
                for sub in subs:
                    n0 = sub * 128
                    for wi_, kx in enumerate((k1, k2)):
                        sps = self.getps()
                        self.mm(sps, sps.ap[:, 0:128], kx, kx.ap[:, n0:n0 + 128], q1, q1.ap[:, n0:n0 + 128],
                                start=True, stop=True)
                        AT = M["AT"][(sub % 2) * 2 + wi_]
                        self.tt("dve", AT, AT.ap[:], sps, sps.ap[:, 0:128], self.cbf, self.cbf.ap[:, 7 + 2 * r + wi_, :], ALU.mult)
                        self.mm(acc, acc.ap[:, n0:n0 + 128], M["Vt"], M["Vt"].ap[:, sub, h * 128:(h + 1) * 128],
                                AT, AT.ap[:], start=(not started), stop=False, skip_group_check=True)
                        started = True
                    psu, pv = self.u_mats(h, sub)
                    js = range(4) if r == 0 else reversed(range(4))
                    for j in js:
                        c = sub * 4 + j
                        sb16 = M["Sbf"][nbf % 2]
                        nbf += 1
                        self.copy("act", sb16, sb16.ap[:], St, St.ap[:])
                        self.mm(acc, acc.ap[:, c * 32:(c + 1) * 32], sb16, sb16.ap[:], qt16, qt16.ap[:, c * 32:(c + 1) * 32],
                                start=False, stop=True, skip_group_check=True)
                        self.stt("dve", St, St.ap[:], St, St.ap[:], M["dec"].ap[:, r, c:c + 1], psu, pv[:, j, :],
                                 ALU.mult, ALU.add, reads=[M["dec"]])
                if r == 0:
                    self.copy("act", osum, osum.ap[:], acc, acc.ap[:, 0:TL])
                else:
                    self.tt("dve", osum, osum.ap[:], osum, osum.ap[:], acc, acc.ap[:, 0:TL], ALU.add)
            sq = M["sq"][0]
            self.act(sq, sq.ap[:], osum, osum.ap[:], AF.Square)
            p2 = self.getps()
            self.mm(p2, p2.ap[:, 0:TL], self.ones, self.ones.ap[:], sq, sq.ap[:], start=True, stop=True)
            vv, rstd = F[4], F[5]
            self.ts("dve", vv, vv.ap[:], p2, p2.ap[:, 0:TL], 1.0 / 128, EPS, ALU.mult, ALU.add)
            self.tt("pool", rstd, rstd.ap[:], vv, vv.ap[:], self.negh, self.negh.ap[:, 0:1].to_broadcast([128, TL]), ALU.pow)
            gcol = self.vecs.ap[:, self.c_hg + l:self.c_hg + l + 1]
            self.stt("dve", osum, osum.ap[:], osum, osum.ap[:], gcol, rstd, rstd.ap[:], ALU.mult, ALU.mult, reads=[self.vecs])
            self.tt("dve", M["oT"], M["oT"].ap[:, h, :], osum, osum.ap[:], OG, OG.ap[:], ALU.mult)

    def merge_branch(self, l, k, first, last):
        M = self.M
        TL = self.TL
        wi = self.w_in[l].rearrange("(kc p) f -> p kc f", p=128)
        off = (O_GA, O_GB, O_GC)[k]
        wb = self.getw()
        wbv = wb.ap[:, 0:4096].rearrange("p (k c) -> p k c", c=1024)
        self.dma("pool", wbv, self.w_branch[l, k].rearrange("(kc p) f -> p kc f", p=128), writes=[wb])
        F = M["f"]
        for half in range(2):
            wg = self.getw()
            wgv = wg.ap[:, 0:4096].rearrange("p (k c) -> p k c", c=512)
            self.dma("pool", wgv, wi[:, :, off + half * 512:off + (half + 1) * 512], writes=[wg])
            for f4 in range(4):
                fc = half * 4 + f4
                gps = self.proj_fm(wg, lambda kc, f4=f4: wgv[:, kc, f4 * 128:(f4 + 1) * 128])
                bps = self.getps()
                for k4 in range(4):
                    self.mm(bps, bps.ap[:, 0:TL], wb, wbv[:, k4, fc * 128:(fc + 1) * 128], M["oT"], M["oT"].ap[:, k4, :],
                            start=(k4 == 0), stop=(k4 == 3))
                r = F[fc % 2]
                self.sigmoid_parts(gps, r)
                mg = M["mgs"][fc]
                if first:
                    self.tt("dve", mg, mg.ap, r, r.ap[:], bps, bps.ap[:, 0:TL], ALU.mult)
                else:
                    self.tt("dve", r, r.ap[:], r, r.ap[:], bps, bps.ap[:, 0:TL], ALU.mult)
                    if last:
                        self.tt("pool", M["mT"], M["mT"].ap[:, fc, :], mg, mg.ap, r, r.ap[:], ALU.add)
                    else:
                        self.tt("pool", mg, mg.ap, mg, mg.ap, r, r.ap[:], ALU.add)

    def pass2(self, l, tl, seq):
        M = self.M
        S = (self.g == 1)
        TL = self.TL
        idx, b, c0, c1, soff = tl
        m = 1 if S else 0
        self.m_adaln(l, tl)
        if S:
            self.dma("sp", M["rc"].ap[:], self.rope[0][:, soff:soff + TL], writes=[M["rc"]])
            self.dma("sp", M["rs"].ap[:], self.rope[1][:, soff:soff + TL], writes=[M["rs"]])
        if SUB < 4:
            return
        self.hgrn2(l, tl)
        if SUB < 5:
            return
        self.merge_branch(l, 1, True, False)
        if SUB < 6:
            return
        self.attention(l, tl, 0)
        self.merge_branch(l, 0, False, False)
        if SUB < 7:
            return
        self.attention(l, tl, 1)
        self.merge_branch(l, 2, False, True)
        wo = self.w_out[l].rearrange("(kc p) f -> p kc f", p=128)
        for half in range(2):
            wt = self.getw()
            wv = wt.ap[:, 0:4096].rearrange("p (k c) -> p k c", c=512)
            self.dma("pool", wv, wo[:, :, half * 512:(half + 1) * 512], writes=[wt])
            for d4 in range(4):
                dc = half * 4 + d4
                ps = self.getps()
                for fc in range(8):
                    self.mm(ps, ps.ap[:, 0:TL], wt, wv[:, fc, d4 * 128:(d4 + 1) * 128], M["mT"], M["mT"].ap[:, fc, :],
                            start=(fc == 0), stop=(fc == 7))
                x = self.xt[dc][b]
                self.stt("dve", x, x.ap[:, c0:c1], ps, ps.ap[:, 0:TL], self.mod_ap(l, 5, dc, m), x, x.ap[:, c0:c1],
                         ALU.mult, ALU.add, reads=[self.modv])


def _consts():
    c = np.zeros((12, 128, 128), np.float32)
    c[0] = np.eye(128, dtype=np.float32)
    c[1] = np.eye(128, dtype=np.float32)
    j = np.arange(128)[:, None]
    p = np.arange(128)[None, :]
    c[2] = (j // 64 == p // 64)
    rot = np.zeros((128, 128), np.float32)
    for i in range(128):
        d = i % 32
        if d < 16:
            rot[i + 16, i] = -1.0
        else:
            rot[i - 16, i] = 1.0
    c[3] = rot
    c[4] = (j >= p)
    c[5] = (j <= p)
    c[6] = (j // 32 == p // 32) & (j <= p)
    c[7] = (j // 32 == p // 32) & (j >= p)
    c[8] = (j // 16 == p // 16) & (j <= p)
    c[9] = (j // 32 == p // 32) & (j % 32 < 16) & (p % 32 >= 16)
    c[10] = (j // 16 == p // 16) & (j >= p)
    c[11] = (j // 32 == p // 32) & (j % 32 >= 16) & (p % 32 < 16)
    t = np.arange(DSEQ)
    inv = 10000.0 ** (-np.arange(0, 32, 2, dtype=np.float64) / 32)
    dd = np.arange(128) % 64
    pos = np.where(dd[:, None] < 32, (t // 64)[None, :], (t % 64)[None, :]).astype(np.float64)
    ang = pos * inv[dd % 16][:, None]
    rope = np.stack([np.cos(ang), np.sin(ang)]).astype(np.float32)
    return c, rope


_CACHE = {}


def _get_nc(stage=99, nlayers=DEPTH, LW=DEPTH):
    key = (stage, nlayers, LW)
    if key not in _CACHE:
        _CACHE[key] = Builder(stage, nlayers, LW).build()
    return _CACHE[key]


def kernel(x_prompt, x_sample, cache_k_attn, cache_v_attn, cache_k_win, cache_v_win, state_hgrn,
           c, c_ctx, w_mod, b_mod, norm_g, w_ffn_in, w_ffn_out, w_in, qk_norm_g, lower_bounds,
           hg_norm_g, sink_logit, w_branch, w_out, final_norm_g, _stage=99, _nlayers=DEPTH, _ncores=NCORES):
    f = lambda a: np.ascontiguousarray(np.asarray(a, dtype=np.float32))
    LW = int(np.asarray(w_mod).shape[0])
    DEPTH = LW
    NCORES = _ncores
    nc = _get_nc(_stage, _nlayers, LW)
    cst, rope = _consts()
    shared = dict(w_mod=f(w_mod), b_mod=f(b_mod), norm_g=f(norm_g), w_ffn_in=f(w_ffn_in),
                  w_ffn_out=f(w_ffn_out), w_in=f(w_in), qk_norm_g=f(qk_norm_g),
                  lower_bounds=f(lower_bounds), hg_norm_g=f(hg_norm_g), sink_logit=f(sink_logit),
                  w_branch=f(w_branch), w_out=f(w_out), final_norm_g=f(final_norm_g), cst=cst, rope=rope)
    xp = f(x_prompt).reshape(-1, NPROMPT * SEQ, D)
    xs = f(x_sample)
    in_maps = []
    for k in range(NCORES):
        d = dict(shared)
        d["xin"] = np.concatenate([xp[k], xs[k]], axis=0)
        d["cond"] = np.stack([f(c_ctx), f(c)[k]], axis=0)
        d["ck_att"] = f(cache_k_attn)[k].reshape(DEPTH, PAST, 128)
        d["cv_att"] = f(cache_v_attn)[k].reshape(DEPTH, PAST, 128)
        d["ck_win"] = f(cache_k_win)[k].reshape(DEPTH, PAST, 128)
        d["cv_win"] = f(cache_v_win)[k].reshape(DEPTH, PAST, 128)
        d["state"] = f(state_hgrn)[k]
        in_maps.append(d)
    res = run_bass_kernel_spmd(nc, in_maps, core_ids=list(range(NCORES)))
    R = res.results
    y = np.stack([r["y"] for r in R])
    y_prompt = y[:, :NPROMPT * SEQ].reshape(NCORES * NPROMPT, SEQ, D)
    y_sample = y[:, NPROMPT * SEQ:]
    cat = lambda n, shp: np.concatenate([r[n] for r in R], axis=0).reshape(shp)
    return (y_prompt, y_sample,
            cat("o_k_att", (-1, DEPTH, SEQ, 2, 64)), cat("o_v_att", (-1, DEPTH, SEQ, 2, 64)),
            cat("o_k_win", (-1, DEPTH, SEQ, 2, 64)), cat("o_v_win", (-1, DEPTH, SEQ, 2, 64)),
            cat("o_state", (-1, DEPTH, 2, 4, 128, 128)))
```

```python
from contextlib import ExitStack
import math
import os
import numpy as np
SUB = int(os.environ.get("MK_SUB", "99"))
SUB2 = int(os.environ.get("MK_SUB2", "99"))

import concourse.bass as bass
import concourse.mybir as mybir
from concourse.bass_utils import run_bass_kernel_spmd

F32 = mybir.dt.float32
BF16 = mybir.dt.bfloat16
AF = mybir.ActivationFunctionType
ALU = mybir.AluOpType

D = 1024
DFF = 2816
DEPTH = 4
NPROMPT = 4
SEQ = 256
DSEQ = 2048
PAST = 256
INW = 7168
EPS = 1e-6
NCORES = 8

O_AQ, O_AK, O_AV = 0, 512, 640
O_WQ, O_WK, O_WV = 768, 1280, 1408
O_HQ, O_HF, O_HB, O_HI, O_HG = 1536, 2048, 2560, 3072, 3584
O_GA, O_GB, O_GC = 4096, 5120, 6144

EPOCH = 12000
DMA_RING = 12


class Tile:
    __slots__ = ("ap", "w", "r", "name", "psum")

    def __init__(self, ap, name=""):
        self.ap = ap
        self.w = {}
        self.r = {}
        self.name = name
        self.psum = False

    def inherit(self, others):
        for o in others:
            for src in (o.w, o.r):
                for k, v in src.items():
                    if self.w.get(k, 0) < v:
                        self.w[k] = v


class Prog:
    ENGS = ("pe", "act", "dve", "pool", "sp")

    def __init__(self, nc):
        self.nc = nc
        self.items = {e: [] for e in self.ENGS}
        self.count = {e: 0 for e in self.ENGS}
        self.epoch = {e: 0 for e in self.ENGS}
        self.seen = {e: {} for e in self.ENGS}
        self.semkeys = []
        self.semset = set()
        self.dma_next = {e: 0 for e in self.ENGS}
        self.dma_val = {}
        self.stack = ExitStack()
        self.scopes = []
        self.residue = Tile(None, "residue")
        self.nops = 0

    def _reg(self, t):
        t.inherit([self.residue])
        if self.scopes:
            self.scopes[-1][1].append(t)
        return t

    def sbuf(self, name, shape, dtype):
        st = self.scopes[-1][0] if self.scopes else self.stack
        self.uid = getattr(self, "uid", 0) + 1
        name = f"{name}_{self.uid}"
        h = st.enter_context(self.nc.sbuf_tensor(name, list(shape), dtype))
        return self._reg(Tile(h, name))

    def view(self, ap, name=""):
        return self._reg(Tile(ap, name))

    def psum(self, name, shape, dtype):
        h = self.stack.enter_context(self.nc.psum_tensor(name, list(shape), dtype))
        t = Tile(h, name)
        t.psum = True
        return t

    def push_scope(self):
        self.scopes.append((ExitStack(), []))

    def pop_scope(self):
        st, tiles = self.scopes.pop()
        self.residue.inherit(tiles)
        st.close()

    def _sem(self, key):
        if key not in self.semset:
            self.semset.add(key)
            self.semkeys.append(key)
        return key

    def _wait(self, eng, key, val):
        if self.seen[eng].get(key, 0) >= val:
            return
        self.seen[eng][key] = val
        self.items[eng].append(("wait", key, val))

    def _deps(self, eng, reads, writes):
        need = {}
        for t in reads:
            for k, v in t.w.items():
                if need.get(k, 0) < v:
                    need[k] = v
            if t.psum:
                for k, v in t.r.items():
                    if k[0] != eng and need.get(k, 0) < v:
                        need[k] = v
        for t in writes:
            for d in (t.w, t.r):
                for k, v in d.items():
                    if need.get(k, 0) < v:
                        need[k] = v
        for k, v in need.items():
            if eng == "pe" and k[0] == "pe":
                continue
            self._wait(eng, k, v)

    def _mark(self, reads, writes, key, val):
        for t in reads:
            if t.r.get(key, 0) < val:
                t.r[key] = val
        for t in writes:
            if t.w.get(key, 0) < val:
                t.w[key] = val

    def op(self, eng, fn, reads=(), writes=()):
        if self.count[eng] >= EPOCH:
            self.epoch[eng] += 1
            self.count[eng] = 0
        key = self._sem((eng, self.epoch[eng]))
        self._deps(eng, reads, writes)
        self.count[eng] += 1
        val = self.count[eng]
        self.items[eng].append(("op", fn, key))
        self._mark(reads, writes, key, val)
        self.nops += 1
        return (key, val)

    def dma(self, eng, fn, reads=(), writes=()):
        i = self.dma_next[eng]
        self.dma_next[eng] = (i + 1) % DMA_RING
        key = self._sem(("dma", eng, i))
        prev = self.dma_val.get(key, 0)
        if prev:
            self._wait(eng, key, prev)
        self._deps(eng, reads, writes)
        val = prev + 16
        self.dma_val[key] = val
        self.items[eng].append(("dma", fn, key))
        self._mark(reads, writes, key, val)

    def wait_all_dmas(self, eng):
        for key, val in self.dma_val.items():
            self._wait(eng, key, val)

    def emit(self):
        nc = self.nc
        sems = {}
        for key in self.semkeys:
            nm = "s_" + "_".join(str(x) for x in key)
            sems[key] = self.stack.enter_context(nc.semaphore(nm))
        items = self.items

        def run(handle, lst):
            for it in lst:
                if it[0] == "wait":
                    handle.wait_ge(sems[it[1]], it[2])
                elif it[0] == "op":
                    it[1](handle).then_inc(sems[it[2]], 1)
                else:
                    it[1](handle).then_inc(sems[it[2]], 16)

        with nc.Block() as block:
            @block.tensor
            def _(e):
                run(e, items["pe"])

            @block.scalar
            def _(e):
                run(e, items["act"])

            @block.vector
            def _(e):
                run(e, items["dve"])

            @block.gpsimd
            def _(e):
                run(e, items["pool"])

            @block.sync
            def _(e):
                run(e, items["sp"])
        self.stack.close()


class Builder:
    def __init__(self, stage=99, nlayers=DEPTH, LW=DEPTH):
        self.stage = stage
        self.nlayers = nlayers
        self.LW = LW
        DEPTH = LW
        nc = bass.Bass("TRN2", target_bir_lowering=False)
        self.nc = nc
        self.P = Prog(nc)

        def din(name, shape):
            return nc.dram_tensor(name, list(shape), F32, kind="ExternalInput").ap()

        def dout(name, shape):
            return nc.dram_tensor(name, list(shape), F32, kind="ExternalOutput").ap()

        self.xin = din("xin", [3072, D])
        self.cond = din("cond", [2, D])
        self.ck_att = din("ck_att", [DEPTH, PAST, 128])
        self.cv_att = din("cv_att", [DEPTH, PAST, 128])
        self.ck_win = din("ck_win", [DEPTH, PAST, 128])
        self.cv_win = din("cv_win", [DEPTH, PAST, 128])
        self.state = din("state", [DEPTH, 2, 4, 128, 128])
        self.w_mod = din("w_mod", [DEPTH, D, 9 * D])
        self.b_mod = din("b_mod", [DEPTH, 9 * D])
        self.norm_g = din("norm_g", [DEPTH, 3, D])
        self.w_ffn_in = din("w_ffn_in", [DEPTH, 2, D, 2 * DFF])
        self.w_ffn_out = din("w_ffn_out", [DEPTH, 2, DFF, D])
        self.w_in = din("w_in", [DEPTH, D, INW])
        self.qk_g = din("qk_norm_g", [DEPTH, 2, 64])
        self.lbounds = din("lower_bounds", [DEPTH, 2, 512])
        self.hg_g = din("hg_norm_g", [DEPTH, 128])
        self.sink = din("sink_logit", [DEPTH, 8])
        self.w_branch = din("w_branch", [DEPTH, 3, 512, D])
        self.w_out = din("w_out", [DEPTH, D, D])
        self.final_g = din("final_norm_g", [D])
        self.cst = din("cst", [12, 128, 128])
        self.rope = din("rope", [2, 128, DSEQ])

        self.y = dout("y", [3072, D])
        self.o_k_att = dout("o_k_att", [NPROMPT, DEPTH, SEQ, 128])
        self.o_v_att = dout("o_v_att", [NPROMPT, DEPTH, SEQ, 128])
        self.o_k_win = dout("o_k_win", [NPROMPT, DEPTH, SEQ, 128])
        self.o_v_win = dout("o_v_win", [NPROMPT, DEPTH, SEQ, 128])
        self.o_state = dout("o_state", [NPROMPT, DEPTH, 2, 4, 128, 128])

    def mm(self, ps, ps_ap, lt, lt_ap, rt, rt_ap, start, stop, **kw):
        return self.P.op("pe", lambda e: e.matmul(ps_ap, lt_ap, rt_ap, start=start, stop=stop, **kw),
                  reads=[lt, rt], writes=[ps])

    def tr(self, ps, ps_ap, src, src_ap, ident, ident_ap):
        self.P.op("pe", lambda e: e.transpose(ps_ap, src_ap, ident_ap), reads=[src, ident], writes=[ps])

    def act(self, out_t, out_ap, in_t, in_ap, func, reads=(), **kw):
        self.P.op("act", lambda e: e.activation(out_ap, in_ap, func, **kw),
                  reads=[in_t] + list(reads), writes=[out_t])

    def tt(self, eng, out_t, out_ap, a_t, a_ap, b_t, b_ap, op):
        self.P.op(eng, lambda e: e.tensor_tensor(out_ap, a_ap, b_ap, op), reads=[a_t, b_t], writes=[out_t])

    def ts(self, eng, out_t, out_ap, a_t, a_ap, s1, s2, op0, op1=ALU.bypass, reads=()):
        self.P.op(eng, lambda e: e.tensor_scalar(out_ap, a_ap, s1, s2, op0, op1),
                  reads=[a_t] + list(reads), writes=[out_t])

    def stt(self, eng, out_t, out_ap, a_t, a_ap, scalar, b_t, b_ap, op0, op1, reads=()):
        self.P.op(eng, lambda e: e.scalar_tensor_tensor(out_ap, a_ap, scalar, b_ap, op0, op1),
                  reads=[a_t, b_t] + list(reads), writes=[out_t])

    def copy(self, eng, out_t, out_ap, in_t, in_ap):
        if eng == "act":
            self.P.op("act", lambda e: e.copy(out_ap, in_ap), reads=[in_t], writes=[out_t])
        else:
            self.P.op(eng, lambda e: e.tensor_copy(out_ap, in_ap), reads=[in_t], writes=[out_t])

    def recip(self, out_t, out_ap, in_t, in_ap):
        self.P.op("dve", lambda e: e.reciprocal(out_ap, in_ap), reads=[in_t], writes=[out_t])

    def memset(self, eng, t, ap, val):
        self.P.op(eng, lambda e: e.memset(ap, val), writes=[t])

    def dma(self, q, out_ap, in_ap, reads=(), writes=(), **kw):
        self.P.dma(q, lambda e: e.dma_start(out=out_ap, in_=in_ap, **kw), reads=reads, writes=writes)

    def getps(self):
        i = self.ps_next
        self.ps_next = (i + 1) % len(self.ps)
        return self.ps[i]

    def getpsl(self):
        i = self.psl_next
        self.psl_next = (i + 1) % 2
        return self.psl[i]

    def getw(self):
        i = self.w_next
        self.w_next = (i + 1) % len(self.wring)
        return self.wring[i]

    def wload(self, src_ap, shape):
        t = self.getw()
        a, b = shape
        v = t.ap[:, 0:a * b].rearrange("p (a b) -> p a b", b=b)
        self.dma("pool", v, src_ap, writes=[t])
        return t, v

    def build(self):
        P = self.P
        nc = self.nc
        allps = [P.psum(f"ps{i}", [128, 512], F32) for i in range(8)]
        self.ps = allps[0:6]
        self.psl = allps[6:8]
        self.psl_next = 0
        self.ps_next = 0
        self.wring = [P.sbuf(f"wr{i}", [128, 4096], BF16) for i in range(4)]
        self.w_next = 0
        ident = P.sbuf("ident", [128, 128], F32)
        self.ident = ident
        self.dma("sp", ident.ap[:], self.cst[0], writes=[ident])
        cbf = P.sbuf("cbf", [128, 11, 128], BF16)
        self.cbf = cbf
        self.dma("pool", cbf.ap[:], self.cst[1:12].rearrange("c p f -> p c f"), writes=[cbf])
        ones = P.sbuf("ones", [128, 128], BF16)
        self.ones = ones
        self.memset("dve", ones, ones.ap[:], 1.0)
        cc = P.sbuf("ccols", [128, 2], F32)
        self.cc = cc
        self.memset("dve", cc, cc.ap[:, 0:1], EPS)
        self.memset("dve", cc, cc.ap[:, 1:2], 1.0)
        negh = P.sbuf("negh", [128, 1], F32)
        self.negh = negh
        self.memset("pool", negh, negh.ap[:], -0.5)

        self.prologue()
        for g in [int(c) for c in os.environ.get("MK_GROUPS", "01")]:
            self.run_group(g)
        P.wait_all_dmas("sp")
        P.emit()
        return nc

    def prologue(self):
        P = self.P
        NV = 5 * 128
        vecs = P.sbuf("vecs", [128, NV], F32)
        self.vecs = vecs
        P.push_scope()
        stg = [P.sbuf(f"stg{i}", [128, 128], F32) for i in range(5)]
        for s in stg:
            self.memset("dve", s, s.ap[:], 0.0)
        self.c_cond = 0
        self.c_ng = 16
        self.c_fg = 112
        self.c_hg = 120
        self.c_lb = 128
        self.c_qk = 160
        self.c_bm = [168, 256, 384, 512]
        q = "sp"
        self.dma(q, stg[0].ap[0:16, :], self.cond.rearrange("m (kc p) -> (m kc) p", p=128), writes=[stg[0]])
        LW = self.LW
        self.dma(q, stg[0].ap[16:16 + LW * 24, :], self.norm_g.rearrange("l j (kc p) -> (l j kc) p", p=128), writes=[stg[0]])
        self.dma(q, stg[0].ap[112:120, :], self.final_g.rearrange("(kc p) -> kc p", p=128), writes=[stg[0]])
        self.dma(q, stg[0].ap[120:120 + LW, :], self.hg_g, writes=[stg[0]])
        self.dma(q, stg[1].ap[0:LW * 8, :], self.lbounds.rearrange("l r (h p) -> (l r h) p", p=128), writes=[stg[1]])
        qkv = self.qk_g.rearrange("l w d -> (l w) d")
        self.dma(q, stg[1].ap[32:32 + 2 * LW, 0:64], qkv, writes=[stg[1]])
        self.dma(q, stg[1].ap[32:32 + 2 * LW, 64:128], qkv, writes=[stg[1]])
        self.dma(q, stg[1].ap[40:112, :], self.b_mod[0].rearrange("(j p) -> j p", p=128), writes=[stg[1]])
        for l in range(1, LW):
            self.dma(q, stg[l + 1].ap[0:72, :], self.b_mod[l].rearrange("(j p) -> j p", p=128), writes=[stg[l + 1]])
        for i in range(5):
            ps = self.getps()
            self.tr(ps, ps.ap[:, 0:128], stg[i], stg[i].ap[:], self.ident, self.ident.ap[:])
            self.copy("dve", vecs, vecs.ap[:, i * 128:(i + 1) * 128], ps, ps.ap[:, 0:128])
        P.pop_scope()
        V = vecs.ap
        P.push_scope()
        tmp = P.sbuf("ptmp", [128, 64], F32)
        self.act(tmp, tmp.ap[:, 0:16], vecs, V[:, 0:16], AF.Exp, scale=-1.0)
        self.ts("dve", tmp, tmp.ap[:, 0:16], tmp, tmp.ap[:, 0:16], 1.0, None, ALU.add)
        self.recip(tmp, tmp.ap[:, 0:16], tmp, tmp.ap[:, 0:16])
        self.tt("dve", vecs, V[:, 0:16], vecs, V[:, 0:16], tmp, tmp.ap[:, 0:16], ALU.mult)
        lbv = V[:, self.c_lb:self.c_lb + 32].rearrange("p (l c) -> p l c", c=8)
        e = tmp.ap[:, 16:48].rearrange("p (l c) -> p l c", c=8)
        self.act(tmp, e, vecs, lbv, AF.Exp)
        s = tmp.ap[:, 48:56]
        self.tt("dve", tmp, s, tmp, e[:, 0, :], tmp, e[:, 1, :], ALU.add)
        self.tt("dve", tmp, s, tmp, s, tmp, e[:, 2, :], ALU.add)
        self.tt("dve", tmp, s, tmp, s, tmp, e[:, 3, :], ALU.add)
        self.recip(tmp, s, tmp, s)
        for l in range(4):
            self.tt("dve", tmp, e[:, l, :], tmp, e[:, l, :], tmp, s, ALU.mult)
        self.memset("dve", vecs, lbv[:, 0, :], 0.0)
        self.copy("dve", vecs, lbv[:, 1, :], tmp, e[:, 1, :])
        self.tt("dve", vecs, lbv[:, 2, :], vecs, lbv[:, 1, :], tmp, e[:, 2, :], ALU.add)
        self.tt("dve", vecs, lbv[:, 3, :], vecs, lbv[:, 2, :], tmp, e[:, 3, :], ALU.add)
        P.pop_scope()
        oml = P.sbuf("oml", [128, 32], F32)
        self.oml = oml
        self.ts("dve", oml, oml.ap[:], vecs, V[:, self.c_lb:self.c_lb + 32], -1.0, 1.0, ALU.mult, ALU.add)
        esink = P.sbuf("esink", [128, 32], F32)
        self.esink = esink
        self.memset("dve", esink, esink.ap[:], 0.0)
        self.dma("sp", esink.ap[:, 0:8 * LW], self.sink.rearrange("l h -> (l h)").partition_broadcast(128), writes=[esink])
        self.act(esink, esink.ap[:], esink, esink.ap[:], AF.Exp)

        modv = P.sbuf("modv", [128, DEPTH, 72, 2], F32)
        self.modv = modv
        P.push_scope()
        wm = [P.sbuf(f"wm{i}", [128, 8, 1152], F32) for i in range(2)]
        nl = self.nlayers
        for l in range(nl):
            psm = self.getps()
            pv = psm.ap[:, 0:144].rearrange("p (j m) -> p j m", m=2)
            for pc in range(8):
                wt = wm[(l * 8 + pc) % 2]
                src = self.w_mod[l].rearrange("(kc p) f -> p kc f", p=128)[:, :, pc * 1152:(pc + 1) * 1152]
                self.dma("sp", wt.ap[:], src, writes=[wt])
                for fj in range(9):
                    j72 = pc * 9 + fj
                    for kc in range(8):
                        self.mm(psm, pv[:, j72, :], wt, wt.ap[:, kc, fj * 128:(fj + 1) * 128],
                                vecs, V[:, kc:kc + 9:8], start=(kc == 0), stop=(kc == 7))
            bm = V[:, self.c_bm[l]:self.c_bm[l] + 72]
            mv = modv.ap[:, l, :, :]
            self.tt("dve", modv, mv, psm, pv, vecs, bm.unsqueeze(2).to_broadcast([128, 72, 2]), ALU.add)
            for j in range(3):
                sc = modv.ap[:, l, (3 * j + 1) * 8:(3 * j + 2) * 8, :]
                g = V[:, self.c_ng + l * 24 + j * 8: self.c_ng + l * 24 + j * 8 + 8]
                self.stt("dve", modv, sc, modv, sc, 1.0, vecs, g.unsqueeze(2).to_broadcast([128, 8, 2]),
                         ALU.add, ALU.mult)
            for j in (0, 2):
                gt = modv.ap[:, l, (3 * j + 2) * 8:(3 * j + 3) * 8, :]
                self.ts("dve", modv, gt, modv, gt, 0.5, None, ALU.mult)
        P.pop_scope()

    def mod_ap(self, l, i, kc, m):
        return self.modv.ap[:, l, i * 8 + kc, m:m + 1]

    def run_group(self, g):
        P = self.P
        ntok = 1024 if g == 0 else 2048
        row0 = 0 if g == 0 else 1024
        nblk = ntok // 512
        self.g = g
        self.ntok = ntok
        P.push_scope()
        xT = P.sbuf(f"xT{g}", [128, 8, ntok], F32)
        xt = [[P.view(xT.ap[:, kc, b * 512:(b + 1) * 512], f"x{kc}_{b}") for b in range(nblk)] for kc in range(8)]
        self.xt = xt
        P.push_scope()
        xs = [P.sbuf(f"xs{i}", [128, 4, D], F32) for i in range(2)]
        for b in range(nblk):
            st = xs[b % 2]
            src = self.xin[row0 + b * 512: row0 + (b + 1) * 512, :].rearrange("(tt p) f -> p tt f", p=128)
            self.dma("sp", st.ap[:], src, writes=[st])
            for kc in range(8):
                ps = self.getps()
                for tt_ in range(4):
                    self.tr(ps, ps.ap[:, tt_ * 128:(tt_ + 1) * 128], st, st.ap[:, tt_, kc * 128:(kc + 1) * 128],
                            self.ident, self.ident.ap[:])
                self.copy("act" if kc % 2 else "dve", xt[kc][b], xt[kc][b].ap, ps, ps.ap[:])
        P.pop_scope()

        for l in range(self.nlayers):
            if self.stage >= 1:
                self.ffn(l, 0)
            if self.stage >= 2:
                self.mixer(l)
            if self.stage >= 3:
                self.ffn(l, 1)

        P.push_scope()
        yo = [P.sbuf(f"yo{i}", [128, 4, D], F32) for i in range(2)]
        sq = [P.sbuf(f"fsq{i}", [128, 512], BF16) for i in range(2)]
        vv = P.sbuf("fv", [128, 512], F32)
        rstd = P.sbuf("frstd", [128, 512], F32)
        tn = [P.sbuf(f"ftn{i}", [128, 512], F32) for i in range(2)]
        for b in range(nblk):
            if self.stage >= 4:
                self.rstd_of(b, sq, vv, rstd)
            ot = yo[b % 2]
            for kc in range(8):
                t = tn[kc % 2]
                if self.stage >= 4:
                    gcol = self.vecs.ap[:, self.c_fg + kc:self.c_fg + kc + 1]
                    self.stt("dve", t, t.ap[:], xt[kc][b], xt[kc][b].ap, gcol, rstd, rstd.ap[:],
                             ALU.mult, ALU.mult, reads=[self.vecs])
                    src_t, src_ap = t, t.ap
                else:
                    src_t, src_ap = xt[kc][b], xt[kc][b].ap
                ps = self.getps()
                for tt_ in range(4):
                    self.tr(ps, ps.ap[:, tt_ * 128:(tt_ + 1) * 128], src_t, src_ap[:, tt_ * 128:(tt_ + 1) * 128],
                            self.ident, self.ident.ap[:])
                self.copy("act", ot, ot.ap[:, :, kc * 128:(kc + 1) * 128],
                          ps, ps.ap[:].rearrange("p (t f) -> p t f", f=128))
            dst = self.y[row0 + b * 512: row0 + (b + 1) * 512, :].rearrange("(tt p) f -> p tt f", p=128)
            self.dma("sp", dst, ot.ap[:], reads=[ot])
        P.pop_scope()
        P.pop_scope()

    def rstd_of(self, b, sq, vv, rstd, cols=None):
        xt = self.xt
        c0, c1 = cols if cols else (0, 512)
        n = c1 - c0
        ps = self.getps()
        for kc in range(8):
            s = sq[kc % 2]
            self.act(s, s.ap[:, 0:n], xt[kc][b], xt[kc][b].ap[:, c0:c1], AF.Square)
            self.mm(ps, ps.ap[:, 0:n], self.ones, self.ones.ap[:], s, s.ap[:, 0:n], start=(kc == 0), stop=(kc == 7))
        self.act(vv, vv.ap[:, 0:n], ps, ps.ap[:, 0:n], AF.Ln, reads=[self.cc], scale=1.0 / D, bias=self.cc.ap[:, 0:1])
        self.act(rstd, rstd.ap[:, 0:n], vv, vv.ap[:, 0:n], AF.Exp, scale=-0.5)

    def adaln(self, l, j, b, hts, sq, vv, rstd, tn, cols=None):
        m = 0 if self.g == 0 else 1
        c0, c1 = cols if cols else (0, 512)
        n = c1 - c0
        self.rstd_of(b, sq, vv, rstd, cols)
        for kc in range(8):
            t = tn[kc % 2]
            x = self.xt[kc][b]
            self.tt("dve", t, t.ap[:, 0:n], x, x.ap[:, c0:c1], rstd, rstd.ap[:, 0:n], ALU.mult)
            ht, hap = hts[kc]
            self.act(ht, hap, t, t.ap[:, 0:n], AF.Identity, reads=[self.modv],
                     scale=self.mod_ap(l, 3 * j + 1, kc, m), bias=self.mod_ap(l, 3 * j, kc, m))

    def ffn(self, l, i):
        P = self.P
        j = 0 if i == 0 else 2
        m = 0 if self.g == 0 else 1
        nst = self.ntok // 1024
        P.push_scope()
        hT = P.sbuf("f_hT", [128, 8, 1024], BF16)
        ht = [[P.view(hT.ap[:, kc, h * 512:(h + 1) * 512]) for h in range(2)] for kc in range(8)]
        aT = P.sbuf("f_aT", [128, 11, 1024], BF16)
        at = [[P.view(aT.ap[:, f, h * 512:(h + 1) * 512]) for h in range(2)] for f in range(11)]
        sq = [P.sbuf(f"f_sq{k}", [128, 512], BF16) for k in range(2)]
        vv = P.sbuf("f_v", [128, 512], F32)
        rstd = P.sbuf("f_rstd", [128, 512], F32)
        tn = [P.sbuf(f"f_tn{k}", [128, 512], F32) for k in range(2)]
        sl = [P.sbuf(f"f_sl{k}", [128, 512], F32) for k in range(2)]
        wgu = self.w_ffn_in[l, i].rearrange("(kc p) f -> p kc f", p=128)
        wdn = self.w_ffn_out[l, i].rearrange("(fc p) d -> p fc d", p=128)
        for st in range(nst):
            for h in range(2):
                b = st * 2 + h
                self.adaln(l, j, b, [(ht[kc][h], ht[kc][h].ap) for kc in range(8)], sq, vv, rstd, tn)
            for fh in range(2):
                f0 = fh * 11
                for fp in range(0, 11, 2):
                    nf = min(2, 11 - fp)
                    wt = self.getw()
                    wv = wt.ap[:, 0:8 * 2 * 256].rearrange("p (k u c) -> p k u c", u=2, c=256)
                    c = (f0 + fp) * 128
                    self.dma("pool", wv[:, :, 0, 0:nf * 128], wgu[:, :, c:c + nf * 128], writes=[wt])
                    self.dma("pool", wv[:, :, 1, 0:nf * 128], wgu[:, :, DFF + c:DFF + c + nf * 128], writes=[wt])
                    for ff in range(nf):
                        for h in range(2):
                            pg, pu = self.getps(), self.getps()
                            for u, ps in ((0, pg), (1, pu)):
                                for kc in range(8):
                                    self.mm(ps, ps.ap[:], wt, wv[:, kc, u, ff * 128:(ff + 1) * 128],
                                            ht[kc][h], ht[kc][h].ap, start=(kc == 0), stop=(kc == 7))
                            s = sl[(ff + h) % 2]
                            self.act(s, s.ap[:], pg, pg.ap[:], AF.Silu)
                            a = at[fp + ff][h]
                            self.tt("dve", a, a.ap, s, s.ap[:], pu, pu.ap[:], ALU.mult)
                for dp in range(0, 8, 2):
                    wt = self.getw()
                    wv = wt.ap[:, 0:11 * 256].rearrange("p (f c) -> p f c", c=256)
                    self.dma("pool", wv, wdn[:, f0:f0 + 11, dp * 128:(dp + 2) * 128], writes=[wt])
                    for dd in range(2):
                        dc = dp + dd
                        for h in range(2):
                            b = st * 2 + h
                            ps = self.getps()
                            for f in range(11):
                                self.mm(ps, ps.ap[:], wt, wv[:, f, dd * 128:(dd + 1) * 128],
                                        at[f][h], at[f][h].ap, start=(f == 0), stop=(f == 10))
                            x = self.xt[dc][b]
                            self.stt("dve", x, x.ap, ps, ps.ap[:], self.mod_ap(l, 3 * j + 2, dc, m), x, x.ap,
                                     ALU.mult, ALU.add, reads=[self.modv])
        P.pop_scope()

    def mixer(self, l):
        P = self.P
        S = (self.g == 1)
        TL = 512 if S else 256
        self.TL = TL
        nsub = TL // 128
        self.nsub = nsub
        NK = (DSEQ + PAST) if S else SEQ
        nkb = NK // 128
        ctxk = PAST if S else 0
        P.push_scope()
        M = {}
        self.M = M
        M["kA"] = P.sbuf("m_kA", [128, NK], BF16)
        M["kW"] = P.sbuf("m_kW", [128, NK], BF16)
        M["vA"] = P.sbuf("m_vA", [128, nkb, 2, 65], BF16)
        M["vW"] = P.sbuf("m_vW", [128, nkb, 2, 65], BF16)
        for n in ("vA", "vW"):
            self.memset("pool", M[n], M[n].ap[:, :, :, 64:65], 1.0)
        M["Sf"] = [P.sbuf(f"m_Sf{h}", [128, 128], F32) for h in range(4)]
        M["Sb"] = [P.sbuf(f"m_Sb{h}", [128, 128], F32) for h in range(4)]
        M["Sbf"] = [P.sbuf(f"m_Sbf{i}", [128, 128], BF16) for i in range(3)]
        ntile = (DSEQ // TL) if S else 1
        M["bound"] = [[P.sbuf(f"m_bd{t}_{h}", [128, 128], F32) for h in range(4)] for t in range(ntile - 1)]
        M["hT"] = P.sbuf("m_hT", [128, 8, TL], BF16)
        M["hts"] = [P.view(M["hT"].ap[:, kc, :]) for kc in range(8)]
        M["sq"] = [P.sbuf(f"m_sq{k}", [128, TL], BF16) for k in range(2)]
        M["f"] = [P.sbuf(f"m_f{k}", [128, TL], F32) for k in range(8)]
        M["qo"] = P.sbuf("m_qo", [128, 8, TL], BF16)
        M["qT"] = Tile(M["qo"].ap[:, 0:4, :], "qT")
        M["otok"] = Tile(M["qo"].ap[:, 4:8, :].rearrange("p a t -> p (a t)").rearrange("p (s f) -> p s f", f=512), "otok")
        M["mT"] = Tile(M["qo"].ap[:], "mT")
        for n_ in ("qT", "otok", "mT"):
            M[n_].w = M["qo"].w
            M[n_].r = M["qo"].r
        M["pT"] = [P.sbuf(f"m_pT{k}", [128, TL], BF16) for k in range(2)]
        M["oT"] = P.sbuf("m_oT", [128, 4, TL], BF16)
        M["mg"] = P.sbuf("m_mg", [128, 8, TL], F32)
        M["mgs"] = [P.view(M["mg"].ap[:, fc, :]) for fc in range(8)]
        M["b16"] = [P.sbuf(f"m_b16_{k}", [128, TL], BF16) for k in range(5)]
        M["khtok"] = P.sbuf("m_khtok", [128, nsub, 128], BF16)
        M["Vt"] = P.sbuf("m_Vt", [128, nsub, 512], BF16)
        M["AT"] = [P.sbuf(f"m_AT{k}", [128, 128], BF16) for k in range(4)]
        M["dec"] = P.sbuf("m_dec", [128, 2, 16], F32)
        M["sm"] = P.sbuf("m_sm", [128, 16], F32)
        M["cm"] = P.sbuf("m_cm", [128, TL], BF16)
        self.memset("pool", M["cm"], M["cm"].ap[:], 1.0)
        self.memset("pool", M["cm"], M["cm"].ap[:, 0:TL:32], 0.0)
        M["cm16"] = P.sbuf("m_cm16", [128, TL], BF16)
        self.memset("pool", M["cm16"], M["cm16"].ap[:], 1.0)
        self.memset("pool", M["cm16"], M["cm16"].ap[:, 0:TL:16], 0.0)
        if S:
            M["rc"] = P.sbuf("m_rc", [128, TL], F32)
            M["rs"] = P.sbuf("m_rs", [128, TL], F32)
        else:
            M["ko"] = P.sbuf("m_ko", [128, nsub, 2, 128], F32)
            M["vo"] = P.sbuf("m_vo", [128, nsub, 256], F32)

        if S:
            for ck, cv, kn, vn in ((self.ck_att, self.cv_att, "kA", "vA"), (self.ck_win, self.cv_win, "kW", "vW")):
                st = M["f"][0]
                stv = st.ap[:, 0:256].rearrange("p (b f) -> p b f", f=128)
                self.dma("sp", stv, ck[l].rearrange("(b p) f -> p b f", p=128), writes=[st])
                ps = self.getps()
                for b_ in range(2):
                    self.tr(ps, ps.ap[:, b_ * 128:(b_ + 1) * 128], st, stv[:, b_, :], self.ident, self.ident.ap[:])
                self.copy("act", M[kn], M[kn].ap[:, 0:256], ps, ps.ap[:, 0:256])
                for b_ in range(2):
                    self.dma("pool", M[vn].ap[:, b_, :, 0:64],
                             cv[l][b_ * 128:(b_ + 1) * 128, :].rearrange("p (g d) -> p g d", d=64), writes=[M[vn]])
            for h in range(4):
                self.dma("sp", M["Sf"][h].ap[:], self.state[l, 0, h], writes=[M["Sf"][h]])
                self.dma("sp", M["Sb"][h].ap[:], self.state[l, 1, h], writes=[M["Sb"][h]])
            tiles = [(t, t, 0, 512, t * 512) for t in range(4)]
            if SUB2 < 2:
                tiles = []
            for tl in reversed(tiles):
                self.pass1(l, tl, None)
            for tl in tiles:
                self.pass2(l, tl, None)
        else:
            for sq_ in range(NPROMPT):
                for h in range(4):
                    self.memset("pool", M["Sf"][h], M["Sf"][h].ap[:], 0.0)
                    self.memset("pool", M["Sb"][h], M["Sb"][h].ap[:], 0.0)
                tl = (0, sq_ // 2, (sq_ % 2) * 256, (sq_ % 2) * 256 + 256, 0)
                self.pass1(l, tl, sq_)
                self.pass2(l, tl, sq_)
                for h in range(4):
                    self.dma("sp", self.o_state[sq_, l, 0, h], M["Sf"][h].ap[:], reads=[M["Sf"][h]])
                    self.dma("sp", self.o_state[sq_, l, 1, h], M["Sb"][h].ap[:], reads=[M["Sb"][h]])
        P.pop_scope()

    def m_adaln(self, l, tl):
        M = self.M
        _, b, c0, c1, _ = tl
        self.adaln(l, 1, b, [(M["hts"][kc], M["hts"][kc].ap) for kc in range(8)], M["sq"],
                   M["f"][0], M["f"][1], M["f"][2:4], cols=(c0, c1))

    def proj_fm(self, wt, wap_fn, n=8):
        M = self.M
        ps = self.getps()
        for kc in range(n):
            self.mm(ps, ps.ap[:, 0:self.TL], wt, wap_fn(kc), M["hts"][kc], M["hts"][kc].ap,
                    start=(kc == 0), stop=(kc == n - 1))
        return ps

    def headnorm(self, ps, gcol, out_t, nrm_mat_ap, dim):
        M = self.M
        TL = self.TL
        sq = M["sq"][0]
        self.act(sq, sq.ap[:], ps, ps.ap[:, 0:TL], AF.Square)
        p2 = self.getps()
        self.mm(p2, p2.ap[:, 0:TL], self.cbf, nrm_mat_ap, sq, sq.ap[:], start=True, stop=True)
        vv, rstd = M["f"][4], M["f"][5]
        self.act(vv, vv.ap[:], p2, p2.ap[:, 0:TL], AF.Ln, reads=[self.cc], scale=1.0 / dim, bias=self.cc.ap[:, 0:1])
        self.act(rstd, rstd.ap[:], vv, vv.ap[:], AF.Exp, scale=-0.5)
        self.stt("dve", out_t, out_t.ap[:], ps, ps.ap[:, 0:TL], gcol, rstd, rstd.ap[:], ALU.mult, ALU.mult,
                 reads=[self.vecs])

    def rope_to(self, kn, dst_t, dst_ap):
        M = self.M
        TL = self.TL
        kb = M["sq"][1]
        if SUB2 < 4:
            self.copy("act", dst_t, dst_ap, kn, kn.ap[:])
            return
        self.copy("act", kb, kb.ap[:], kn, kn.ap[:])
        ps = self.getps()
        self.mm(ps, ps.ap[:, 0:TL], self.cbf, self.cbf.ap[:, 2, :], kb, kb.ap[:], start=True, stop=True)
        t1, t2 = M["f"][6], M["f"][7]
        if SUB2 < 5:
            self.copy("act", dst_t, dst_ap, ps, ps.ap[:, 0:TL])
            return
        self.tt("pool", t1, t1.ap[:], kn, kn.ap[:], M["rc"], M["rc"].ap[:], ALU.mult)
        if SUB2 < 6:
            self.copy("act", dst_t, dst_ap, t1, t1.ap[:])
            return
        self.tt("dve", t2, t2.ap[:], ps, ps.ap[:, 0:TL], M["rs"], M["rs"].ap[:], ALU.mult)
        if SUB2 < 7:
            self.copy("act", dst_t, dst_ap, t2, t2.ap[:])
            return
        self.tt("dve", dst_t, dst_ap, t1, t1.ap[:], t2, t2.ap[:], ALU.add)

    def sigmoid_parts(self, ps, r_t):
        TL = self.TL
        self.act(r_t, r_t.ap[:], ps, ps.ap[:, 0:TL], AF.Exp, scale=-1.0)
        self.act(r_t, r_t.ap[:], r_t, r_t.ap[:], AF.Ln, reads=[self.cc], bias=self.cc.ap[:, 1:2])
        self.act(r_t, r_t.ap[:], r_t, r_t.ap[:], AF.Exp, scale=-1.0)

    def gates_f(self, l, h, r, ps, f_t, k_t, lg_t, b_t):
        M = self.M
        col = l * 8 + r * 4 + h
        lb = self.vecs.ap[:, self.c_lb + col:self.c_lb + col + 1]
        om = self.oml.ap[:, col:col + 1]
        self.sigmoid_parts(ps, f_t)
        self.ts("dve", f_t, f_t.ap[:], f_t, f_t.ap[:], om, lb, ALU.mult, ALU.add, reads=[self.vecs, self.oml])
        self.ts("pool", k_t, k_t.ap[:], f_t, f_t.ap[:], -1.0, 1.0, ALU.mult, ALU.add)
        self.act(lg_t, lg_t.ap[:], f_t, f_t.ap[:], AF.Ln)
        cm = M["cm"]
        self.P.op("dve", lambda e: e.tensor_tensor_scan(b_t.ap[:], cm.ap[:], lg_t.ap[:], 0.0, ALU.mult, ALU.add),
                  reads=[cm, lg_t], writes=[b_t])

    def vtok(self, wt, wv):
        M = self.M
        for sub in range(self.nsub):
            ps = self.getps()
            for kc in range(8):
                self.mm(ps, ps.ap[:], M["hts"][kc], M["hts"][kc].ap[:, sub * 128:(sub + 1) * 128],
                        wt, wv[:, kc, :], start=(kc == 0), stop=(kc == 7))
            self.copy("act", M["Vt"], M["Vt"].ap[:, sub, :], ps, ps.ap[:])

    def kh_transpose(self, kh):
        M = self.M
        ps = self.getps()
        pb = ps.ap[:].bitcast(BF16)
        for sub in range(self.nsub):
            self.tr(ps, pb[:, sub * 128:(sub + 1) * 128], kh, kh.ap[:, sub * 128:(sub + 1) * 128],
                    self.cbf, self.cbf.ap[:, 0, :])
        n = self.nsub * 128
        self.copy("act", M["khtok"], M["khtok"].ap[:].rearrange("p s d -> p (s d)"), ps, pb[:, 0:n])

    def u_mats(self, h, sub):
        M = self.M
        ps = self.getps()
        pv = ps.ap[:].rearrange("p (j v) -> p j v", v=128)
        for j in range(4):
            tok = self.mm(ps, pv[:, j, :], M["khtok"], M["khtok"].ap[32 * j:32 * j + 32, sub, :],
                          M["Vt"], M["Vt"].ap[32 * j:32 * j + 32, sub, h * 128:(h + 1) * 128],
                          start=True, stop=True, tile_position=(32 * j, 0))
            self.P._wait("pe", tok[0], tok[1])
        return ps, pv

    def pass1(self, l, tl, seq):
        M = self.M
        P = self.P
        S = (self.g == 1)
        TL, nsub = self.TL, self.nsub
        idx, b, c0, c1, soff = tl
        koff = (PAST if S else 0) + soff
        wi = self.w_in[l].rearrange("(kc p) f -> p kc f", p=128)
        self.m_adaln(l, tl)
        if S:
            self.dma("sp", M["rc"].ap[:], self.rope[0][:, soff:soff + TL], writes=[M["rc"]])
            self.dma("sp", M["rs"].ap[:], self.rope[1][:, soff:soff + TL], writes=[M["rs"]])
        if SUB2 < 3:
            return
        wt = self.getw()
        wv = wt.ap[:, 0:8 * 256].rearrange("p (k c) -> p k c", c=256)
        self.dma("pool", wv[:, :, 0:128], wi[:, :, O_AK:O_AK + 128], writes=[wt])
        self.dma("pool", wv[:, :, 128:256], wi[:, :, O_WK:O_WK + 128], writes=[wt])
        for which, kn_name in ((0, "kA"), (1, "kW")):
            ps = self.proj_fm(wt, lambda kc, w=which: wv[:, kc, w * 128:(w + 1) * 128])
            kn = M["f"][2 + which]
            if which == 0:
                gk = self.vecs.ap[:, self.c_qk + l * 2 + 1:self.c_qk + l * 2 + 2]
                self.headnorm(ps, gk, kn, self.cbf.ap[:, 1, :], 64)
            else:
                self.copy("act", kn, kn.ap[:], ps, ps.ap[:, 0:TL])
            dst = M[kn_name]
            if S:
                self.rope_to(kn, dst, dst.ap[:, koff:koff + TL])
            else:
                self.copy("act", dst, dst.ap[:, koff:koff + TL], kn, kn.ap[:])
                for sub in range(nsub):
                    p2 = self.getps()
                    self.tr(p2, p2.ap[:, 0:128], kn, kn.ap[:, sub * 128:(sub + 1) * 128], self.ident, self.ident.ap[:])
                    self.copy("act", M["ko"], M["ko"].ap[:, sub, which, :], p2, p2.ap[:, 0:128])
        if not S:
            self.dma("sp", self.o_k_att[seq, l].rearrange("(s p) f -> p s f", p=128), M["ko"].ap[:, :, 0, :], reads=[M["ko"]])
            self.dma("sp", self.o_k_win[seq, l].rearrange("(s p) f -> p s f", p=128), M["ko"].ap[:, :, 1, :], reads=[M["ko"]])
        if SUB < 2:
            return
        wt = self.getw()
        wv2 = wt.ap[:, 0:8 * 256].rearrange("p (k c) -> p k c", c=256)
        self.dma("pool", wv2[:, :, 0:128], wi[:, :, O_AV:O_AV + 128], writes=[wt])
        self.dma("pool", wv2[:, :, 128:256], wi[:, :, O_WV:O_WV + 128], writes=[wt])
        for sub in range(nsub):
            ps = self.getps()
            for kc in range(8):
                self.mm(ps, ps.ap[:, 0:256], M["hts"][kc], M["hts"][kc].ap[:, sub * 128:(sub + 1) * 128],
                        wt, wv2[:, kc, :], start=(kc == 0), stop=(kc == 7))
            kb = koff // 128 + sub
            self.copy("act", M["vA"], M["vA"].ap[:, kb, :, 0:64], ps, ps.ap[:, 0:128].rearrange("p (g d) -> p g d", d=64))
            self.copy("dve", M["vW"], M["vW"].ap[:, kb, :, 0:64], ps, ps.ap[:, 128:256].rearrange("p (g d) -> p g d", d=64))
            if not S:
                self.copy("act", M["vo"], M["vo"].ap[:, sub, :], ps, ps.ap[:, 0:256])
        if not S:
            self.dma("sp", self.o_v_att[seq, l].rearrange("(s p) f -> p s f", p=128), M["vo"].ap[:, :, 0:128], reads=[M["vo"]])
            self.dma("sp", self.o_v_win[seq, l].rearrange("(s p) f -> p s f", p=128), M["vo"].ap[:, :, 128:256], reads=[M["vo"]])
        if SUB < 3:
            return
        wti = self.getw()
        wvi = wti.ap[:, 0:4096].rearrange("p (k c) -> p k c", c=512)
        self.dma("pool", wvi, wi[:, :, O_HI:O_HI + 512], writes=[wti])
        self.vtok(wti, wvi)
        wtb = self.getw()
        wvb = wtb.ap[:, 0:4096].rearrange("p (k c) -> p k c", c=512)
        self.dma("pool", wvb, wi[:, :, O_HB:O_HB + 512], writes=[wtb])
        nch = TL // 32
        for h in range(4):
            if idx < len(M["bound"]):
                self.copy("pool", M["bound"][idx][h], M["bound"][idx][h].ap[:], M["Sb"][h], M["Sb"][h].ap[:])
            ps = self.proj_fm(wtb, lambda kc, h=h: wvb[:, kc, h * 128:(h + 1) * 128])
            f_t, k_t, lg_t, b_t = M["f"][2], M["f"][3], M["f"][4], M["f"][5]
            self.gates_f(l, h, 1, ps, f_t, k_t, lg_t, b_t)
            self.khat_dec_b(k_t, lg_t, b_t, M["b16"][0], 1)
            self.kh_transpose(M["b16"][0])
            Sb = M["Sb"][h]
            for sub in reversed(range(nsub)):
                psu, pv = self.u_mats(h, sub)
                for j in reversed(range(4)):
                    c = sub * 4 + j
                    self.stt("dve", Sb, Sb.ap[:], Sb, Sb.ap[:], M["dec"].ap[:, 1, c:c + 1], psu, pv[:, j, :],
                             ALU.mult, ALU.add, reads=[M["dec"]])

    def khat_dec_b(self, k_t, lg_t, b_t, kh16, r):
        M = self.M
        TL = self.TL
        nch = TL // 32
        t = M["f"][6]
        self.tt("pool", t, t.ap[:], b_t, b_t.ap[:], lg_t, lg_t.ap[:], ALU.subtract)
        self.act(t, t.ap[:], t, t.ap[:], AF.Exp)
        self.tt("dve", kh16, kh16.ap[:], k_t, k_t.ap[:], t, t.ap[:], ALU.mult)
        self.act(M["dec"], M["dec"].ap[:, r, 0:nch], b_t, b_t.ap[:, 31:TL:32], AF.Exp)

    def attention(self, l, tl, which):
        M = self.M
        S = (self.g == 1)
        TL, nsub = self.TL, self.nsub
        idx, b, c0, c1, soff = tl
        wi = self.w_in[l].rearrange("(kc p) f -> p kc f", p=128)
        o_q = O_AQ if which == 0 else O_WQ
        kX, vX = (M["kA"], M["vA"]) if which == 0 else (M["kW"], M["vW"])
        wt = self.getw()
        wv = wt.ap[:, 0:4096].rearrange("p (k c) -> p k c", c=512)
        for c in range(4):
            self.dma("pool", wv[:, :, c * 128:c * 128 + 64], wi[:, :, o_q + c * 64:o_q + c * 64 + 64], writes=[wt])
            self.dma("pool", wv[:, :, c * 128 + 64:c * 128 + 128], wi[:, :, o_q + (4 + c) * 64:o_q + (4 + c) * 64 + 64], writes=[wt])
        qT = M["qT"]
        for c in range(4):
            ps = self.proj_fm(wt, lambda kc, c=c: wv[:, kc, c * 128:(c + 1) * 128])
            if which == 0:
                qn = M["f"][2]
                gq = self.vecs.ap[:, self.c_qk + l * 2:self.c_qk + l * 2 + 1]
                self.headnorm(ps, gq, qn, self.cbf.ap[:, 1, :], 64)
                if S:
                    self.rope_to(qn, qT, qT.ap[:, c, :])
                else:
                    self.copy("act", qT, qT.ap[:, c, :], qn, qn.ap[:])
            else:
                if S:
                    qn = M["f"][2]
                    self.copy("act", qn, qn.ap[:], ps, ps.ap[:, 0:TL])
                    self.rope_to(qn, qT, qT.ap[:, c, :])
                else:
                    self.copy("act", qT, qT.ap[:, c, :], ps, ps.ap[:, 0:TL])
        sched = []
        if which == 0 or not S:
            nkb = (DSEQ + PAST) // 128 if S else SEQ // 128
            sched = [(kb, 0, nsub - 1, {}) for kb in range(nkb)]
        else:
            sched = [(0, 0, nsub - 1, {}), (1, 0, nsub - 1, {})]
            qb0 = soff // 128
            for kbi in range(qb0 - 1, qb0 + nsub + 1):
                if kbi < 0 or kbi >= DSEQ // 128:
                    continue
                lo = max(kbi - 1, qb0) - qb0
                hi = min(kbi + 1, qb0 + nsub - 1) - qb0
                masks = {}
                for sub in range(lo, hi + 1):
                    qb = qb0 + sub
                    if kbi == qb - 1:
                        masks[sub] = 3
                    elif kbi == qb + 1:
                        masks[sub] = 4
                sched.append((2 + kbi, lo, hi, masks))
        otok = M["otok"]
        for c in range(4):
            for a in range(2):
                head = c + 4 * a
                acc = self.getpsl()
                av = acc.ap[:, 0:nsub * 65].rearrange("p (s e) -> p s e", e=65)
                first = {sub: True for sub in range(nsub)}
                started = False
                last_kb = {}
                for (kb, lo, hi, masks) in sched:
                    for sub in range(lo, hi + 1):
                        last_kb[sub] = kb
                def score(si):
                    kb, lo, hi, masks = sched[si]
                    n0, n1 = lo * 128, (hi + 1) * 128
                    sps = self.getps()
                    self.mm(sps, sps.ap[:, n0:n1], kX, kX.ap[64 * a:64 * a + 64, kb * 128:(kb + 1) * 128],
                            qT, qT.ap[64 * a:64 * a + 64, c, n0:n1], start=True, stop=True)
                    return sps

                SKEW = 2
                pend = [score(si) for si in range(min(SKEW, len(sched)))]
                for si, (kb, lo, hi, masks) in enumerate(sched):
                    n0, n1 = lo * 128, (hi + 1) * 128
                    sps = pend.pop(0)
                    if si + SKEW < len(sched):
                        pend.append(score(si + SKEW))
                    pT = M["pT"][si % 2]
                    self.act(pT, pT.ap[:, n0:n1], sps, sps.ap[:, n0:n1], AF.Exp, scale=0.125)
                    for sub, mi in masks.items():
                        self.tt("pool", pT, pT.ap[:, sub * 128:(sub + 1) * 128], pT, pT.ap[:, sub * 128:(sub + 1) * 128],
                                self.cbf, self.cbf.ap[:, mi, :], ALU.mult)
                    for sub in range(lo, hi + 1):
                        self.mm(acc, av[:, sub, :], pT, pT.ap[:, sub * 128:(sub + 1) * 128],
                                vX, vX.ap[:, kb, a, :], start=(not started), stop=(last_kb[sub] == kb),
                                skip_group_check=True)
                        started = True
                sm = M["sm"]
                if which == 1:
                    self.ts("dve", sm, sm.ap[:, 0:nsub], acc, av[:, :, 64], self.esink.ap[:, l * 8 + head:l * 8 + head + 1],
                            None, ALU.add, reads=[self.esink])
                    self.recip(sm, sm.ap[:, 0:nsub], sm, sm.ap[:, 0:nsub])
                else:
                    self.recip(sm, sm.ap[:, 0:nsub], acc, av[:, :, 64])
                self.tt("dve", otok, otok.ap[:, :, head * 64:(head + 1) * 64], acc, av[:, :, 0:64],
                        sm, sm.ap[:, 0:nsub].unsqueeze(2).to_broadcast([128, nsub, 64]), ALU.mult)
        for k4 in range(4):
            ps = self.getps()
            pb = ps.ap[:].bitcast(BF16)
            for sub in range(nsub):
                self.tr(ps, pb[:, sub * 128:(sub + 1) * 128], otok, otok.ap[:, sub, k4 * 128:(k4 + 1) * 128],
                        self.cbf, self.cbf.ap[:, 0, :])
            self.copy("act", M["oT"], M["oT"].ap[:, k4, :], ps, pb[:, 0:TL])

    def hgrn2(self, l, tl):
        M = self.M
        S = (self.g == 1)
        TL, nsub = self.TL, self.nsub
        idx, b, c0, c1, soff = tl
        nch = TL // 32
        wi = self.w_in[l].rearrange("(kc p) f -> p kc f", p=128)
        wti = self.getw()
        wvi = wti.ap[:, 0:4096].rearrange("p (k c) -> p k c", c=512)
        self.dma("pool", wvi, wi[:, :, O_HI:O_HI + 512], writes=[wti])
        self.vtok(wti, wvi)
        F = M["f"]
        for h in range(4):
            wt = self.getw()
            wv = wt.ap[:, 0:4096].rearrange("p (k c) -> p k c", c=512)
            for ti, off in enumerate((O_HQ, O_HF, O_HB, O_HG)):
                self.dma("pool", wv[:, :, ti * 128:(ti + 1) * 128], wi[:, :, off + h * 128:off + (h + 1) * 128], writes=[wt])
            ps = self.proj_fm(wt, lambda kc: wv[:, kc, 0:128])
            Q = F[0]
            self.sigmoid_parts(ps, Q)
            self.tt("dve", Q, Q.ap[:], Q, Q.ap[:], ps, ps.ap[:, 0:TL], ALU.mult)
            ps = self.proj_fm(wt, lambda kc: wv[:, kc, 384:512])
            OG = F[1]
            self.sigmoid_parts(ps, OG)
            self.tt("dve", OG, OG.ap[:], OG, OG.ap[:], ps, ps.ap[:, 0:TL], ALU.mult)
            osum = F[7]
            for r in range(2):
                ps = self.proj_fm(wt, lambda kc, r=r: wv[:, kc, 128 + r * 128:256 + r * 128])
                f_t, k_t, lg_t, b_t = F[2], F[3], F[4], F[5]
                self.gates_f(l, h, r, ps, f_t, k_t, lg_t, b_t)
                qI, q1, k1, k2, kh16 = M["b16"]
                X, Y = f_t, F[6]
                n16 = TL // 16
                bl = b_t.ap[:].rearrange("p (c i) -> p c i", i=32)[:, :, 31:32].to_broadcast([128, nch, 32])
                b3 = b_t.ap[:].rearrange("p (c i) -> p c i", i=32)
                v32 = lambda t_: t_.ap[:].rearrange("p (c i) -> p c i", i=32)
                v16 = lambda t_: t_.ap[:].rearrange("p (c i) -> p c i", i=16)
                xl16 = X.ap[:].rearrange("p (c i) -> p c i", i=16)[:, :, 15:16].to_broadcast([128, n16, 16])
                cm16 = M["cm16"]

                def expmul(dst16, src_t, scale, base_t):
                    tmp = Y if src_t is X else X
                    self.act(tmp, tmp.ap[:], src_t, src_t.ap[:], AF.Exp, scale=scale)
                    self.tt("dve", dst16, dst16.ap[:], base_t, base_t.ap[:], tmp, tmp.ap[:], ALU.mult)

                if r == 0:
                    self.tt("pool", Y, v32(Y), b_t, bl, b_t, b3, ALU.subtract)
                    expmul(kh16, Y, 1.0, k_t)
                    self.act(M["dec"], M["dec"].ap[:, 0, 0:nch], b_t, b_t.ap[:, 31:TL:32], AF.Exp)
                    self.act(Y, Y.ap[:], b_t, b_t.ap[:], AF.Exp)
                    self.tt("dve", qI, qI.ap[:], Q, Q.ap[:], Y, Y.ap[:], ALU.mult)
                    self.P.op("dve", lambda e: e.tensor_tensor_scan(X.ap[:], cm16.ap[:], lg_t.ap[:], 0.0, ALU.mult, ALU.add),
                              reads=[cm16, lg_t], writes=[X])
                    expmul(q1, X, 1.0, Q)
                    expmul(k1, X, -1.0, k_t)
                    self.tt("pool", Y, v16(Y), X, xl16, X, v16(X), ALU.subtract)
                    self.act(Y, Y.ap[:], Y, Y.ap[:], AF.Exp)
                    self.tt("dve", k2, k2.ap[:], k_t, k_t.ap[:], Y, Y.ap[:], ALU.mult)
                else:
                    self.khat_dec_b(k_t, lg_t, b_t, kh16, 1)
                    self.tt("pool", X, v32(X), b_t, bl, b_t, b3, ALU.subtract)
                    self.tt("pool", X, X.ap[:], X, X.ap[:], lg_t, lg_t.ap[:], ALU.add)
                    expmul(qI, X, 1.0, Q)
                    self.P.op("dve", lambda e: e.tensor_tensor_scan(X.ap[:], cm16.ap[:], lg_t.ap[:], 0.0, ALU.mult, ALU.add),
                              reads=[cm16, lg_t], writes=[X])
                    self.tt("pool", Y, Y.ap[:], X, X.ap[:], lg_t, lg_t.ap[:], ALU.subtract)
                    self.act(Y, Y.ap[:], Y, Y.ap[:], AF.Exp)
                    self.tt("dve", k2, k2.ap[:], k_t, k_t.ap[:], Y, Y.ap[:], ALU.mult)
                    self.tt("pool", Y, v16(Y), X, xl16, X, v16(X), ALU.subtract)
                    self.tt("pool", Y, Y.ap[:], Y, Y.ap[:], lg_t, lg_t.ap[:], ALU.add)
                    expmul(q1, Y, 1.0, Q)
                    expmul(k1, Y, -1.0, k_t)
                qt16 = qI
                self.kh_transpose(kh16)
                St = M["Sf"][h] if r == 0 else M["Sb"][h]
                if r == 1:
                    if idx < len(M["bound"]):
                        self.copy("pool", St, St.ap[:], M["bound"][idx][h], M["bound"][idx][h].ap[:])
                    elif S:
                        self.dma("sp", St.ap[:], self.state[l, 1, h], writes=[St])
                    else:
                        self.memset("pool", St, St.ap[:], 0.0)
                acc = self.getpsl()
                started = False
                subs = range(nsub) if r == 0 else reversed(range(nsub))
                nbf = 0
                for sub in subs:
                    n0 = sub * 128
                    for wi_, kx in enumerate((k1, k2)):
                        sps = self.getps()
                        self.mm(sps, sps.ap[:, 0:128], kx, kx.ap[:, n0:n0 + 128], q1, q1.ap[:, n0:n0 + 128],
                                start=True, stop=True)
                        AT = M["AT"][(sub % 2) * 2 + wi_]
                        self.tt("dve", AT, AT.ap[:], sps, sps.ap[:, 0:128], self.cbf, self.cbf.ap[:, 7 + 2 * r + wi_, :], ALU.mult)
                        self.mm(acc, acc.ap[:, n0:n0 + 128], M["Vt"], M["Vt"].ap[:, sub, h * 128:(h + 1) * 128],
                                AT, AT.ap[:], start=(not started), stop=False, skip_group_check=True)
                        started = True
                    psu, pv = self.u_mats(h, sub)
                    js = range(4) if r == 0 else reversed(range(4))
                    for j in js:
                        c = sub * 4 + j
                        if nbf == 0:
                            sb16 = M["Sbf"][0]
                            self.copy("act", sb16, sb16.ap[:], St, St.ap[:])
                        else:
                            sb16 = M["Sbf"][nbf % 3]
                        nbf += 1
                        self.mm(acc, acc.ap[:, c * 32:(c + 1) * 32], sb16, sb16.ap[:], qt16, qt16.ap[:, c * 32:(c + 1) * 32],
                                start=False, stop=True, skip_group_check=True)
                        nx16 = M["Sbf"][nbf % 3]
                        self.stt("dve", nx16, nx16.ap[:], St, St.ap[:], M["dec"].ap[:, r, c:c + 1], psu, pv[:, j, :],
                                 ALU.mult, ALU.add, reads=[M["dec"]])
                        self.stt("dve", St, St.ap[:], St, St.ap[:], M["dec"].ap[:, r, c:c + 1], psu, pv[:, j, :],
                                 ALU.mult, ALU.add, reads=[M["dec"]])
                if r == 0:
                    self.copy("act", osum, osum.ap[:], acc, acc.ap[:, 0:TL])
                else:
                    self.tt("dve", osum, osum.ap[:], osum, osum.ap[:], acc, acc.ap[:, 0:TL], ALU.add)
            sq = M["sq"][0]
            self.act(sq, sq.ap[:], osum, osum.ap[:], AF.Square)
            p2 = self.getps()
            self.mm(p2, p2.ap[:, 0:TL], self.ones, self.ones.ap[:], sq, sq.ap[:], start=True, stop=True)
            vv, rstd = F[4], F[5]
            self.act(vv, vv.ap[:], p2, p2.ap[:, 0:TL], AF.Ln, reads=[self.cc], scale=1.0 / 128, bias=self.cc.ap[:, 0:1])
            self.act(rstd, rstd.ap[:], vv, vv.ap[:], AF.Exp, scale=-0.5)
            gcol = self.vecs.ap[:, self.c_hg + l:self.c_hg + l + 1]
            self.stt("dve", osum, osum.ap[:], osum, osum.ap[:], gcol, rstd, rstd.ap[:], ALU.mult, ALU.mult, reads=[self.vecs])
            self.tt("dve", M["oT"], M["oT"].ap[:, h, :], osum, osum.ap[:], OG, OG.ap[:], ALU.mult)

    def merge_branch(self, l, k, first, last):
        M = self.M
        TL = self.TL
        wi = self.w_in[l].rearrange("(kc p) f -> p kc f", p=128)
        off = (O_GA, O_GB, O_GC)[k]
        wb = self.getw()
        wbv = wb.ap[:, 0:4096].rearrange("p (k c) -> p k c", c=1024)
        self.dma("pool", wbv, self.w_branch[l, k].rearrange("(kc p) f -> p kc f", p=128), writes=[wb])
        F = M["f"]
        for half in range(2):
            wg = self.getw()
            wgv = wg.ap[:, 0:4096].rearrange("p (k c) -> p k c", c=512)
            self.dma("pool", wgv, wi[:, :, off + half * 512:off + (half + 1) * 512], writes=[wg])
            for f4 in range(4):
                fc = half * 4 + f4
                gps = self.proj_fm(wg, lambda kc, f4=f4: wgv[:, kc, f4 * 128:(f4 + 1) * 128])
                bps = self.getps()
                for k4 in range(4):
                    self.mm(bps, bps.ap[:, 0:TL], wb, wbv[:, k4, fc * 128:(fc + 1) * 128], M["oT"], M["oT"].ap[:, k4, :],
                            start=(k4 == 0), stop=(k4 == 3))
                r = F[fc % 2]
                self.sigmoid_parts(gps, r)
                mg = M["mgs"][fc]
                if first:
                    self.tt("dve", mg, mg.ap, r, r.ap[:], bps, bps.ap[:, 0:TL], ALU.mult)
                else:
                    self.tt("dve", r, r.ap[:], r, r.ap[:], bps, bps.ap[:, 0:TL], ALU.mult)
                    if last:
                        self.tt("pool", M["mT"], M["mT"].ap[:, fc, :], mg, mg.ap, r, r.ap[:], ALU.add)
                    else:
                        self.tt("pool", mg, mg.ap, mg, mg.ap, r, r.ap[:], ALU.add)

    def pass2(self, l, tl, seq):
        M = self.M
        S = (self.g == 1)
        TL = self.TL
        idx, b, c0, c1, soff = tl
        m = 1 if S else 0
        self.m_adaln(l, tl)
        if S:
            self.dma("sp", M["rc"].ap[:], self.rope[0][:, soff:soff + TL], writes=[M["rc"]])
            self.dma("sp", M["rs"].ap[:], self.rope[1][:, soff:soff + TL], writes=[M["rs"]])
        if SUB < 4:
            return
        self.hgrn2(l, tl)
        if SUB < 5:
            return
        self.merge_branch(l, 1, True, False)
        if SUB < 6:
            return
        self.attention(l, tl, 0)
        self.merge_branch(l, 0, False, False)
        if SUB < 7:
            return
        self.attention(l, tl, 1)
        self.merge_branch(l, 2, False, True)
        wo = self.w_out[l].rearrange("(kc p) f -> p kc f", p=128)
        for half in range(2):
            wt = self.getw()
            wv = wt.ap[:, 0:4096].rearrange("p (k c) -> p k c", c=512)
            self.dma("pool", wv, wo[:, :, half * 512:(half + 1) * 512], writes=[wt])
            for d4 in range(4):
                dc = half * 4 + d4
                ps = self.getps()
                for fc in range(8):
                    self.mm(ps, ps.ap[:, 0:TL], wt, wv[:, fc, d4 * 128:(d4 + 1) * 128], M["mT"], M["mT"].ap[:, fc, :],
                            start=(fc == 0), stop=(fc == 7))
                x = self.xt[dc][b]
                self.stt("dve", x, x.ap[:, c0:c1], ps, ps.ap[:, 0:TL], self.mod_ap(l, 5, dc, m), x, x.ap[:, c0:c1],
                         ALU.mult, ALU.add, reads=[self.modv])


def _consts():
    c = np.zeros((12, 128, 128), np.float32)
    c[0] = np.eye(128, dtype=np.float32)
    c[1] = np.eye(128, dtype=np.float32)
    j = np.arange(128)[:, None]
    p = np.arange(128)[None, :]
    c[2] = (j // 64 == p // 64)
    rot = np.zeros((128, 128), np.float32)
    for i in range(128):
        d = i % 32
        if d < 16:
            rot[i + 16, i] = -1.0
        else:
            rot[i - 16, i] = 1.0
    c[3] = rot
    c[4] = (j >= p)
    c[5] = (j <= p)
    c[6] = (j // 32 == p // 32) & (j <= p)
    c[7] = (j // 32 == p // 32) & (j >= p)
    c[8] = (j // 16 == p // 16) & (j <= p)
    c[9] = (j // 32 == p // 32) & (j % 32 < 16) & (p % 32 >= 16)
    c[10] = (j // 16 == p // 16) & (j >= p)
    c[11] = (j // 32 == p // 32) & (j % 32 >= 16) & (p % 32 < 16)
    t = np.arange(DSEQ)
    inv = 10000.0 ** (-np.arange(0, 32, 2, dtype=np.float64) / 32)
    dd = np.arange(128) % 64
    pos = np.where(dd[:, None] < 32, (t // 64)[None, :], (t % 64)[None, :]).astype(np.float64)
    ang = pos * inv[dd % 16][:, None]
    rope = np.stack([np.cos(ang), np.sin(ang)]).astype(np.float32)
    return c, rope


_CACHE = {}


def _get_nc(stage=99, nlayers=DEPTH, LW=DEPTH):
    key = (stage, nlayers, LW)
    if key not in _CACHE:
        _CACHE[key] = Builder(stage, nlayers, LW).build()
    return _CACHE[key]


def kernel(x_prompt, x_sample, cache_k_attn, cache_v_attn, cache_k_win, cache_v_win, state_hgrn,
           c, c_ctx, w_mod, b_mod, norm_g, w_ffn_in, w_ffn_out, w_in, qk_norm_g, lower_bounds,
           hg_norm_g, sink_logit, w_branch, w_out, final_norm_g, _stage=99, _nlayers=DEPTH, _ncores=NCORES):
    f = lambda a: np.ascontiguousarray(np.asarray(a, dtype=np.float32))
    LW = int(np.asarray(w_mod).shape[0])
    DEPTH = LW
    NCORES = _ncores
    nc = _get_nc(_stage, _nlayers, LW)
    cst, rope = _consts()
    shared = dict(w_mod=f(w_mod), b_mod=f(b_mod), norm_g=f(norm_g), w_ffn_in=f(w_ffn_in),
                  w_ffn_out=f(w_ffn_out), w_in=f(w_in), qk_norm_g=f(qk_norm_g),
                  lower_bounds=f(lower_bounds), hg_norm_g=f(hg_norm_g), sink_logit=f(sink_logit),
                  w_branch=f(w_branch), w_out=f(w_out), final_norm_g=f(final_norm_g), cst=cst, rope=rope)
    xp = f(x_prompt).reshape(-1, NPROMPT * SEQ, D)
    xs = f(x_sample)
    in_maps = []
    for k in range(NCORES):
        d = dict(shared)
        d["xin"] = np.concatenate([xp[k], xs[k]], axis=0)
        d["cond"] = np.stack([f(c_ctx), f(c)[k]], axis=0)
        d["ck_att"] = f(cache_k_attn)[k].reshape(DEPTH, PAST, 128)
        d["cv_att"] = f(cache_v_attn)[k].reshape(DEPTH, PAST, 128)
        d["ck_win"] = f(cache_k_win)[k].reshape(DEPTH, PAST, 128)
        d["cv_win"] = f(cache_v_win)[k].reshape(DEPTH, PAST, 128)
        d["state"] = f(state_hgrn)[k]
        in_maps.append(d)
    res = run_bass_kernel_spmd(nc, in_maps, core_ids=list(range(NCORES)))
    R = res.results
    y = np.stack([r["y"] for r in R])
    y_prompt = y[:, :NPROMPT * SEQ].reshape(NCORES * NPROMPT, SEQ, D)
    y_sample = y[:, NPROMPT * SEQ:]
    cat = lambda n, shp: np.concatenate([r[n] for r in R], axis=0).reshape(shp)
    return (y_prompt, y_sample,
            cat("o_k_att", (-1, DEPTH, SEQ, 2, 64)), cat("o_v_att", (-1, DEPTH, SEQ, 2, 64)),
            cat("o_k_win", (-1, DEPTH, SEQ, 2, 64)), cat("o_v_win", (-1, DEPTH, SEQ, 2, 64)),
            cat("o_state", (-1, DEPTH, 2, 4, 128, 128)))
```

```python
from contextlib import ExitStack
import math
import os
import numpy as np
SUB = int(os.environ.get("MK_SUB", "99"))
SUB2 = int(os.environ.get("MK_SUB2", "99"))

import concourse.bass as bass
import concourse.mybir as mybir
from concourse.bass_utils import run_bass_kernel_spmd

F32 = mybir.dt.float32
BF16 = mybir.dt.bfloat16
AF = mybir.ActivationFunctionType
ALU = mybir.AluOpType

D = 1024
DFF = 2816
DEPTH = 4
NPROMPT = 4
SEQ = 256
DSEQ = 2048
PAST = 256
INW = 7168
EPS = 1e-6
NCORES = 8

O_AQ, O_AK, O_AV = 0, 512, 640
O_WQ, O_WK, O_WV = 768, 1280, 1408
O_HQ, O_HF, O_HB, O_HI, O_HG = 1536, 2048, 2560, 3072, 3584
O_GA, O_GB, O_GC = 4096, 5120, 6144

EPOCH = 12000
DMA_RING = 12


class Tile:
    __slots__ = ("ap", "w", "r", "name", "psum")

    def __init__(self, ap, name=""):
        self.ap = ap
        self.w = {}
        self.r = {}
        self.name = name
        self.psum = False

    def inherit(self, others):
        for o in others:
            for src in (o.w, o.r):
                for k, v in src.items():
                    if self.w.get(k, 0) < v:
                        self.w[k] = v


class Prog:
    ENGS = ("pe", "act", "dve", "pool", "sp")

    def __init__(self, nc):
        self.nc = nc
        self.items = {e: [] for e in self.ENGS}
        self.count = {e: 0 for e in self.ENGS}
        self.epoch = {e: 0 for e in self.ENGS}
        self.seen = {e: {} for e in self.ENGS}
        self.semkeys = []
        self.semset = set()
        self.dma_next = {e: 0 for e in self.ENGS}
        self.dma_val = {}
        self.stack = ExitStack()
        self.scopes = []
        self.residue = Tile(None, "residue")
        self.nops = 0

    def _reg(self, t):
        t.inherit([self.residue])
        if self.scopes:
            self.scopes[-1][1].append(t)
        return t

    def sbuf(self, name, shape, dtype):
        st = self.scopes[-1][0] if self.scopes else self.stack
        self.uid = getattr(self, "uid", 0) + 1
        name = f"{name}_{self.uid}"
        h = st.enter_context(self.nc.sbuf_tensor(name, list(shape), dtype))
        return self._reg(Tile(h, name))

    def view(self, ap, name=""):
        return self._reg(Tile(ap, name))

    def psum(self, name, shape, dtype):
        h = self.stack.enter_context(self.nc.psum_tensor(name, list(shape), dtype))
        t = Tile(h, name)
        t.psum = True
        return t

    def push_scope(self):
        self.scopes.append((ExitStack(), []))

    def pop_scope(self):
        st, tiles = self.scopes.pop()
        self.residue.inherit(tiles)
        st.close()

    def _sem(self, key):
        if key not in self.semset:
            self.semset.add(key)
            self.semkeys.append(key)
        return key

    def _wait(self, eng, key, val):
        if self.seen[eng].get(key, 0) >= val:
            return
        self.seen[eng][key] = val
        self.items[eng].append(("wait", key, val))

    def _deps(self, eng, reads, writes):
        need = {}
        for t in reads:
            for k, v in t.w.items():
                if need.get(k, 0) < v:
                    need[k] = v
            if t.psum:
                for k, v in t.r.items():
                    if k[0] != eng and need.get(k, 0) < v:
                        need[k] = v
        for t in writes:
            for d in (t.w, t.r):
                for k, v in d.items():
                    if need.get(k, 0) < v:
                        need[k] = v
        for k, v in need.items():
            if eng == "pe" and k[0] == "pe":
                continue
            self._wait(eng, k, v)

    def _mark(self, reads, writes, key, val):
        for t in reads:
            if t.r.get(key, 0) < val:
                t.r[key] = val
        for t in writes:
            if t.w.get(key, 0) < val:
                t.w[key] = val

    def op(self, eng, fn, reads=(), writes=()):
        if self.count[eng] >= EPOCH:
            self.epoch[eng] += 1
            self.count[eng] = 0
        key = self._sem((eng, self.epoch[eng]))
        self._deps(eng, reads, writes)
        self.count[eng] += 1
        val = self.count[eng]
        self.items[eng].append(("op", fn, key))
        self._mark(reads, writes, key, val)
        self.nops += 1
        return (key, val)

    def dma(self, eng, fn, reads=(), writes=()):
        i = self.dma_next[eng]
        self.dma_next[eng] = (i + 1) % DMA_RING
        key = self._sem(("dma", eng, i))
        prev = self.dma_val.get(key, 0)
        if prev:
            self._wait(eng, key, prev)
        self._deps(eng, reads, writes)
        val = prev + 16
        self.dma_val[key] = val
        self.items[eng].append(("dma", fn, key))
        self._mark(reads, writes, key, val)

    def wait_all_dmas(self, eng):
        for key, val in self.dma_val.items():
            self._wait(eng, key, val)

    def emit(self):
        nc = self.nc
        sems = {}
        for key in self.semkeys:
            nm = "s_" + "_".join(str(x) for x in key)
            sems[key] = self.stack.enter_context(nc.semaphore(nm))
        items = self.items

        def run(handle, lst):
            for it in lst:
                if it[0] == "wait":
                    handle.wait_ge(sems[it[1]], it[2])
                elif it[0] == "op":
                    it[1](handle).then_inc(sems[it[2]], 1)
                else:
                    it[1](handle).then_inc(sems[it[2]], 16)

        with nc.Block() as block:
            @block.tensor
            def _(e):
                run(e, items["pe"])

            @block.scalar
            def _(e):
                run(e, items["act"])

            @block.vector
            def _(e):
                run(e, items["dve"])

            @block.gpsimd
            def _(e):
                run(e, items["pool"])

            @block.sync
            def _(e):
                run(e, items["sp"])
        self.stack.close()


class Builder:
    def __init__(self, stage=99, nlayers=DEPTH, LW=DEPTH):
        self.stage = stage
        self.nlayers = nlayers
        self.LW = LW
        DEPTH = LW
        nc = bass.Bass("TRN2", target_bir_lowering=False)
        self.nc = nc
        self.P = Prog(nc)

        def din(name, shape):
            return nc.dram_tensor(name, list(shape), F32, kind="ExternalInput").ap()

        def dout(name, shape):
            return nc.dram_tensor(name, list(shape), F32, kind="ExternalOutput").ap()

        self.xin = din("xin", [3072, D])
        self.cond = din("cond", [2, D])
        self.ck_att = din("ck_att", [DEPTH, PAST, 128])
        self.cv_att = din("cv_att", [DEPTH, PAST, 128])
        self.ck_win = din("ck_win", [DEPTH, PAST, 128])
        self.cv_win = din("cv_win", [DEPTH, PAST, 128])
        self.state = din("state", [DEPTH, 2, 4, 128, 128])
        self.w_mod = din("w_mod", [DEPTH, D, 9 * D])
        self.b_mod = din("b_mod", [DEPTH, 9 * D])
        self.norm_g = din("norm_g", [DEPTH, 3, D])
        self.w_ffn_in = din("w_ffn_in", [DEPTH, 2, D, 2 * DFF])
        self.w_ffn_out = din("w_ffn_out", [DEPTH, 2, DFF, D])
        self.w_in = din("w_in", [DEPTH, D, INW])
        self.qk_g = din("qk_norm_g", [DEPTH, 2, 64])
        self.lbounds = din("lower_bounds", [DEPTH, 2, 512])
        self.hg_g = din("hg_norm_g", [DEPTH, 128])
        self.sink = din("sink_logit", [DEPTH, 8])
        self.w_branch = din("w_branch", [DEPTH, 3, 512, D])
        self.w_out = din("w_out", [DEPTH, D, D])
        self.final_g = din("final_norm_g", [D])
        self.cst = din("cst", [12, 128, 128])
        self.rope = din("rope", [2, 128, DSEQ])

        self.y = dout("y", [3072, D])
        self.o_k_att = dout("o_k_att", [NPROMPT, DEPTH, SEQ, 128])
        self.o_v_att = dout("o_v_att", [NPROMPT, DEPTH, SEQ, 128])
        self.o_k_win = dout("o_k_win", [NPROMPT, DEPTH, SEQ, 128])
        self.o_v_win = dout("o_v_win", [NPROMPT, DEPTH, SEQ, 128])
        self.o_state = dout("o_state", [NPROMPT, DEPTH, 2, 4, 128, 128])

    def mm(self, ps, ps_ap, lt, lt_ap, rt, rt_ap, start, stop, **kw):
        return self.P.op("pe", lambda e: e.matmul(ps_ap, lt_ap, rt_ap, start=start, stop=stop, **kw),
                  reads=[lt, rt], writes=[ps])

    def tr(self, ps, ps_ap, src, src_ap, ident, ident_ap):
        self.P.op("pe", lambda e: e.transpose(ps_ap, src_ap, ident_ap), reads=[src, ident], writes=[ps])

    def act(self, out_t, out_ap, in_t, in_ap, func, reads=(), **kw):
        self.P.op("act", lambda e: e.activation(out_ap, in_ap, func, **kw),
                  reads=[in_t] + list(reads), writes=[out_t])

    def tt(self, eng, out_t, out_ap, a_t, a_ap, b_t, b_ap, op):
        self.P.op(eng, lambda e: e.tensor_tensor(out_ap, a_ap, b_ap, op), reads=[a_t, b_t], writes=[out_t])

    def ts(self, eng, out_t, out_ap, a_t, a_ap, s1, s2, op0, op1=ALU.bypass, reads=()):
        self.P.op(eng, lambda e: e.tensor_scalar(out_ap, a_ap, s1, s2, op0, op1),
                  reads=[a_t] + list(reads), writes=[out_t])

    def stt(self, eng, out_t, out_ap, a_t, a_ap, scalar, b_t, b_ap, op0, op1, reads=()):
        self.P.op(eng, lambda e: e.scalar_tensor_tensor(out_ap, a_ap, scalar, b_ap, op0, op1),
                  reads=[a_t, b_t] + list(reads), writes=[out_t])

    def copy(self, eng, out_t, out_ap, in_t, in_ap):
        if eng == "act":
            self.P.op("act", lambda e: e.copy(out_ap, in_ap), reads=[in_t], writes=[out_t])
        else:
            self.P.op(eng, lambda e: e.tensor_copy(out_ap, in_ap), reads=[in_t], writes=[out_t])

    def recip(self, out_t, out_ap, in_t, in_ap):
        self.P.op("dve", lambda e: e.reciprocal(out_ap, in_ap), reads=[in_t], writes=[out_t])

    def memset(self, eng, t, ap, val):
        self.P.op(eng, lambda e: e.memset(ap, val), writes=[t])

    def dma(self, q, out_ap, in_ap, reads=(), writes=(), **kw):
        self.P.dma(q, lambda e: e.dma_start(out=out_ap, in_=in_ap, **kw), reads=reads, writes=writes)

    def getps(self):
        i = self.ps_next
        self.ps_next = (i + 1) % len(self.ps)
        return self.ps[i]

    def getpsl(self):
        i = self.psl_next
        self.psl_next = (i + 1) % 2
        return self.psl[i]

    def getw(self):
        i = self.w_next
        self.w_next = (i + 1) % len(self.wring)
        return self.wring[i]

    def wload(self, src_ap, shape):
        t = self.getw()
        a, b = shape
        v = t.ap[:, 0:a * b].rearrange("p (a b) -> p a b", b=b)
        self.dma("pool", v, src_ap, writes=[t])
        return t, v

    def build(self):
        P = self.P
        nc = self.nc
        allps = [P.psum(f"ps{i}", [128, 512], F32) for i in range(8)]
        self.ps = allps[0:6]
        self.psl = allps[6:8]
        self.psl_next = 0
        self.ps_next = 0
        self.wring = [P.sbuf(f"wr{i}", [128, 4096], BF16) for i in range(4)]
        self.w_next = 0
        ident = P.sbuf("ident", [128, 128], F32)
        self.ident = ident
        self.dma("sp", ident.ap[:], self.cst[0], writes=[ident])
        cbf = P.sbuf("cbf", [128, 11, 128], BF16)
        self.cbf = cbf
        self.dma("pool", cbf.ap[:], self.cst[1:12].rearrange("c p f -> p c f"), writes=[cbf])
        ones = P.sbuf("ones", [128, 128], BF16)
        self.ones = ones
        self.memset("dve", ones, ones.ap[:], 1.0)
        cc = P.sbuf("ccols", [128, 2], F32)
        self.cc = cc
        self.memset("dve", cc, cc.ap[:, 0:1], EPS)
        self.memset("dve", cc, cc.ap[:, 1:2], 1.0)
        negh = P.sbuf("negh", [128, 1], F32)
        self.negh = negh
        self.memset("pool", negh, negh.ap[:], -0.5)

        self.prologue()
        for g in [int(c) for c in os.environ.get("MK_GROUPS", "01")]:
            self.run_group(g)
        P.wait_all_dmas("sp")
        P.emit()
        return nc

    def prologue(self):
        P = self.P
        NV = 5 * 128
        vecs = P.sbuf("vecs", [128, NV], F32)
        self.vecs = vecs
        P.push_scope()
        stg = [P.sbuf(f"stg{i}", [128, 128], F32) for i in range(5)]
        for s in stg:
            self.memset("dve", s, s.ap[:], 0.0)
        self.c_cond = 0
        self.c_ng = 16
        self.c_fg = 112
        self.c_hg = 120
        self.c_lb = 128
        self.c_qk = 160
        self.c_bm = [168, 256, 384, 512]
        q = "sp"
        self.dma(q, stg[0].ap[0:16, :], self.cond.rearrange("m (kc p) -> (m kc) p", p=128), writes=[stg[0]])
        LW = self.LW
        self.dma(q, stg[0].ap[16:16 + LW * 24, :], self.norm_g.rearrange("l j (kc p) -> (l j kc) p", p=128), writes=[stg[0]])
        self.dma(q, stg[0].ap[112:120, :], self.final_g.rearrange("(kc p) -> kc p", p=128), writes=[stg[0]])
        self.dma(q, stg[0].ap[120:120 + LW, :], self.hg_g, writes=[stg[0]])
        self.dma(q, stg[1].ap[0:LW * 8, :], self.lbounds.rearrange("l r (h p) -> (l r h) p", p=128), writes=[stg[1]])
        qkv = self.qk_g.rearrange("l w d -> (l w) d")
        self.dma(q, stg[1].ap[32:32 + 2 * LW, 0:64], qkv, writes=[stg[1]])
        self.dma(q, stg[1].ap[32:32 + 2 * LW, 64:128], qkv, writes=[stg[1]])
        self.dma(q, stg[1].ap[40:112, :], self.b_mod[0].rearrange("(j p) -> j p", p=128), writes=[stg[1]])
        for l in range(1, LW):
            self.dma(q, stg[l + 1].ap[0:72, :], self.b_mod[l].rearrange("(j p) -> j p", p=128), writes=[stg[l + 1]])
        for i in range(5):
            ps = self.getps()
            self.tr(ps, ps.ap[:, 0:128], stg[i], stg[i].ap[:], self.ident, self.ident.ap[:])
            self.copy("dve", vecs, vecs.ap[:, i * 128:(i + 1) * 128], ps, ps.ap[:, 0:128])
        P.pop_scope()
        V = vecs.ap
        P.push_scope()
        tmp = P.sbuf("ptmp", [128, 64], F32)
        self.act(tmp, tmp.ap[:, 0:16], vecs, V[:, 0:16], AF.Exp, scale=-1.0)
        self.ts("dve", tmp, tmp.ap[:, 0:16], tmp, tmp.ap[:, 0:16], 1.0, None, ALU.add)
        self.recip(tmp, tmp.ap[:, 0:16], tmp, tmp.ap[:, 0:16])
        self.tt("dve", vecs, V[:, 0:16], vecs, V[:, 0:16], tmp, tmp.ap[:, 0:16], ALU.mult)
        lbv = V[:, self.c_lb:self.c_lb + 32].rearrange("p (l c) -> p l c", c=8)
        e = tmp.ap[:, 16:48].rearrange("p (l c) -> p l c", c=8)
        self.act(tmp, e, vecs, lbv, AF.Exp)
        s = tmp.ap[:, 48:56]
        self.tt("dve", tmp, s, tmp, e[:, 0, :], tmp, e[:, 1, :], ALU.add)
        self.tt("dve", tmp, s, tmp, s, tmp, e[:, 2, :], ALU.add)
        self.tt("dve", tmp, s, tmp, s, tmp, e[:, 3, :], ALU.add)
        self.recip(tmp, s, tmp, s)
        for l in range(4):
            self.tt("dve", tmp, e[:, l, :], tmp, e[:, l, :], tmp, s, ALU.mult)
        self.memset("dve", vecs, lbv[:, 0, :], 0.0)
        self.copy("dve", vecs, lbv[:, 1, :], tmp, e[:, 1, :])
        self.tt("dve", vecs, lbv[:, 2, :], vecs, lbv[:, 1, :], tmp, e[:, 2, :], ALU.add)
        self.tt("dve", vecs, lbv[:, 3, :], vecs, lbv[:, 2, :], tmp, e[:, 3, :], ALU.add)
        P.pop_scope()
        oml = P.sbuf("oml", [128, 32], F32)
        self.oml = oml
        self.ts("dve", oml, oml.ap[:], vecs, V[:, self.c_lb:self.c_lb + 32], -1.0, 1.0, ALU.mult, ALU.add)
        esink = P.sbuf("esink", [128, 32], F32)
        self.esink = esink
        self.memset("dve", esink, esink.ap[:], 0.0)
        self.dma("sp", esink.ap[:, 0:8 * LW], self.sink.rearrange("l h -> (l h)").partition_broadcast(128), writes=[esink])
        self.act(esink, esink.ap[:], esink, esink.ap[:], AF.Exp)

        modv = P.sbuf("modv", [128, DEPTH, 72, 2], F32)
        self.modv = modv
        P.push_scope()
        wm = [P.sbuf(f"wm{i}", [128, 8, 1152], BF16) for i in range(2)]
        csb = P.sbuf("csb", [128, 16], BF16)
        self.copy("dve", csb, csb.ap[:], vecs, V[:, 0:16])
        nl = self.nlayers
        for l in range(nl):
            psm = self.getps()
            pv = psm.ap[:, 0:144].rearrange("p (j m) -> p j m", m=2)
            for pc in range(8):
                wt = wm[(l * 8 + pc) % 2]
                src = self.w_mod[l].rearrange("(kc p) f -> p kc f", p=128)[:, :, pc * 1152:(pc + 1) * 1152]
                self.dma("pool", wt.ap[:], src, writes=[wt])
                for fj in range(9):
                    j72 = pc * 9 + fj
                    for kc in range(8):
                        self.mm(psm, pv[:, j72, :], wt, wt.ap[:, kc, fj * 128:(fj + 1) * 128],
                                csb, csb.ap[:, kc:kc + 9:8], start=(kc == 0), stop=(kc == 7))
            bm = V[:, self.c_bm[l]:self.c_bm[l] + 72]
            mv = modv.ap[:, l, :, :]
            self.tt("dve", modv, mv, psm, pv, vecs, bm.unsqueeze(2).to_broadcast([128, 72, 2]), ALU.add)
            for j in range(3):
                sc = modv.ap[:, l, (3 * j + 1) * 8:(3 * j + 2) * 8, :]
                g = V[:, self.c_ng + l * 24 + j * 8: self.c_ng + l * 24 + j * 8 + 8]
                self.stt("dve", modv, sc, modv, sc, 1.0, vecs, g.unsqueeze(2).to_broadcast([128, 8, 2]),
                         ALU.add, ALU.mult)
            for j in (0, 2):
                gt = modv.ap[:, l, (3 * j + 2) * 8:(3 * j + 3) * 8, :]
                self.ts("dve", modv, gt, modv, gt, 0.5, None, ALU.mult)
        P.pop_scope()

    def mod_ap(self, l, i, kc, m):
        return self.modv.ap[:, l, i * 8 + kc, m:m + 1]

    def run_group(self, g):
        P = self.P
        ntok = 1024 if g == 0 else 2048
        row0 = 0 if g == 0 else 1024
        nblk = ntok // 512
        self.g = g
        self.ntok = ntok
        P.push_scope()
        xT = P.sbuf(f"xT{g}", [128, 8, ntok], F32)
        xt = [[P.view(xT.ap[:, kc, b * 512:(b + 1) * 512], f"x{kc}_{b}") for b in range(nblk)] for kc in range(8)]
        self.xt = xt
        P.push_scope()
        xs = [P.sbuf(f"xs{i}", [128, 4, D], F32) for i in range(2)]
        for b in range(nblk):
            st = xs[b % 2]
            src = self.xin[row0 + b * 512: row0 + (b + 1) * 512, :].rearrange("(tt p) f -> p tt f", p=128)
            self.dma("sp", st.ap[:], src, writes=[st])
            for kc in range(8):
                ps = self.getps()
                for tt_ in range(4):
                    self.tr(ps, ps.ap[:, tt_ * 128:(tt_ + 1) * 128], st, st.ap[:, tt_, kc * 128:(kc + 1) * 128],
                            self.ident, self.ident.ap[:])
                self.copy("act" if kc % 2 else "dve", xt[kc][b], xt[kc][b].ap, ps, ps.ap[:])
        P.pop_scope()

        for l in range(self.nlayers):
            if self.stage >= 1:
                self.ffn(l, 0)
            if self.stage >= 2:
                self.mixer(l)
            if self.stage >= 3:
                self.ffn(l, 1)

        P.push_scope()
        yo = [P.sbuf(f"yo{i}", [128, 4, D], F32) for i in range(2)]
        sq = [P.sbuf(f"fsq{i}", [128, 512], BF16) for i in range(2)]
        vv = P.sbuf("fv", [128, 512], F32)
        rstd = P.sbuf("frstd", [128, 512], F32)
        tn = [P.sbuf(f"ftn{i}", [128, 512], F32) for i in range(2)]
        for b in range(nblk):
            if self.stage >= 4:
                self.rstd_of(b, sq, vv, rstd)
            ot = yo[b % 2]
            for kc in range(8):
                t = tn[kc % 2]
                if self.stage >= 4:
                    gcol = self.vecs.ap[:, self.c_fg + kc:self.c_fg + kc + 1]
                    self.stt("dve", t, t.ap[:], xt[kc][b], xt[kc][b].ap, gcol, rstd, rstd.ap[:],
                             ALU.mult, ALU.mult, reads=[self.vecs])
                    src_t, src_ap = t, t.ap
                else:
                    src_t, src_ap = xt[kc][b], xt[kc][b].ap
                ps = self.getps()
                for tt_ in range(4):
                    self.tr(ps, ps.ap[:, tt_ * 128:(tt_ + 1) * 128], src_t, src_ap[:, tt_ * 128:(tt_ + 1) * 128],
                            self.ident, self.ident.ap[:])
                self.copy("act", ot, ot.ap[:, :, kc * 128:(kc + 1) * 128],
                          ps, ps.ap[:].rearrange("p (t f) -> p t f", f=128))
            dst = self.y[row0 + b * 512: row0 + (b + 1) * 512, :].rearrange("(tt p) f -> p tt f", p=128)
            self.dma("sp", dst, ot.ap[:], reads=[ot])
        P.pop_scope()
        P.pop_scope()

    def rstd_of(self, b, sq, vv, rstd, cols=None):
        xt = self.xt
        c0, c1 = cols if cols else (0, 512)
        n = c1 - c0
        ps = self.getps()
        for kc in range(8):
            s = sq[kc % 2]
            self.act(s, s.ap[:, 0:n], xt[kc][b], xt[kc][b].ap[:, c0:c1], AF.Square)
            self.mm(ps, ps.ap[:, 0:n], self.ones, self.ones.ap[:], s, s.ap[:, 0:n], start=(kc == 0), stop=(kc == 7))
        self.act(vv, vv.ap[:, 0:n], ps, ps.ap[:, 0:n], AF.Ln, reads=[self.cc], scale=1.0 / D, bias=self.cc.ap[:, 0:1])
        self.act(rstd, rstd.ap[:, 0:n], vv, vv.ap[:, 0:n], AF.Exp, scale=-0.5)

    def adaln(self, l, j, b, hts, sq, vv, rstd, tn, cols=None):
        m = 0 if self.g == 0 else 1
        c0, c1 = cols if cols else (0, 512)
        n = c1 - c0
        self.rstd_of(b, sq, vv, rstd, cols)
        for kc in range(8):
            t = tn[kc % 2]
            x = self.xt[kc][b]
            self.tt("dve", t, t.ap[:, 0:n], x, x.ap[:, c0:c1], rstd, rstd.ap[:, 0:n], ALU.mult)
            ht, hap = hts[kc]
            self.act(ht, hap, t, t.ap[:, 0:n], AF.Identity, reads=[self.modv],
                     scale=self.mod_ap(l, 3 * j + 1, kc, m), bias=self.mod_ap(l, 3 * j, kc, m))

    def ffn(self, l, i):
        P = self.P
        j = 0 if i == 0 else 2
        m = 0 if self.g == 0 else 1
        nst = self.ntok // 1024
        P.push_scope()
        hT = P.sbuf("f_hT", [128, 8, 1024], BF16)
        ht = [[P.view(hT.ap[:, kc, h * 512:(h + 1) * 512]) for h in range(2)] for kc in range(8)]
        aT = P.sbuf("f_aT", [128, 11, 1024], BF16)
        at = [[P.view(aT.ap[:, f, h * 512:(h + 1) * 512]) for h in range(2)] for f in range(11)]
        sq = [P.sbuf(f"f_sq{k}", [128, 512], BF16) for k in range(2)]
        vv = P.sbuf("f_v", [128, 512], F32)
        rstd = P.sbuf("f_rstd", [128, 512], F32)
        tn = [P.sbuf(f"f_tn{k}", [128, 512], F32) for k in range(2)]
        sl = [P.sbuf(f"f_sl{k}", [128, 512], F32) for k in range(2)]
        wgu = self.w_ffn_in[l, i].rearrange("(kc p) f -> p kc f", p=128)
        wdn = self.w_ffn_out[l, i].rearrange("(fc p) d -> p fc d", p=128)
        for st in range(nst):
            for h in range(2):
                b = st * 2 + h
                self.adaln(l, j, b, [(ht[kc][h], ht[kc][h].ap) for kc in range(8)], sq, vv, rstd, tn)
            for fh in range(2):
                f0 = fh * 11
                for fp in range(0, 11, 2):
                    nf = min(2, 11 - fp)
                    wt = self.getw()
                    wv = wt.ap[:, 0:8 * 2 * 256].rearrange("p (k u c) -> p k u c", u=2, c=256)
                    c = (f0 + fp) * 128
                    self.dma("pool", wv[:, :, 0, 0:nf * 128], wgu[:, :, c:c + nf * 128], writes=[wt])
                    self.dma("pool", wv[:, :, 1, 0:nf * 128], wgu[:, :, DFF + c:DFF + c + nf * 128], writes=[wt])
                    for ff in range(nf):
                        for h in range(2):
                            pg, pu = self.getps(), self.getps()
                            for u, ps in ((0, pg), (1, pu)):
                                for kc in range(8):
                                    self.mm(ps, ps.ap[:], wt, wv[:, kc, u, ff * 128:(ff + 1) * 128],
                                            ht[kc][h], ht[kc][h].ap, start=(kc == 0), stop=(kc == 7))
                            s = sl[(ff + h) % 2]
                            self.act(s, s.ap[:], pg, pg.ap[:], AF.Silu)
                            a = at[fp + ff][h]
                            self.tt("dve", a, a.ap, s, s.ap[:], pu, pu.ap[:], ALU.mult)
                for dp in range(0, 8, 2):
                    wt = self.getw()
                    wv = wt.ap[:, 0:11 * 256].rearrange("p (f c) -> p f c", c=256)
                    self.dma("pool", wv, wdn[:, f0:f0 + 11, dp * 128:(dp + 2) * 128], writes=[wt])
                    for dd in range(2):
                        dc = dp + dd
                        for h in range(2):
                            b = st * 2 + h
                            ps = self.getps()
                            for f in range(11):
                                self.mm(ps, ps.ap[:], wt, wv[:, f, dd * 128:(dd + 1) * 128],
                                        at[f][h], at[f][h].ap, start=(f == 0), stop=(f == 10))
                            x = self.xt[dc][b]
                            self.stt("dve", x, x.ap, ps, ps.ap[:], self.mod_ap(l, 3 * j + 2, dc, m), x, x.ap,
                                     ALU.mult, ALU.add, reads=[self.modv])
        P.pop_scope()

    def mixer(self, l):
        P = self.P
        S = (self.g == 1)
        TL = 512 if S else 256
        self.TL = TL
        nsub = TL // 128
        self.nsub = nsub
        NK = (DSEQ + PAST) if S else SEQ
        nkb = NK // 128
        ctxk = PAST if S else 0
        P.push_scope()
        M = {}
        self.M = M
        M["kA"] = P.sbuf("m_kA", [128, NK], BF16)
        M["kW"] = P.sbuf("m_kW", [128, NK], BF16)
        M["vA"] = P.sbuf("m_vA", [128, nkb, 2, 65], BF16)
        M["vW"] = P.sbuf("m_vW", [128, nkb, 2, 65], BF16)
        for n in ("vA", "vW"):
            self.memset("pool", M[n], M[n].ap[:, :, :, 64:65], 1.0)
        M["Sf"] = [P.sbuf(f"m_Sf{h}", [128, 128], F32) for h in range(4)]
        M["Sb"] = [P.sbuf(f"m_Sb{h}", [128, 128], F32) for h in range(4)]
        M["Sbf"] = [P.sbuf(f"m_Sbf{i}", [128, 128], BF16) for i in range(3)]
        ntile = (DSEQ // TL) if S else 1
        M["bound"] = [[P.sbuf(f"m_bd{t}_{h}", [128, 128], F32) for h in range(4)] for t in range(ntile - 1)]
        M["hT"] = P.sbuf("m_hT", [128, 8, TL], BF16)
        M["hts"] = [P.view(M["hT"].ap[:, kc, :]) for kc in range(8)]
        M["sq"] = [P.sbuf(f"m_sq{k}", [128, TL], BF16) for k in range(2)]
        M["f"] = [P.sbuf(f"m_f{k}", [128, TL], F32) for k in range(8)]
        M["qo"] = P.sbuf("m_qo", [128, 8, TL], BF16)
        M["qT"] = Tile(M["qo"].ap[:, 0:4, :], "qT")
        M["otok"] = Tile(M["qo"].ap[:, 4:8, :].rearrange("p a t -> p (a t)").rearrange("p (s f) -> p s f", f=512), "otok")
        M["mT"] = Tile(M["qo"].ap[:], "mT")
        for n_ in ("qT", "otok", "mT"):
            M[n_].w = M["qo"].w
            M[n_].r = M["qo"].r
        M["pT"] = [P.sbuf(f"m_pT{k}", [128, TL], BF16) for k in range(2)]
        M["oT"] = P.sbuf("m_oT", [128, 4, TL], BF16)
        M["mg"] = P.sbuf("m_mg", [128, 8, TL], F32)
        M["mgs"] = [P.view(M["mg"].ap[:, fc, :]) for fc in range(8)]
        M["b16"] = [P.sbuf(f"m_b16_{k}", [128, TL], BF16) for k in range(5)]
        M["khtok"] = P.sbuf("m_khtok", [128, nsub, 128], BF16)
        M["Vt"] = P.sbuf("m_Vt", [128, nsub, 512], BF16)
        M["AT"] = [P.sbuf(f"m_AT{k}", [128, 128], BF16) for k in range(4)]
        M["dec"] = P.sbuf("m_dec", [128, 2, 16], F32)
        M["sm"] = P.sbuf("m_sm", [128, 16], F32)
        M["cm"] = P.sbuf("m_cm", [128, TL], BF16)
        self.memset("pool", M["cm"], M["cm"].ap[:], 1.0)
        self.memset("pool", M["cm"], M["cm"].ap[:, 0:TL:32], 0.0)
        M["cm16"] = P.sbuf("m_cm16", [128, TL], BF16)
        self.memset("pool", M["cm16"], M["cm16"].ap[:], 1.0)
        self.memset("pool", M["cm16"], M["cm16"].ap[:, 0:TL:16], 0.0)
        if S:
            M["rc"] = P.sbuf("m_rc", [128, TL], F32)
            M["rs"] = P.sbuf("m_rs", [128, TL], F32)
        else:
            M["ko"] = P.sbuf("m_ko", [128, nsub, 2, 128], F32)
            M["vo"] = P.sbuf("m_vo", [128, nsub, 256], F32)

        if S:
            for ck, cv, kn, vn in ((self.ck_att, self.cv_att, "kA", "vA"), (self.ck_win, self.cv_win, "kW", "vW")):
                st = M["f"][0]
                stv = st.ap[:, 0:256].rearrange("p (b f) -> p b f", f=128)
                self.dma("sp", stv, ck[l].rearrange("(b p) f -> p b f", p=128), writes=[st])
                ps = self.getps()
                for b_ in range(2):
                    self.tr(ps, ps.ap[:, b_ * 128:(b_ + 1) * 128], st, stv[:, b_, :], self.ident, self.ident.ap[:])
                self.copy("act", M[kn], M[kn].ap[:, 0:256], ps, ps.ap[:, 0:256])
                for b_ in range(2):
                    self.dma("pool", M[vn].ap[:, b_, :, 0:64],
                             cv[l][b_ * 128:(b_ + 1) * 128, :].rearrange("p (g d) -> p g d", d=64), writes=[M[vn]])
            for h in range(4):
                self.dma("sp", M["Sf"][h].ap[:], self.state[l, 0, h], writes=[M["Sf"][h]])
                self.dma("sp", M["Sb"][h].ap[:], self.state[l, 1, h], writes=[M["Sb"][h]])
            tiles = [(t, t, 0, 512, t * 512) for t in range(4)]
            if SUB2 < 2:
                tiles = []
            for tl in reversed(tiles):
                self.pass1(l, tl, None)
            for tl in tiles:
                self.pass2(l, tl, None)
        else:
            for sq_ in range(NPROMPT):
                for h in range(4):
                    self.memset("pool", M["Sf"][h], M["Sf"][h].ap[:], 0.0)
                    self.memset("pool", M["Sb"][h], M["Sb"][h].ap[:], 0.0)
                tl = (0, sq_ // 2, (sq_ % 2) * 256, (sq_ % 2) * 256 + 256, 0)
                self.pass1(l, tl, sq_)
                self.pass2(l, tl, sq_)
                for h in range(4):
                    self.dma("sp", self.o_state[sq_, l, 0, h], M["Sf"][h].ap[:], reads=[M["Sf"][h]])
                    self.dma("sp", self.o_state[sq_, l, 1, h], M["Sb"][h].ap[:], reads=[M["Sb"][h]])
        P.pop_scope()

    def m_adaln(self, l, tl):
        M = self.M
        _, b, c0, c1, _ = tl
        self.adaln(l, 1, b, [(M["hts"][kc], M["hts"][kc].ap) for kc in range(8)], M["sq"],
                   M["f"][0], M["f"][1], M["f"][2:4], cols=(c0, c1))

    def proj_fm(self, wt, wap_fn, n=8):
        M = self.M
        ps = self.getps()
        for kc in range(n):
            self.mm(ps, ps.ap[:, 0:self.TL], wt, wap_fn(kc), M["hts"][kc], M["hts"][kc].ap,
                    start=(kc == 0), stop=(kc == n - 1))
        return ps

    def headnorm(self, ps, gcol, out_t, nrm_mat_ap, dim):
        M = self.M
        TL = self.TL
        sq = M["sq"][0]
        self.act(sq, sq.ap[:], ps, ps.ap[:, 0:TL], AF.Square)
        p2 = self.getps()
        self.mm(p2, p2.ap[:, 0:TL], self.cbf, nrm_mat_ap, sq, sq.ap[:], start=True, stop=True)
        vv, rstd = M["f"][4], M["f"][5]
        self.act(vv, vv.ap[:], p2, p2.ap[:, 0:TL], AF.Ln, reads=[self.cc], scale=1.0 / dim, bias=self.cc.ap[:, 0:1])
        self.act(rstd, rstd.ap[:], vv, vv.ap[:], AF.Exp, scale=-0.5)
        self.stt("dve", out_t, out_t.ap[:], ps, ps.ap[:, 0:TL], gcol, rstd, rstd.ap[:], ALU.mult, ALU.mult,
                 reads=[self.vecs])

    def rope_to(self, kn, dst_t, dst_ap):
        M = self.M
        TL = self.TL
        kb = M["sq"][1]
        if SUB2 < 4:
            self.copy("act", dst_t, dst_ap, kn, kn.ap[:])
            return
        self.copy("act", kb, kb.ap[:], kn, kn.ap[:])
        ps = self.getps()
        self.mm(ps, ps.ap[:, 0:TL], self.cbf, self.cbf.ap[:, 2, :], kb, kb.ap[:], start=True, stop=True)
        t1, t2 = M["f"][6], M["f"][7]
        if SUB2 < 5:
            self.copy("act", dst_t, dst_ap, ps, ps.ap[:, 0:TL])
            return
        self.tt("pool", t1, t1.ap[:], kn, kn.ap[:], M["rc"], M["rc"].ap[:], ALU.mult)
        if SUB2 < 6:
            self.copy("act", dst_t, dst_ap, t1, t1.ap[:])
            return
        self.tt("dve", t2, t2.ap[:], ps, ps.ap[:, 0:TL], M["rs"], M["rs"].ap[:], ALU.mult)
        if SUB2 < 7:
            self.copy("act", dst_t, dst_ap, t2, t2.ap[:])
            return
        self.tt("dve", dst_t, dst_ap, t1, t1.ap[:], t2, t2.ap[:], ALU.add)

    def sigmoid_parts(self, ps, r_t):
        TL = self.TL
        self.act(r_t, r_t.ap[:], ps, ps.ap[:, 0:TL], AF.Exp, scale=-1.0)
        self.act(r_t, r_t.ap[:], r_t, r_t.ap[:], AF.Ln, reads=[self.cc], bias=self.cc.ap[:, 1:2])
        self.act(r_t, r_t.ap[:], r_t, r_t.ap[:], AF.Exp, scale=-1.0)

    def gates_f(self, l, h, r, ps, f_t, k_t, lg_t, b_t):
        M = self.M
        col = l * 8 + r * 4 + h
        lb = self.vecs.ap[:, self.c_lb + col:self.c_lb + col + 1]
        om = self.oml.ap[:, col:col + 1]
        self.sigmoid_parts(ps, f_t)
        self.ts("dve", f_t, f_t.ap[:], f_t, f_t.ap[:], om, lb, ALU.mult, ALU.add, reads=[self.vecs, self.oml])
        self.ts("pool", k_t, k_t.ap[:], f_t, f_t.ap[:], -1.0, 1.0, ALU.mult, ALU.add)
        self.act(lg_t, lg_t.ap[:], f_t, f_t.ap[:], AF.Ln)
        cm = M["cm"]
        self.P.op("dve", lambda e: e.tensor_tensor_scan(b_t.ap[:], cm.ap[:], lg_t.ap[:], 0.0, ALU.mult, ALU.add),
                  reads=[cm, lg_t], writes=[b_t])

    def vtok(self, wt, wv):
        M = self.M
        for sub in range(self.nsub):
            ps = self.getps()
            for kc in range(8):
                self.mm(ps, ps.ap[:], M["hts"][kc], M["hts"][kc].ap[:, sub * 128:(sub + 1) * 128],
                        wt, wv[:, kc, :], start=(kc == 0), stop=(kc == 7))
            self.copy("act", M["Vt"], M["Vt"].ap[:, sub, :], ps, ps.ap[:])

    def kh_transpose(self, kh):
        M = self.M
        ps = self.getps()
        pb = ps.ap[:].bitcast(BF16)
        for sub in range(self.nsub):
            self.tr(ps, pb[:, sub * 128:(sub + 1) * 128], kh, kh.ap[:, sub * 128:(sub + 1) * 128],
                    self.cbf, self.cbf.ap[:, 0, :])
        n = self.nsub * 128
        self.copy("act", M["khtok"], M["khtok"].ap[:].rearrange("p s d -> p (s d)"), ps, pb[:, 0:n])

    def u_mats(self, h, sub):
        M = self.M
        ps = self.getps()
        pv = ps.ap[:].rearrange("p (j v) -> p j v", v=128)
        for j in range(4):
            tok = self.mm(ps, pv[:, j, :], M["khtok"], M["khtok"].ap[32 * j:32 * j + 32, sub, :],
                          M["Vt"], M["Vt"].ap[32 * j:32 * j + 32, sub, h * 128:(h + 1) * 128],
                          start=True, stop=True, tile_position=(32 * j, 0))
            self.P._wait("pe", tok[0], tok[1])
        return ps, pv

    def pass1(self, l, tl, seq):
        M = self.M
        P = self.P
        S = (self.g == 1)
        TL, nsub = self.TL, self.nsub
        idx, b, c0, c1, soff = tl
        koff = (PAST if S else 0) + soff
        wi = self.w_in[l].rearrange("(kc p) f -> p kc f", p=128)
        self.m_adaln(l, tl)
        if S:
            self.dma("sp", M["rc"].ap[:], self.rope[0][:, soff:soff + TL], writes=[M["rc"]])
            self.dma("sp", M["rs"].ap[:], self.rope[1][:, soff:soff + TL], writes=[M["rs"]])
        if SUB2 < 3:
            return
        wt = self.getw()
        wv = wt.ap[:, 0:8 * 256].rearrange("p (k c) -> p k c", c=256)
        self.dma("pool", wv[:, :, 0:128], wi[:, :, O_AK:O_AK + 128], writes=[wt])
        self.dma("pool", wv[:, :, 128:256], wi[:, :, O_WK:O_WK + 128], writes=[wt])
        for which, kn_name in ((0, "kA"), (1, "kW")):
            ps = self.proj_fm(wt, lambda kc, w=which: wv[:, kc, w * 128:(w + 1) * 128])
            kn = M["f"][2 + which]
            if which == 0:
                gk = self.vecs.ap[:, self.c_qk + l * 2 + 1:self.c_qk + l * 2 + 2]
                self.headnorm(ps, gk, kn, self.cbf.ap[:, 1, :], 64)
            else:
                self.copy("act", kn, kn.ap[:], ps, ps.ap[:, 0:TL])
            dst = M[kn_name]
            if S:
                self.rope_to(kn, dst, dst.ap[:, koff:koff + TL])
            else:
                self.copy("act", dst, dst.ap[:, koff:koff + TL], kn, kn.ap[:])
                for sub in range(nsub):
                    p2 = self.getps()
                    self.tr(p2, p2.ap[:, 0:128], kn, kn.ap[:, sub * 128:(sub + 1) * 128], self.ident, self.ident.ap[:])
                    self.copy("act", M["ko"], M["ko"].ap[:, sub, which, :], p2, p2.ap[:, 0:128])
        if not S:
            self.dma("sp", self.o_k_att[seq, l].rearrange("(s p) f -> p s f", p=128), M["ko"].ap[:, :, 0, :], reads=[M["ko"]])
            self.dma("sp", self.o_k_win[seq, l].rearrange("(s p) f -> p s f", p=128), M["ko"].ap[:, :, 1, :], reads=[M["ko"]])
        if SUB < 2:
            return
        wt = self.getw()
        wv2 = wt.ap[:, 0:8 * 256].rearrange("p (k c) -> p k c", c=256)
        self.dma("pool", wv2[:, :, 0:128], wi[:, :, O_AV:O_AV + 128], writes=[wt])
        self.dma("pool", wv2[:, :, 128:256], wi[:, :, O_WV:O_WV + 128], writes=[wt])
        for sub in range(nsub):
            ps = self.getps()
            for kc in range(8):
                self.mm(ps, ps.ap[:, 0:256], M["hts"][kc], M["hts"][kc].ap[:, sub * 128:(sub + 1) * 128],
                        wt, wv2[:, kc, :], start=(kc == 0), stop=(kc == 7))
            kb = koff // 128 + sub
            self.copy("act", M["vA"], M["vA"].ap[:, kb, :, 0:64], ps, ps.ap[:, 0:128].rearrange("p (g d) -> p g d", d=64))
            self.copy("dve", M["vW"], M["vW"].ap[:, kb, :, 0:64], ps, ps.ap[:, 128:256].rearrange("p (g d) -> p g d", d=64))
            if not S:
                self.copy("act", M["vo"], M["vo"].ap[:, sub, :], ps, ps.ap[:, 0:256])
        if not S:
            self.dma("sp", self.o_v_att[seq, l].rearrange("(s p) f -> p s f", p=128), M["vo"].ap[:, :, 0:128], reads=[M["vo"]])
            self.dma("sp", self.o_v_win[seq, l].rearrange("(s p) f -> p s f", p=128), M["vo"].ap[:, :, 128:256], reads=[M["vo"]])
        if SUB < 3:
            return
        wti = self.getw()
        wvi = wti.ap[:, 0:4096].rearrange("p (k c) -> p k c", c=512)
        self.dma("pool", wvi, wi[:, :, O_HI:O_HI + 512], writes=[wti])
        self.vtok(wti, wvi)
        wtb = self.getw()
        wvb = wtb.ap[:, 0:4096].rearrange("p (k c) -> p k c", c=512)
        self.dma("pool", wvb, wi[:, :, O_HB:O_HB + 512], writes=[wtb])
        nch = TL // 32
        for h in range(4):
            if idx < len(M["bound"]):
                self.copy("pool", M["bound"][idx][h], M["bound"][idx][h].ap[:], M["Sb"][h], M["Sb"][h].ap[:])
            ps = self.proj_fm(wtb, lambda kc, h=h: wvb[:, kc, h * 128:(h + 1) * 128])
            f_t, k_t, lg_t, b_t = M["f"][2], M["f"][3], M["f"][4], M["f"][5]
            self.gates_f(l, h, 1, ps, f_t, k_t, lg_t, b_t)
            self.khat_dec_b(k_t, lg_t, b_t, M["b16"][0], 1)
            self.kh_transpose(M["b16"][0])
            Sb = M["Sb"][h]
            for sub in reversed(range(nsub)):
                psu, pv = self.u_mats(h, sub)
                for j in reversed(range(4)):
                    c = sub * 4 + j
                    self.stt("dve", Sb, Sb.ap[:], Sb, Sb.ap[:], M["dec"].ap[:, 1, c:c + 1], psu, pv[:, j, :],
                             ALU.mult, ALU.add, reads=[M["dec"]])

    def khat_dec_b(self, k_t, lg_t, b_t, kh16, r):
        M = self.M
        TL = self.TL
        nch = TL // 32
        t = M["f"][6]
        self.tt("pool", t, t.ap[:], b_t, b_t.ap[:], lg_t, lg_t.ap[:], ALU.subtract)
        self.act(t, t.ap[:], t, t.ap[:], AF.Exp)
        self.tt("dve", kh16, kh16.ap[:], k_t, k_t.ap[:], t, t.ap[:], ALU.mult)
        self.act(M["dec"], M["dec"].ap[:, r, 0:nch], b_t, b_t.ap[:, 31:TL:32], AF.Exp)

    def attention(self, l, tl, which):
        M = self.M
        S = (self.g == 1)
        TL, nsub = self.TL, self.nsub
        idx, b, c0, c1, soff = tl
        wi = self.w_in[l].rearrange("(kc p) f -> p kc f", p=128)
        o_q = O_AQ if which == 0 else O_WQ
        kX, vX = (M["kA"], M["vA"]) if which == 0 else (M["kW"], M["vW"])
        wt = self.getw()
        wv = wt.ap[:, 0:4096].rearrange("p (k c) -> p k c", c=512)
        for c in range(4):
            self.dma("pool", wv[:, :, c * 128:c * 128 + 64], wi[:, :, o_q + c * 64:o_q + c * 64 + 64], writes=[wt])
            self.dma("pool", wv[:, :, c * 128 + 64:c * 128 + 128], wi[:, :, o_q + (4 + c) * 64:o_q + (4 + c) * 64 + 64], writes=[wt])
        qT = M["qT"]
        for c in range(4):
            ps = self.proj_fm(wt, lambda kc, c=c: wv[:, kc, c * 128:(c + 1) * 128])
            if which == 0:
                qn = M["f"][2]
                gq = self.vecs.ap[:, self.c_qk + l * 2:self.c_qk + l * 2 + 1]
                self.headnorm(ps, gq, qn, self.cbf.ap[:, 1, :], 64)
                if S:
                    self.rope_to(qn, qT, qT.ap[:, c, :])
                else:
                    self.copy("act", qT, qT.ap[:, c, :], qn, qn.ap[:])
            else:
                if S:
                    qn = M["f"][2]
                    self.copy("act", qn, qn.ap[:], ps, ps.ap[:, 0:TL])
                    self.rope_to(qn, qT, qT.ap[:, c, :])
                else:
                    self.copy("act", qT, qT.ap[:, c, :], ps, ps.ap[:, 0:TL])
        sched = []
        if which == 0 or not S:
            nkb = (DSEQ + PAST) // 128 if S else SEQ // 128
            sched = [(kb, 0, nsub - 1, {}) for kb in range(nkb)]
        else:
            sched = [(0, 0, nsub - 1, {}), (1, 0, nsub - 1, {})]
            qb0 = soff // 128
            for kbi in range(qb0 - 1, qb0 + nsub + 1):
                if kbi < 0 or kbi >= DSEQ // 128:
                    continue
                lo = max(kbi - 1, qb0) - qb0
                hi = min(kbi + 1, qb0 + nsub - 1) - qb0
                masks = {}
                for sub in range(lo, hi + 1):
                    qb = qb0 + sub
                    if kbi == qb - 1:
                        masks[sub] = 3
                    elif kbi == qb + 1:
                        masks[sub] = 4
                sched.append((2 + kbi, lo, hi, masks))
        otok = M["otok"]
        for c in range(4):
            for a in range(2):
                head = c + 4 * a
                acc = self.getpsl()
                av = acc.ap[:, 0:nsub * 65].rearrange("p (s e) -> p s e", e=65)
                first = {sub: True for sub in range(nsub)}
                started = False
                last_kb = {}
                for (kb, lo, hi, masks) in sched:
                    for sub in range(lo, hi + 1):
                        last_kb[sub] = kb
                def score(si):
                    kb, lo, hi, masks = sched[si]
                    n0, n1 = lo * 128, (hi + 1) * 128
                    sps = self.getps()
                    self.mm(sps, sps.ap[:, n0:n1], kX, kX.ap[64 * a:64 * a + 64, kb * 128:(kb + 1) * 128],
                            qT, qT.ap[64 * a:64 * a + 64, c, n0:n1], start=True, stop=True)
                    return sps

                SKEW = 2
                pend = [score(si) for si in range(min(SKEW, len(sched)))]
                for si, (kb, lo, hi, masks) in enumerate(sched):
                    n0, n1 = lo * 128, (hi + 1) * 128
                    sps = pend.pop(0)
                    if si + SKEW < len(sched):
                        pend.append(score(si + SKEW))
                    pT = M["pT"][si % 2]
                    self.act(pT, pT.ap[:, n0:n1], sps, sps.ap[:, n0:n1], AF.Exp, scale=0.125)
                    for sub, mi in masks.items():
                        self.tt("pool", pT, pT.ap[:, sub * 128:(sub + 1) * 128], pT, pT.ap[:, sub * 128:(sub + 1) * 128],
                                self.cbf, self.cbf.ap[:, mi, :], ALU.mult)
                    for sub in range(lo, hi + 1):
                        self.mm(acc, av[:, sub, :], pT, pT.ap[:, sub * 128:(sub + 1) * 128],
                                vX, vX.ap[:, kb, a, :], start=(not started), stop=(last_kb[sub] == kb),
                                skip_group_check=True)
                        started = True
                sm = M["sm"]
                if which == 1:
                    self.ts("dve", sm, sm.ap[:, 0:nsub], acc, av[:, :, 64], self.esink.ap[:, l * 8 + head:l * 8 + head + 1],
                            None, ALU.add, reads=[self.esink])
                    self.recip(sm, sm.ap[:, 0:nsub], sm, sm.ap[:, 0:nsub])
                else:
                    self.recip(sm, sm.ap[:, 0:nsub], acc, av[:, :, 64])
                self.tt("dve", otok, otok.ap[:, :, head * 64:(head + 1) * 64], acc, av[:, :, 0:64],
                        sm, sm.ap[:, 0:nsub].unsqueeze(2).to_broadcast([128, nsub, 64]), ALU.mult)
        for k4 in range(4):
            ps = self.getps()
            pb = ps.ap[:].bitcast(BF16)
            for sub in range(nsub):
                self.tr(ps, pb[:, sub * 128:(sub + 1) * 128], otok, otok.ap[:, sub, k4 * 128:(k4 + 1) * 128],
                        self.cbf, self.cbf.ap[:, 0, :])
            self.copy("act", M["oT"], M["oT"].ap[:, k4, :], ps, pb[:, 0:TL])

    def hgrn2(self, l, tl):
        M = self.M
        S = (self.g == 1)
        TL, nsub = self.TL, self.nsub
        idx, b, c0, c1, soff = tl
        nch = TL // 32
        wi = self.w_in[l].rearrange("(kc p) f -> p kc f", p=128)
        wti = self.getw()
        wvi = wti.ap[:, 0:4096].rearrange("p (k c) -> p k c", c=512)
        self.dma("pool", wvi, wi[:, :, O_HI:O_HI + 512], writes=[wti])
        self.vtok(wti, wvi)
        F = M["f"]
        for h in range(4):
            wt = self.getw()
            wv = wt.ap[:, 0:4096].rearrange("p (k c) -> p k c", c=512)
            for ti, off in enumerate((O_HQ, O_HF, O_HB, O_HG)):
                self.dma("pool", wv[:, :, ti * 128:(ti + 1) * 128], wi[:, :, off + h * 128:off + (h + 1) * 128], writes=[wt])
            ps = self.proj_fm(wt, lambda kc: wv[:, kc, 0:128])
            Q = F[0]
            self.sigmoid_parts(ps, Q)
            self.tt("dve", Q, Q.ap[:], Q, Q.ap[:], ps, ps.ap[:, 0:TL], ALU.mult)
            ps = self.proj_fm(wt, lambda kc: wv[:, kc, 384:512])
            OG = F[1]
            self.sigmoid_parts(ps, OG)
            self.tt("dve", OG, OG.ap[:], OG, OG.ap[:], ps, ps.ap[:, 0:TL], ALU.mult)
            osum = F[7]
            for r in range(2):
                ps = self.proj_fm(wt, lambda kc, r=r: wv[:, kc, 128 + r * 128:256 + r * 128])
                f_t, k_t, lg_t, b_t = F[2], F[3], F[4], F[5]
                self.gates_f(l, h, r, ps, f_t, k_t, lg_t, b_t)
                qI, q1, k1, k2, kh16 = M["b16"]
                X, Y = f_t, F[6]
                n16 = TL // 16
                bl = b_t.ap[:].rearrange("p (c i) -> p c i", i=32)[:, :, 31:32].to_broadcast([128, nch, 32])
                b3 = b_t.ap[:].rearrange("p (c i) -> p c i", i=32)
                v32 = lambda t_: t_.ap[:].rearrange("p (c i) -> p c i", i=32)
                v16 = lambda t_: t_.ap[:].rearrange("p (c i) -> p c i", i=16)
                xl16 = X.ap[:].rearrange("p (c i) -> p c i", i=16)[:, :, 15:16].to_broadcast([128, n16, 16])
                cm16 = M["cm16"]

                def expmul(dst16, src_t, scale, base_t):
                    tmp = Y if src_t is X else X
                    self.act(tmp, tmp.ap[:], src_t, src_t.ap[:], AF.Exp, scale=scale)
                    self.tt("dve", dst16, dst16.ap[:], base_t, base_t.ap[:], tmp, tmp.ap[:], ALU.mult)

                if r == 0:
                    self.tt("pool", Y, v32(Y), b_t, bl, b_t, b3, ALU.subtract)
                    expmul(kh16, Y, 1.0, k_t)
                    self.act(M["dec"], M["dec"].ap[:, 0, 0:nch], b_t, b_t.ap[:, 31:TL:32], AF.Exp)
                    self.act(Y, Y.ap[:], b_t, b_t.ap[:], AF.Exp)
                    self.tt("dve", qI, qI.ap[:], Q, Q.ap[:], Y, Y.ap[:], ALU.mult)
                    self.P.op("dve", lambda e: e.tensor_tensor_scan(X.ap[:], cm16.ap[:], lg_t.ap[:], 0.0, ALU.mult, ALU.add),
                              reads=[cm16, lg_t], writes=[X])
                    expmul(q1, X, 1.0, Q)
                    expmul(k1, X, -1.0, k_t)
                    self.tt("pool", Y, v16(Y), X, xl16, X, v16(X), ALU.subtract)
                    self.act(Y, Y.ap[:], Y, Y.ap[:], AF.Exp)
                    self.tt("dve", k2, k2.ap[:], k_t, k_t.ap[:], Y, Y.ap[:], ALU.mult)
                else:
                    self.khat_dec_b(k_t, lg_t, b_t, kh16, 1)
                    self.tt("pool", X, v32(X), b_t, bl, b_t, b3, ALU.subtract)
                    self.tt("pool", X, X.ap[:], X, X.ap[:], lg_t, lg_t.ap[:], ALU.add)
                    expmul(qI, X, 1.0, Q)
                    self.P.op("dve", lambda e: e.tensor_tensor_scan(X.ap[:], cm16.ap[:], lg_t.ap[:], 0.0, ALU.mult, ALU.add),
                              reads=[cm16, lg_t], writes=[X])
                    self.tt("pool", Y, Y.ap[:], X, X.ap[:], lg_t, lg_t.ap[:], ALU.subtract)
                    self.act(Y, Y.ap[:], Y, Y.ap[:], AF.Exp)
                    self.tt("dve", k2, k2.ap[:], k_t, k_t.ap[:], Y, Y.ap[:], ALU.mult)
                    self.tt("pool", Y, v16(Y), X, xl16, X, v16(X), ALU.subtract)
                    self.tt("pool", Y, Y.ap[:], Y, Y.ap[:], lg_t, lg_t.ap[:], ALU.add)
                    expmul(q1, Y, 1.0, Q)
                    expmul(k1, Y, -1.0, k_t)
                qt16 = qI
                self.kh_transpose(kh16)
                St = M["Sf"][h] if r == 0 else M["Sb"][h]
                if r == 1:
                    if idx < len(M["bound"]):
                        self.copy("pool", St, St.ap[:], M["bound"][idx][h], M["bound"][idx][h].ap[:])
                    elif S:
                        self.dma("sp", St.ap[:], self.state[l, 1, h], writes=[St])
                    else:
                        self.memset("pool", St, St.ap[:], 0.0)
                acc = self.getpsl()
                started = False
                subs = range(nsub) if r == 0 else reversed(range(nsub))
                nbf = 0
                for sub in subs:
                    n0 = sub * 128
                    for wi_, kx in enumerate((k1, k2)):
                        sps = self.getps()
                        self.mm(sps, sps.ap[:, 0:128], kx, kx.ap[:, n0:n0 + 128], q1, q1.ap[:, n0:n0 + 128],
                                start=True, stop=True)
                        AT = M["AT"][(sub % 2) * 2 + wi_]
                        self.tt("dve", AT, AT.ap[:], sps, sps.ap[:, 0:128], self.cbf, self.cbf.ap[:, 7 + 2 * r + wi_, :], ALU.mult)
                        self.mm(acc, acc.ap[:, n0:n0 + 128], M["Vt"], M["Vt"].ap[:, sub, h * 128:(h + 1) * 128],
                                AT, AT.ap[:], start=(not started), stop=False, skip_group_check=True)
                        started = True
                    psu, pv = self.u_mats(h, sub)
                    js = range(4) if r == 0 else reversed(range(4))
                    for j in js:
                        c = sub * 4 + j
                        if nbf == 0:
                            sb16 = M["Sbf"][0]
                            self.copy("act", sb16, sb16.ap[:], St, St.ap[:])
                        else:
                            sb16 = M["Sbf"][nbf % 3]
                        nbf += 1
                        self.mm(acc, acc.ap[:, c * 32:(c + 1) * 32], sb16, sb16.ap[:], qt16, qt16.ap[:, c * 32:(c + 1) * 32],
                                start=False, stop=True, skip_group_check=True)
                        nx16 = M["Sbf"][nbf % 3]
                        self.stt("dve", nx16, nx16.ap[:], St, St.ap[:], M["dec"].ap[:, r, c:c + 1], psu, pv[:, j, :],
                                 ALU.mult, ALU.add, reads=[M["dec"]])
                        self.stt("dve", St, St.ap[:], St, St.ap[:], M["dec"].ap[:, r, c:c + 1], psu, pv[:, j, :],
                                 ALU.mult, ALU.add, reads=[M["dec"]])
                if r == 0:
                    self.copy("act", osum, osum.ap[:], acc, acc.ap[:, 0:TL])
                else:
                    self.tt("dve", osum, osum.ap[:], osum, osum.ap[:], acc, acc.ap[:, 0:TL], ALU.add)
            sq = M["sq"][0]
            self.act(sq, sq.ap[:], osum, osum.ap[:], AF.Square)
            p2 = self.getps()
            self.mm(p2, p2.ap[:, 0:TL], self.ones, self.ones.ap[:], sq, sq.ap[:], start=True, stop=True)
            vv, rstd = F[4], F[5]
            self.act(vv, vv.ap[:], p2, p2.ap[:, 0:TL], AF.Ln, reads=[self.cc], scale=1.0 / 128, bias=self.cc.ap[:, 0:1])
            self.act(rstd, rstd.ap[:], vv, vv.ap[:], AF.Exp, scale=-0.5)
            gcol = self.vecs.ap[:, self.c_hg + l:self.c_hg + l + 1]
            self.stt("dve", osum, osum.ap[:], osum, osum.ap[:], gcol, rstd, rstd.ap[:], ALU.mult, ALU.mult, reads=[self.vecs])
            self.tt("dve", M["oT"], M["oT"].ap[:, h, :], osum, osum.ap[:], OG, OG.ap[:], ALU.mult)

    def merge_branch(self, l, k, first, last):
        M = self.M
        TL = self.TL
        wi = self.w_in[l].rearrange("(kc p) f -> p kc f", p=128)
        off = (O_GA, O_GB, O_GC)[k]
        wb = self.getw()
        wbv = wb.ap[:, 0:4096].rearrange("p (k c) -> p k c", c=1024)
        self.dma("pool", wbv, self.w_branch[l, k].rearrange("(kc p) f -> p kc f", p=128), writes=[wb])
        F = M["f"]
        for half in range(2):
            wg = self.getw()
            wgv = wg.ap[:, 0:4096].rearrange("p (k c) -> p k c", c=512)
            self.dma("pool", wgv, wi[:, :, off + half * 512:off + (half + 1) * 512], writes=[wg])
            for f4 in range(4):
                fc = half * 4 + f4
                gps = self.proj_fm(wg, lambda kc, f4=f4: wgv[:, kc, f4 * 128:(f4 + 1) * 128])
                bps = self.getps()
                for k4 in range(4):
                    self.mm(bps, bps.ap[:, 0:TL], wb, wbv[:, k4, fc * 128:(fc + 1) * 128], M["oT"], M["oT"].ap[:, k4, :],
                            start=(k4 == 0), stop=(k4 == 3))
                r = F[fc % 2]
                self.sigmoid_parts(gps, r)
                mg = M["mgs"][fc]
                if first:
                    self.tt("dve", mg, mg.ap, r, r.ap[:], bps, bps.ap[:, 0:TL], ALU.mult)
                else:
                    self.tt("dve", r, r.ap[:], r, r.ap[:], bps, bps.ap[:, 0:TL], ALU.mult)
                    if last:
                        self.tt("pool", M["mT"], M["mT"].ap[:, fc, :], mg, mg.ap, r, r.ap[:], ALU.add)
                    else:
                        self.tt("pool", mg, mg.ap, mg, mg.ap, r, r.ap[:], ALU.add)

    def pass2(self, l, tl, seq):
        M = self.M
        S = (self.g == 1)
        TL = self.TL
        idx, b, c0, c1, soff = tl
        m = 1 if S else 0
        self.m_adaln(l, tl)
        if S:
            self.dma("sp", M["rc"].ap[:], self.rope[0][:, soff:soff + TL], writes=[M["rc"]])
            self.dma("sp", M["rs"].ap[:], self.rope[1][:, soff:soff + TL], writes=[M["rs"]])
        if SUB < 4:
            return
        self.hgrn2(l, tl)
        if SUB < 5:
            return
        self.merge_branch(l, 1, True, False)
        if SUB < 6:
            return
        self.attention(l, tl, 0)
        self.merge_branch(l, 0, False, False)
        if SUB < 7:
            return
        self.attention(l, tl, 1)
        self.merge_branch(l, 2, False, True)
        wo = self.w_out[l].rearrange("(kc p) f -> p kc f", p=128)
        for half in range(2):
            wt = self.getw()
            wv = wt.ap[:, 0:4096].rearrange("p (k c) -> p k c", c=512)
            self.dma("pool", wv, wo[:, :, half * 512:(half + 1) * 512], writes=[wt])
            for d4 in range(4):
                dc = half * 4 + d4
                ps = self.getps()
                for fc in range(8):
                    self.mm(ps, ps.ap[:, 0:TL], wt, wv[:, fc, d4 * 128:(d4 + 1) * 128], M["mT"], M["mT"].ap[:, fc, :],
                            start=(fc == 0), stop=(fc == 7))
                x = self.xt[dc][b]
                self.stt("dve", x, x.ap[:, c0:c1], ps, ps.ap[:, 0:TL], self.mod_ap(l, 5, dc, m), x, x.ap[:, c0:c1],
                         ALU.mult, ALU.add, reads=[self.modv])


def _consts():
    c = np.zeros((12, 128, 128), np.float32)
    c[0] = np.eye(128, dtype=np.float32)
    c[1] = np.eye(128, dtype=np.float32)
    j = np.arange(128)[:, None]
    p = np.arange(128)[None, :]
    c[2] = (j // 64 == p // 64)
    rot = np.zeros((128, 128), np.float32)
    for i in range(128):
        d = i % 32
        if d < 16:
            rot[i + 16, i] = -1.0
        else:
            rot[i - 16, i] = 1.0
    c[3] = rot
    c[4] = (j >= p)
    c[5] = (j <= p)
    c[6] = (j // 32 == p // 32) & (j <= p)
    c[7] = (j // 32 == p // 32) & (j >= p)
    c[8] = (j // 16 == p // 16) & (j <= p)
    c[9] = (j // 32 == p // 32) & (j % 32 < 16) & (p % 32 >= 16)
    c[10] = (j // 16 == p // 16) & (j >= p)
    c[11] = (j // 32 == p // 32) & (j % 32 >= 16) & (p % 32 < 16)
    t = np.arange(DSEQ)
    inv = 10000.0 ** (-np.arange(0, 32, 2, dtype=np.float64) / 32)
    dd = np.arange(128) % 64
    pos = np.where(dd[:, None] < 32, (t // 64)[None, :], (t % 64)[None, :]).astype(np.float64)
    ang = pos * inv[dd % 16][:, None]
    rope = np.stack([np.cos(ang), np.sin(ang)]).astype(np.float32)
    return c, rope


_CACHE = {}


def _get_nc(stage=99, nlayers=DEPTH, LW=DEPTH):
    key = (stage, nlayers, LW)
    if key not in _CACHE:
        _CACHE[key] = Builder(stage, nlayers, LW).build()
    return _CACHE[key]


def kernel(x_prompt, x_sample, cache_k_attn, cache_v_attn, cache_k_win, cache_v_win, state_hgrn,
           c, c_ctx, w_mod, b_mod, norm_g, w_ffn_in, w_ffn_out, w_in, qk_norm_g, lower_bounds,
           hg_norm_g, sink_logit, w_branch, w_out, final_norm_g, _stage=99, _nlayers=DEPTH, _ncores=NCORES):
    f = lambda a: np.ascontiguousarray(np.asarray(a, dtype=np.float32))
    LW = int(np.asarray(w_mod).shape[0])
    DEPTH = LW
    NCORES = _ncores
    nc = _get_nc(_stage, _nlayers, LW)
    cst, rope = _consts()
    shared = dict(w_mod=f(w_mod), b_mod=f(b_mod), norm_g=f(norm_g), w_ffn_in=f(w_ffn_in),
                  w_ffn_out=f(w_ffn_out), w_in=f(w_in), qk_norm_g=f(qk_norm_g),
                  lower_bounds=f(lower_bounds), hg_norm_g=f(hg_norm_g), sink_logit=f(sink_logit),
                  w_branch=f(w_branch), w_out=f(w_out), final_norm_g=f(final_norm_g), cst=cst, rope=rope)
    xp = f(x_prompt).reshape(-1, NPROMPT * SEQ, D)
    xs = f(x_sample)
    in_maps = []
    for k in range(NCORES):
        d = dict(shared)
        d["xin"] = np.concatenate([xp[k], xs[k]], axis=0)
        d["cond"] = np.stack([f(c_ctx), f(c)[k]], axis=0)
        d["ck_att"] = f(cache_k_attn)[k].reshape(DEPTH, PAST, 128)
        d["cv_att"] = f(cache_v_attn)[k].reshape(DEPTH, PAST, 128)
        d["ck_win"] = f(cache_k_win)[k].reshape(DEPTH, PAST, 128)
        d["cv_win"] = f(cache_v_win)[k].reshape(DEPTH, PAST, 128)
        d["state"] = f(state_hgrn)[k]
        in_maps.append(d)
    res = run_bass_kernel_spmd(nc, in_maps, core_ids=list(range(NCORES)))
    R = res.results
    y = np.stack([r["y"] for r in R])
    y_prompt = y[:, :NPROMPT * SEQ].reshape(NCORES * NPROMPT, SEQ, D)
    y_sample = y[:, NPROMPT * SEQ:]
    cat = lambda n, shp: np.concatenate([r[n] for r in R], axis=0).reshape(shp)
    return (y_prompt, y_sample,
            cat("o_k_att", (-1, DEPTH, SEQ, 2, 64)), cat("o_v_att", (-1, DEPTH, SEQ, 2, 64)),
            cat("o_k_win", (-1, DEPTH, SEQ, 2, 64)), cat("o_v_win", (-1, DEPTH, SEQ, 2, 64)),
            cat("o_state", (-1, DEPTH, 2, 4, 128, 128)))
```
